# Optimizing a Trainium2 kernel written in Bass

```python
import math
import jax, jax.numpy as jnp
from jax import lax
import numpy as np

D_MODEL = 1024
BATCH = 4
SEQ = 4096
DEPTH = 4

CHUNK = 64
S5_WIDTH = 512
S5_GROUP = 16
S5_GROUPS = S5_WIDTH // S5_GROUP
S5_STATE = 64
DT_MIN = 1e-3
DT_MAX = 1e-1
MLA_HEADS = 8
QK_NOPE = 64
QK_ROPE = 32
V_HEAD = 64
Q_LORA = 384
KV_LORA = 256
ROPE_THETA = 10000.0
Q_BLOCK = 128
SGU_WIDTH = 512
SGU_GROUPS = 4
SGU_CHUNK = 128
N_BRANCH = 3
BRANCH_WIDTH = 512
FF_HIDDEN = -(-8 * D_MODEL // (3 * 256)) * 256
DEEPNORM_ALPHA = (2 * DEPTH) ** 0.25
DEEPNORM_BETA = (8 * DEPTH) ** -0.25
LN_EPS = 1e-5
RMS_EPS = 1e-6
NEG_INF = -1e30
IN_WIDTHS = (S5_WIDTH, Q_LORA, KV_LORA, QK_ROPE, SGU_WIDTH, SGU_WIDTH, N_BRANCH * D_MODEL)
IN_OFFSETS = tuple(int(o) for o in np.cumsum(IN_WIDTHS)[:-1])
IN_WIDTH = sum(IN_WIDTHS)

kernel_name = 'hybrid_s5_mla_sgu_deepnorm_adaln'


def layer_norm(x, g, b):
    xf = x.astype(jnp.float32)
    mu = jnp.mean(xf, axis=-1, keepdims=True)
    var = jnp.mean(jnp.square(xf - mu), axis=-1, keepdims=True)
    return ((xf - mu) * lax.rsqrt(var + LN_EPS)).astype(x.dtype) * g + b


def rms_norm(x, g):
    xf = x.astype(jnp.float32)
    return (xf * lax.rsqrt(jnp.mean(xf * xf, axis=-1, keepdims=True) + RMS_EPS)).astype(x.dtype) * g


def rope(x, cos, sin):
    x1, x2 = jnp.split(x, 2, axis=-1)
    return jnp.concatenate([x1 * cos - x2 * sin, x2 * cos + x1 * sin], axis=-1)


def _complex_affine_combine(left, right):
    ar1, ai1, br1, bi1 = left
    ar2, ai2, br2, bi2 = right
    return (ar2 * ar1 - ai2 * ai1,
            ar2 * ai1 + ai2 * ar1,
            ar2 * br1 - ai2 * bi1 + br2,
            ar2 * bi1 + ai2 * br1 + bi2)


def s5_mixer(u, lam_re, lam_im, log_dt, b_re, b_im, c_re, c_im, d, w_glu, b_glu):
    f32 = jnp.float32
    bsz, seq, _ = u.shape
    uf = u.astype(f32)
    ug = uf.reshape(bsz, seq, S5_GROUPS, S5_GROUP)
    dt = jnp.exp(log_dt.astype(f32))[:, None]
    lr = lam_re.astype(f32)
    li = lam_im.astype(f32)
    mag = jnp.exp(lr * dt)
    a_re = mag * jnp.cos(li * dt)
    a_im = mag * jnp.sin(li * dt)
    den = lr * lr + li * li
    f_re = ((a_re - 1.0) * lr + a_im * li) / den
    f_im = (a_im * lr - (a_re - 1.0) * li) / den
    br = b_re.astype(f32)
    bi = b_im.astype(f32)
    bb_re = f_re[..., None] * br - f_im[..., None] * bi
    bb_im = f_re[..., None] * bi + f_im[..., None] * br
    bu_re = jnp.einsum('bsgc,gpc->bsgp', ug, bb_re)
    bu_im = jnp.einsum('bsgc,gpc->bsgp', ug, bb_im)
    a_re_t = jnp.broadcast_to(a_re, (1, seq) + a_re.shape)
    a_im_t = jnp.broadcast_to(a_im, (1, seq) + a_im.shape)
    _, _, h_re, h_im = lax.associative_scan(
        _complex_affine_combine, (a_re_t, a_im_t, bu_re, bu_im), axis=1)
    y = (jnp.einsum('bsgp,gcp->bsgc', h_re, c_re.astype(f32))
         - jnp.einsum('bsgp,gcp->bsgc', h_im, c_im.astype(f32)))
    y = y.reshape(bsz, seq, S5_WIDTH) + d.astype(f32) * uf
    z = jax.nn.gelu(y)
    out = z * jax.nn.sigmoid(z @ w_glu.astype(f32) + b_glu.astype(f32))
    return out.astype(u.dtype)


def mla_mixer(cq, ckv, k_pe, q_norm, w_q_up, kv_norm, w_kv_up, cos, sin):
    bsz, seq, _ = cq.shape
    q = (rms_norm(cq, q_norm) @ w_q_up).reshape(bsz, seq, MLA_HEADS, QK_NOPE + QK_ROPE)
    q_nope = q[..., :QK_NOPE]
    q_pe = rope(q[..., QK_NOPE:], cos[:, None, :], sin[:, None, :])
    kv = (rms_norm(ckv, kv_norm) @ w_kv_up).reshape(bsz, seq, MLA_HEADS, QK_NOPE + V_HEAD)
    k_nope = kv[..., :QK_NOPE]
    v = kv[..., QK_NOPE:]
    k_pe = rope(k_pe, cos, sin)
    n_blk = seq // Q_BLOCK
    scale = (QK_NOPE + QK_ROPE) ** -0.5
    q_nope_b = q_nope.reshape(bsz, n_blk, Q_BLOCK, MLA_HEADS, QK_NOPE).transpose(1, 0, 2, 3, 4)
    q_pe_b = q_pe.reshape(bsz, n_blk, Q_BLOCK, MLA_HEADS, QK_ROPE).transpose(1, 0, 2, 3, 4)
    key_chunk = jnp.arange(seq) // CHUNK

    def attend_block(args):
        blk, qn, qp = args
        s = (jnp.einsum('bqhd,bkhd->bhqk', qn, k_nope)
             + jnp.einsum('bqhr,bkr->bhqk', qp, k_pe))
        s = s.astype(jnp.float32) * scale
        q_chunk = (blk * Q_BLOCK + jnp.arange(Q_BLOCK)) // CHUNK
        mask = key_chunk[None, :] <= q_chunk[:, None]
        s = jnp.where(mask, s, NEG_INF)
        p = jax.nn.softmax(s, axis=-1).astype(v.dtype)
        return jnp.einsum('bhqk,bkhd->bqhd', p, v)

    o = lax.map(attend_block, (jnp.arange(n_blk), q_nope_b, q_pe_b))
    return o.transpose(1, 0, 2, 3, 4).reshape(bsz, seq, MLA_HEADS * V_HEAD)


def sgu_mixer(u, v, ln_g, ln_b, w_s, b_s):
    bsz, seq, _ = u.shape
    u = jax.nn.gelu(u)
    v = layer_norm(jax.nn.gelu(v), ln_g, ln_b)
    n_chunk = seq // SGU_CHUNK
    vg = v.reshape(bsz, n_chunk, SGU_CHUNK, SGU_GROUPS, SGU_WIDTH // SGU_GROUPS)
    pos_chunk = jnp.arange(SGU_CHUNK) // CHUNK
    mask = pos_chunk[None, :] <= pos_chunk[:, None]
    w = jnp.where(mask[None], w_s, 0.0)
    mixed = jnp.einsum('gij,bnjgc->bnigc', w, vg) + b_s.T[:, :, None]
    return u * mixed.reshape(bsz, seq, SGU_WIDTH)


def hybrid_mixer(h, w_in, b_in, lam_re, lam_im, log_dt, b_re, b_im, c_re, c_im, d, w_glu, b_glu,
                 q_norm, w_q_up, kv_norm, w_kv_up, sgu_g, sgu_b, w_s, b_s, w_branch, w_out, cos, sin):
    bsz, seq, _ = h.shape
    proj = h @ w_in + b_in
    u_s5, cq, ckv, k_pe, u_sgu, v_sgu, gate_logits = jnp.split(proj, IN_OFFSETS, axis=-1)
    y_s5 = s5_mixer(u_s5, lam_re, lam_im, log_dt, b_re, b_im, c_re, c_im, d, w_glu, b_glu)
    y_mla = mla_mixer(cq, ckv, k_pe, q_norm, w_q_up, kv_norm, w_kv_up, cos, sin)
    y_sgu = sgu_mixer(u_sgu, v_sgu, sgu_g, sgu_b, w_s, b_s)
    gates = jax.nn.sigmoid(gate_logits).reshape(bsz, seq, N_BRANCH, D_MODEL)
    merged = (gates[:, :, 0] * (y_s5 @ w_branch[0])
              + gates[:, :, 1] * (y_mla @ w_branch[1])
              + gates[:, :, 2] * (y_sgu @ w_branch[2]))
    return merged @ w_out


def swiglu(h, w_in, w_out):
    a, b = jnp.split(h @ w_in, 2, axis=-1)
    return (jax.nn.silu(a) * b) @ w_out


def setup_inputs(seed: int = 0) -> dict:
    key = jax.random.key(seed)
    ks = jax.random.split(key, 32)
    L = DEPTH

    def nrm(k, shape, std):
        return jax.random.normal(k, shape, jnp.float32) * std

    def gain(k, shape):
        return 1.0 + nrm(k, shape, 0.01)

    lam_im0 = jnp.broadcast_to(math.pi * jnp.arange(S5_STATE, dtype=jnp.float32), (L, S5_GROUPS, S5_STATE))
    return {
        'x': nrm(ks[0], (BATCH, SEQ, D_MODEL), 1.0),
        'c': nrm(ks[1], (BATCH, D_MODEL), 1.0),
        'w_ada': nrm(ks[2], (L, D_MODEL, 6 * D_MODEL), 0.02),
        'b_ada': nrm(ks[3], (L, 6 * D_MODEL), 0.01),
        'w_in': nrm(ks[4], (L, D_MODEL, IN_WIDTH), D_MODEL ** -0.5),
        'b_in': nrm(ks[5], (L, IN_WIDTH), 0.01),
        's5_lambda_re': -0.5 + nrm(ks[6], (L, S5_GROUPS, S5_STATE), 0.01),
        's5_lambda_im': lam_im0 + nrm(ks[7], (L, S5_GROUPS, S5_STATE), 0.01),
        's5_log_dt': jax.random.uniform(ks[8], (L, S5_GROUPS), jnp.float32, math.log(DT_MIN), math.log(DT_MAX)),
        's5_b_re': nrm(ks[9], (L, S5_GROUPS, S5_STATE, S5_GROUP), (2 * S5_GROUP) ** -0.5),
        's5_b_im': nrm(ks[10], (L, S5_GROUPS, S5_STATE, S5_GROUP), (2 * S5_GROUP) ** -0.5),
        's5_c_re': nrm(ks[11], (L, S5_GROUPS, S5_GROUP, S5_STATE), S5_STATE ** -0.5),
        's5_c_im': nrm(ks[12], (L, S5_GROUPS, S5_GROUP, S5_STATE), S5_STATE ** -0.5),
        's5_d': nrm(ks[13], (L, S5_WIDTH), 1.0),
        's5_w_glu': nrm(ks[14], (L, S5_WIDTH, S5_WIDTH), S5_WIDTH ** -0.5),
        's5_b_glu': nrm(ks[15], (L, S5_WIDTH), 0.01),
        'mla_q_norm': gain(ks[16], (L, Q_LORA)),
        'mla_w_q_up': nrm(ks[17], (L, Q_LORA, MLA_HEADS * (QK_NOPE + QK_ROPE)), Q_LORA ** -0.5),
        'mla_kv_norm': gain(ks[18], (L, KV_LORA)),
        'mla_w_kv_up': nrm(ks[19], (L, KV_LORA, MLA_HEADS * (QK_NOPE + V_HEAD)), KV_LORA ** -0.5),
        'sgu_ln_g': gain(ks[20], (L, SGU_WIDTH)),
        'sgu_ln_b': nrm(ks[21], (L, SGU_WIDTH), 0.01),
        'sgu_w_s': nrm(ks[22], (L, SGU_GROUPS, SGU_CHUNK, SGU_CHUNK), SGU_CHUNK ** -0.5),
        'sgu_b_s': gain(ks[23], (L, SGU_GROUPS, SGU_CHUNK)),
        'w_branch': nrm(ks[24], (L, N_BRANCH, BRANCH_WIDTH, D_MODEL), BRANCH_WIDTH ** -0.5),
        'w_out': nrm(ks[25], (L, D_MODEL, D_MODEL), DEEPNORM_BETA * D_MODEL ** -0.5),
        'ln1_g': gain(ks[26], (L, D_MODEL)),
        'ln1_b': nrm(ks[27], (L, D_MODEL), 0.01),
        'ffn_w_in': nrm(ks[28], (L, D_MODEL, 2 * FF_HIDDEN), D_MODEL ** -0.5),
        'ffn_w_out': nrm(ks[29], (L, FF_HIDDEN, D_MODEL), DEEPNORM_BETA * FF_HIDDEN ** -0.5),
        'ln2_g': gain(ks[30], (L, D_MODEL)),
        'ln2_b': nrm(ks[31], (L, D_MODEL), 0.01),
    }


def reference(x, c, w_ada, b_ada, w_in, b_in, s5_lambda_re, s5_lambda_im, s5_log_dt, s5_b_re, s5_b_im,
              s5_c_re, s5_c_im, s5_d, s5_w_glu, s5_b_glu, mla_q_norm, mla_w_q_up, mla_kv_norm, mla_w_kv_up,
              sgu_ln_g, sgu_ln_b, sgu_w_s, sgu_b_s, w_branch, w_out, ln1_g, ln1_b, ffn_w_in, ffn_w_out,
              ln2_g, ln2_b):
    seq = x.shape[1]
    inv_freq = 1.0 / (ROPE_THETA ** (jnp.arange(0, QK_ROPE, 2, dtype=jnp.float32) / QK_ROPE))
    ang = jnp.arange(seq, dtype=jnp.float32)[:, None] * inv_freq[None, :]
    cos = jnp.cos(ang).astype(x.dtype)
    sin = jnp.sin(ang).astype(x.dtype)
    c_act = jax.nn.silu(c)
    for l in range(DEPTH):
        ada = (c_act @ w_ada[l] + b_ada[l])[:, None, :]
        sh1, sc1, g1, sh2, sc2, g2 = jnp.split(ada, 6, axis=-1)
        h = x * (1.0 + sc1) + sh1
        y = hybrid_mixer(h, w_in[l], b_in[l], s5_lambda_re[l], s5_lambda_im[l], s5_log_dt[l],
                         s5_b_re[l], s5_b_im[l], s5_c_re[l], s5_c_im[l], s5_d[l], s5_w_glu[l], s5_b_glu[l],
                         mla_q_norm[l], mla_w_q_up[l], mla_kv_norm[l], mla_w_kv_up[l],
                         sgu_ln_g[l], sgu_ln_b[l], sgu_w_s[l], sgu_b_s[l], w_branch[l], w_out[l], cos, sin)
        x = layer_norm(DEEPNORM_ALPHA * x + (1.0 + g1) * y, ln1_g[l], ln1_b[l])
        h = x * (1.0 + sc2) + sh2
        f = swiglu(h, ffn_w_in[l], ffn_w_out[l])
        x = layer_norm(DEEPNORM_ALPHA * x + (1.0 + g2) * f, ln2_g[l], ln2_b[l])
    return x
```

```python
import contextlib
import math
import os

import ml_dtypes
import numpy as np

import concourse.bass as bass
import concourse.mybir as mybir
from concourse.bass_utils import run_bass_kernel_spmd

F32 = mybir.dt.float32
BF16 = mybir.dt.bfloat16
AF = mybir.ActivationFunctionType
ALU = mybir.AluOpType
AX = mybir.AxisListType

D = 1024
S = 4096
NB = 4
DEPTH = 4
TT = 512
NSLOT = 4
NTOK = NSLOT * TT
INW = 5280
FF = 2816
ALPHA = (2 * DEPTH) ** 0.25
LN_EPS = 1e-5
RMS_EPS = 1e-6
TILES = {0: [0, 3, 4, 7], 1: [1, 2, 5, 6]}
NEG = -30000.0
PAIRS = [[0, 1], [2, 3], [4, 5], [6, 7]]


class KB:
    NDMA = 28

    def __init__(self, nc, es):
        self.nc = nc
        self.es = es
        self.eng = {'pe': nc.tensor, 'act': nc.scalar, 'dve': nc.vector, 'pool': nc.gpsimd, 'sp': nc.sync}
        self.sem = {}
        for n in ['pe', 'act', 'dve', 'pool', 'cc']:
            self.sem[n] = es.enter_context(nc.semaphore('s_' + n))
        for i in range(self.NDMA):
            self.sem['d%d' % i] = es.enter_context(nc.semaphore('s_d%d' % i))
        self.cnt = {k: 0 for k in self.sem}
        self.known = {e: {} for e in self.eng}
        self.res = {}
        self.dma_rr = 0
        self.ninst = 0
        self.ps_rr = 0
        self.psb = []

    def sbuf(self, name, shape, dt, es=None):
        self.uid = getattr(self, 'uid', 0) + 1
        return (es or self.es).enter_context(self.nc.sbuf_tensor('sb%d_%s' % (self.uid, name), list(shape), dt))

    def psum(self, name, shape, dt):
        return self.es.enter_context(self.nc.psum_tensor(name, list(shape), dt))

    def next_ps(self):
        t = self.psb[self.ps_rr % len(self.psb)]
        self.ps_rr += 1
        return t

    def _wait(self, e, s, v):
        if v <= 0 or (e == 'pe' and s == 'pe'):
            return
        kn = self.known[e]
        if kn.get(s, 0) >= v:
            return
        self.eng[e].wait_ge(self.sem[s], v)
        kn[s] = v
        self.ninst += 1

    @staticmethod
    def _psfix(R, W):
        R = list(R)
        W = list(W)
        W += [k for k in R if k.startswith('ps')]
        R = [k for k in R if not k.startswith('ps')]
        return R, W

    def _deps(self, e, reads, writes):
        toks = {}

        def add(s, v):
            if toks.get(s, 0) < v:
                toks[s] = v
        for r in reads:
            st = self.res.get(r)
            if st is not None and st[0] is not None:
                add(*st[0])
        for w in writes:
            st = self.res.get(w)
            if st is not None:
                if st[0] is not None:
                    add(*st[0])
                for s, v in st[1].items():
                    add(s, v)
        for s, v in toks.items():
            self._wait(e, s, v)

    def _commit(self, tok, reads, writes):
        s, v = tok
        for r in reads:
            st = self.res.get(r)
            if st is None:
                st = [None, {}]
                self.res[r] = st
            if st[1].get(s, 0) < v:
                st[1][s] = v
        for w in writes:
            self.res[w] = [tok, {}]

    def op(self, e, meth, R=(), W=(), **kw):
        R, W = self._psfix(R, W)
        self._deps(e, R, W)
        inst = getattr(self.eng[e], meth)(**kw)
        self.cnt[e] += 1
        inst.then_inc(self.sem[e], 1)
        self._commit((e, self.cnt[e]), R, W)
        self.ninst += 1

    def mm(self, out, pairs, R=(), W=()):
        self._deps('pe', R, W)
        n = len(pairs)
        inst = None
        for i, (lt, rh) in enumerate(pairs):
            inst = self.nc.tensor.matmul(out, lhsT=lt, rhs=rh, start=(i == 0), stop=(i == n - 1))
            self.ninst += 1
        self.cnt['pe'] += 1
        inst.then_inc(self.sem['pe'], 1)
        self._commit(('pe', self.cnt['pe']), R, W)

    def dma(self, q, out, in_, R=(), W=(), **kw):
        s = 'd%d' % self.dma_rr
        self.dma_rr = (self.dma_rr + 1) % self.NDMA
        self._wait(q, s, self.cnt[s])
        self._deps(q, R, W)
        inst = self.eng[q].dma_start(out=out, in_=in_, **kw)
        self.cnt[s] += 16
        inst.then_inc(self.sem[s], 16)
        self._commit((s, self.cnt[s]), R, W)
        self.ninst += 1

    def bg_dma(self, q, group, out, in_, **kw):
        s = 'bg_' + group
        if s not in self.sem:
            self.sem[s] = self.es.enter_context(self.nc.semaphore('s_' + s))
            self.cnt[s] = 0
        inst = self.eng[q].dma_start(out=out, in_=in_, **kw)
        self.cnt[s] += 16
        inst.then_inc(self.sem[s], 16)
        self.res[group] = [(s, self.cnt[s]), {}]
        self.ninst += 1

    def mmg(self, items, R=(), W=()):
        self._deps('pe', R, W)
        inst = None
        for (out, lt, rh, st, sp) in items:
            inst = self.nc.tensor.matmul(out, lhsT=lt, rhs=rh, start=st, stop=sp, skip_group_check=True)
            self.ninst += 1
        self.cnt['pe'] += 1
        inst.then_inc(self.sem['pe'], 1)
        self._commit(('pe', self.cnt['pe']), R, W)

    def allgather(self, src_t, dst_t, R=(), W=()):
        self._deps('pool', R, W)
        inst = self.nc.gpsimd.collective_compute("AllGather", ALU.bypass, replica_groups=PAIRS,
                                                 ins=[src_t.ap().opt()], outs=[dst_t.ap().opt()])
        self.cnt['cc'] += 1
        inst.then_inc(self.sem['cc'], 1)
        self._commit(('cc', self.cnt['cc']), R, W)
        self.ninst += 1

    def barrier(self):
        for e in self.eng:
            for s, v in self.cnt.items():
                if not s.startswith('bg_'):
                    self._wait(e, s, v)
        self.res = {k: [st[0], {}] for k, st in self.res.items()
                    if st[0] is not None and st[0][0].startswith('bg_')}

    def finish(self):
        for s, v in self.cnt.items():
            self._wait('sp', s, v)


def _fm(ap1d):
    return ap1d.rearrange("(t p) -> p t", p=128)


def build_program(n_layers=DEPTH, dbg=()):
    nc = bass.Bass("TRN2", target_bir_lowering=False)
    inp = {}

    def I(name, shape, dt=F32):
        inp[name] = nc.dram_tensor(name, list(shape), dt, kind="ExternalInput").ap()
        return inp[name]
    xT_in = I("xT", [D, NTOK])
    cvec = I("cvec", [128, 8])
    ropec = I("ropec", [32, NTOK])
    ropes = I("ropes", [32, NTOK])
    qmask_in = I("qmask", [8, NSLOT * 2 * TT], BF16)
    e8_in = I("e8", [8, TT], BF16)
    egrid_in = I("egrid", [128, 72])
    selv_in = I("selv", [128, 8])
    w_ada = I("w_ada", [n_layers, D, 6 * D])
    b_ada = I("b_ada", [n_layers, 6 * D])
    w_in = I("w_in", [n_layers, D, INW])
    b_in = I("b_in", [n_layers, INW])
    lam_re = I("s5_lambda_re", [n_layers, 32, 64])
    lam_im = I("s5_lambda_im", [n_layers, 32, 64])
    log_dt = I("s5_log_dt", [n_layers, 32])
    s5_b_re = I("s5_b_re", [n_layers, 32, 64, 16])
    s5_b_im = I("s5_b_im", [n_layers, 32, 64, 16])
    s5_c_re = I("s5_c_re", [n_layers, 32, 16, 64])
    s5_c_im = I("s5_c_im", [n_layers, 32, 16, 64])
    s5_d = I("s5_d", [n_layers, 512])
    s5_w_glu = I("s5_w_glu", [n_layers, 512, 512])
    s5_b_glu = I("s5_b_glu", [n_layers, 512])
    q_norm = I("mla_q_norm", [n_layers, 384])
    w_q_up = I("mla_w_q_up", [n_layers, 384, 768])
    kv_norm = I("mla_kv_norm", [n_layers, 256])
    w_kv_up = I("mla_w_kv_up", [n_layers, 256, 1024])
    sgu_ln_g = I("sgu_ln_g", [n_layers, 512])
    sgu_ln_b = I("sgu_ln_b", [n_layers, 512])
    sgu_w_s = I("sgu_w_s", [n_layers, 4, 128, 128])
    sgu_b_s = I("sgu_b_s", [n_layers, 4, 128])
    w_branch = I("w_branch", [n_layers, 3, 512, D])
    w_out = I("w_out", [n_layers, D, D])
    ln1_g = I("ln1_g", [n_layers, D])
    ln1_b = I("ln1_b", [n_layers, D])
    ffn_w_in = I("ffn_w_in", [n_layers, D, 2 * FF])
    ffn_w_out = I("ffn_w_out", [n_layers, FF, D])
    ln2_g = I("ln2_g", [n_layers, D])
    ln2_b = I("ln2_b", [n_layers, D])
    outT = nc.dram_tensor("outT", [D, NTOK], F32, kind="ExternalOutput").ap()
    dbg_out = {}

    def DBG(name, shape, dt=F32):
        dbg_out[name] = nc.dram_tensor("dbg_" + name, list(shape), dt, kind="ExternalOutput").ap()
        return dbg_out[name]

    def SCR(name, shape, dt=BF16):
        return nc.dram_tensor(name, list(shape), dt)
    wsc_g = [[SCR("wsc_g%d_%d" % (l, b), [8, 128, 8, 128]).ap() for b in range(3)] for l in range(n_layers)]
    wsc_br = [[SCR("wsc_br%d_%d" % (l, b), [8, 128, 4, 128]).ap() for b in range(3)] for l in range(n_layers)]
    wsc_o = [SCR("wsc_o%d" % l, [8, 128, 8, 128]).ap() for l in range(n_layers)]
    wsc_fi = [[SCR("wsc_fi%d_%d" % (l, h), [22, 128, 8, 128]).ap() for h in range(2)] for l in range(n_layers)]
    wsc_fo = [SCR("wsc_fo%d" % l, [8, 128, 22, 128]).ap() for l in range(n_layers)]
    park = {k: SCR("park_" + k, [128, 4, NTOK]).ap() for k in ("s5", "mla", "sgu")}
    cc1_src = SCR("cc1_src", [288, NTOK])
    cc1_dst = SCR("cc1_dst", [576, NTOK])
    cc2_src = SCR("cc2_src", [256, 64 * 32], F32)
    cc2_dst = SCR("cc2_dst", [512, 64 * 32], F32)

    with contextlib.ExitStack() as es:
        kb = KB(nc, es)
        kb.psb = [(kb.psum("psA%d" % i, [128, 512], F32), "psA%d" % i) for i in range(6)]
        psX = [(kb.psum("psX%d" % i, [128, 512], F32), "psX%d" % i) for i in range(2)]

        xT = kb.sbuf("xTres", [128, 8, NTOK], F32)
        ident_bf = kb.sbuf("ident_bf", [128, 128], BF16)
        ident_f = kb.sbuf("ident_f", [128, 128], F32)
        ones_bf = kb.sbuf("ones_bf", [128, 128], BF16)
        ones_f = kb.sbuf("ones_f", [128, 128], F32)
        mask01 = kb.sbuf("mask01", [128, 128], F32)
        jmat = kb.sbuf("jmat", [128, 128], F32)
        e8 = kb.sbuf("e8", [8, TT], BF16)
        qmask = kb.sbuf("qmask", [8, NSLOT * 2 * TT], BF16)
        cact = kb.sbuf("cact", [128, 8], BF16)
        ada = kb.sbuf("ada", [128, n_layers, 48], F32)
        lnp = kb.sbuf("lnp", [128, n_layers, 4, 8], F32)

        def OP(e, meth, R=(), W=(), **kw):
            kb.op(e, meth, R, W, **kw)

        OP('pool', 'memset', W=['ident_f'], ap=ident_f[:], constant=1.0)
        OP('pool', 'affine_select', R=['ident_f'], W=['ident_f'], out=ident_f[:], in_=ident_f[:], pattern=[[-1, 128]],
           compare_op=ALU.is_equal, fill=0.0, base=0, channel_multiplier=1)
        OP('pool', 'tensor_copy', R=['ident_f'], W=['ident_bf'], out=ident_bf[:], in_=ident_f[:])
        OP('pool', 'memset', W=['ones_bf'], ap=ones_bf[:], constant=1.0)
        OP('pool', 'memset', W=['ones_f'], ap=ones_f[:], constant=1.0)
        OP('pool', 'memset', W=['mask01'], ap=mask01[:], constant=1.0)
        OP('pool', 'affine_select', R=['mask01'], W=['mask01'], out=mask01[:].rearrange("p (a b) -> p a b", b=16),
           in_=mask01[:].rearrange("p (a b) -> p a b", b=16), pattern=[[16, 8], [0, 16]],
           compare_op=ALU.is_ge, fill=0.0, base=-112, channel_multiplier=1)
        OP('pool', 'memset', W=['jmat'], ap=jmat[:], constant=1.0)
        OP('pool', 'affine_select', R=['jmat'], W=['jmat'], out=jmat[:].rearrange("p (a b) -> p a b", b=16),
           in_=jmat[:].rearrange("p (a b) -> p a b", b=16), pattern=[[16, 8], [-1, 16]],
           compare_op=ALU.is_equal, fill=0.0, base=-112, channel_multiplier=1)
        kb.dma('sp', e8[:], e8_in[:, :], W=['e8'])
        kb.dma('sp', qmask[:], qmask_in[:, :], W=['qmask'])
        for k in range(8):
            kb.dma('sp', xT[:, k, :], xT_in[k * 128:(k + 1) * 128, :], W=['xT%d' % k])
        XK = ['xT%d' % k for k in range(8)]

        def cast_mt(dst, src2d, KT, NM, key):
            for m in range(NM):
                kb.bg_dma('pool', key, dst[m], src2d[:, m * 128:(m + 1) * 128].rearrange("(kt p) c -> p kt c", p=128))

        def cast_layer(l):
            for b in range(3):
                cast_mt(wsc_g[l][b], w_in[l][:, 2208 + b * 1024: 2208 + (b + 1) * 1024], 8, 8, 'wsc_g%d_%d' % (l, b))
                cast_mt(wsc_br[l][b], w_branch[l][b], 4, 8, 'wsc_br%d_%d' % (l, b))
            cast_mt(wsc_o[l], w_out[l], 8, 8, 'wsc_o%d' % l)
            for h in range(2):
                cast_mt(wsc_fi[l][h], ffn_w_in[l][:, h * FF:(h + 1) * FF], 8, 22, 'wsc_fi%d_%d' % (l, h))
            cast_mt(wsc_fo[l], ffn_w_out[l], 22, 8, 'wsc_fo%d' % l)
        cast_layer(0)

        with contextlib.ExitStack() as ph:
            cv = kb.sbuf("cv", [128, 8], F32, ph)
            wada = [kb.sbuf("wada%d" % i, [128, 8, 512], BF16, ph) for i in range(2)]
            arow = kb.sbuf("arow", [1, 6 * D], F32, ph)
            brow = kb.sbuf("brow", [1, 6 * D], F32, ph)
            kb.dma('sp', cv[:], cvec[:, :], W=['cv'])
            OP('act', 'activation', R=['cv'], W=['cact'], out=cact[:], in_=cv[:], func=AF.Silu)
            for l in range(n_layers):
                kb.dma('sp', brow[:], b_ada[l:l + 1, :], W=['brow'])
                for cb in range(12):
                    wb = wada[cb % 2]
                    wk = 'wada%d' % (cb % 2)
                    kb.dma('pool', wb[:], w_ada[l][:, cb * 512:(cb + 1) * 512].rearrange("(kt p) c -> p kt c", p=128),
                           W=[wk])
                    ps, pk = kb.next_ps()
                    kb.mm(ps[0:1, :], [(cact[:, kt:kt + 1], wb[:, kt, :]) for kt in range(8)], R=['cact', wk], W=[pk])
                    OP('dve', 'tensor_tensor', R=[pk, 'brow'], W=['arow'], out=arow[:, cb * 512:(cb + 1) * 512],
                       in0=ps[0:1, :], in1=brow[:, cb * 512:(cb + 1) * 512], op=ALU.add)
                for j in (1, 4):
                    OP('dve', 'tensor_scalar_add', R=['arow'], W=['arow'], out=arow[:, j * D:(j + 1) * D],
                       in0=arow[:, j * D:(j + 1) * D], scalar1=1.0)
                for j in (2, 5):
                    OP('dve', 'tensor_scalar', R=['arow'], W=['arow'], out=arow[:, j * D:(j + 1) * D],
                       in0=arow[:, j * D:(j + 1) * D], scalar1=1.0, scalar2=1.0 / ALPHA, op0=ALU.add, op1=ALU.mult)
                ps, pk = kb.next_ps()
                for t in range(48):
                    kb.mm(ps[:, t:t + 1], [(arow[0:1, t * 128:(t + 1) * 128], ones_f[0:1, 0:1])],
                          R=['arow', 'ones_f'], W=[pk])
                OP('dve', 'tensor_copy', R=[pk], W=['ada'], out=ada[:, l, :], in_=ps[:, 0:48])
                for j, src in enumerate((ln1_g, ln1_b, ln2_g, ln2_b)):
                    kb.dma('sp', lnp[:, l, j, :], _fm(src[l]), W=['lnp'], allow_slow_non_contiguous=True)
            kb.barrier()

        state = dict(nc=nc, kb=kb, OP=OP, inp=inp, xT=xT, XK=XK, ada=ada, lnp=lnp, psX=psX,
                     ident_bf=ident_bf, ident_f=ident_f, ones_bf=ones_bf, ones_f=ones_f, mask01=mask01, jmat=jmat,
                     e8=e8, qmask=qmask, park=park, wsc_g=wsc_g, wsc_br=wsc_br, wsc_o=wsc_o, wsc_fi=wsc_fi,
                     wsc_fo=wsc_fo, cc1_src=cc1_src, cc1_dst=cc1_dst, cc2_src=cc2_src, cc2_dst=cc2_dst,
                     ropec=ropec, ropes=ropes, dbg=dbg, DBG=DBG, egrid_in=egrid_in, selv_in=selv_in)
        state['cast_layer'] = cast_layer
        state['n_layers'] = n_layers
        for l in range(n_layers):
            layer(state, l)

        for k in range(8):
            kb.dma('sp', outT[k * 128:(k + 1) * 128, :], xT[:, k, :], R=['xT%d' % k], W=['out%d' % k])
        kb.finish()
        print("instructions:", kb.ninst)
    return nc, list(dbg_out.keys())


def _ln_slot(st, l, slot, gi, pj, zk, ph_bufs):
    kb, OP, xT, lnp = st['kb'], st['OP'], st['xT'], st['lnp']
    ones_bf = st['ones_bf']
    zb, zq, mean_sb, r_sb, tmpf = ph_bufs
    sl = slice(slot * TT, (slot + 1) * TT)
    ps_s, pks = kb.next_ps()
    kb.mm(ps_s[:, :], [(ones_bf[:, :], zb[:, m, :]) for m in range(8)], R=['zb%d' % m for m in range(8)] + ['ones_bf'], W=[pks])
    ps_q, pkq = kb.next_ps()
    kb.mm(ps_q[:, :], [(ones_bf[:, :], zq[:, m, :]) for m in range(8)], R=['zq%d' % m for m in range(8)] + ['ones_bf'], W=[pkq])
    OP('act', 'activation', R=[pks], W=['mean_sb'], out=mean_sb[:], in_=ps_s[:, :], func=AF.Copy, scale=1.0 / D)
    OP('dve', 'tensor_tensor', R=['mean_sb'], W=['r_sb'], out=r_sb[:], in0=mean_sb[:], in1=mean_sb[:], op=ALU.mult)
    OP('dve', 'scalar_tensor_tensor', R=[pkq, 'r_sb'], W=['r_sb'], out=r_sb[:], in0=ps_q[:, :], scalar=1.0 / D,
       in1=r_sb[:], op0=ALU.mult, op1=ALU.subtract)
    OP('dve', 'tensor_scalar', R=['r_sb'], W=['r_sb'], out=r_sb[:], in0=r_sb[:], scalar1=0.0,
       scalar2=LN_EPS / (ALPHA * ALPHA), op0=ALU.max, op1=ALU.add)
    OP('act', 'activation', R=['r_sb'], W=['r_sb'], out=r_sb[:], in_=r_sb[:], func=AF.Sqrt)
    OP('dve', 'reciprocal', R=['r_sb'], W=['r_sb'], out=r_sb[:], in_=r_sb[:])
    for m in range(8):
        xk = 'xT%d' % m
        e1 = 'dve' if m % 2 == 0 else 'pool'
        OP(e1, 'tensor_tensor', R=[xk, 'mean_sb'], W=['tmpf%d' % (m % 2)], out=tmpf[:, m % 2, :], in0=xT[:, m, sl],
           in1=mean_sb[:], op=ALU.subtract)
        OP(e1, 'tensor_tensor', R=['tmpf%d' % (m % 2), 'r_sb'], W=['tmpf%d' % (m % 2)], out=tmpf[:, m % 2, :],
           in0=tmpf[:, m % 2, :], in1=r_sb[:], op=ALU.mult)
        OP('act', 'activation', R=['tmpf%d' % (m % 2), 'lnp'], W=[xk], out=xT[:, m, sl], in_=tmpf[:, m % 2, :],
           func=AF.Identity, scale=lnp[:, l, pj, m:m + 1], bias=lnp[:, l, pj + 1, m:m + 1])


def _make_hT(st, l, slot, hT, j_shift, j_scale):
    kb, OP, xT, ada = st['kb'], st['OP'], st['xT'], st['ada']
    sl = slice(slot * TT, (slot + 1) * TT)
    for k in range(8):
        OP('act', 'activation', R=['xT%d' % k, 'ada'], W=['hT%d' % k], out=hT[:, k, :], in_=xT[:, k, sl], func=AF.Identity,
           scale=ada[:, l, j_scale * 8 + k:j_scale * 8 + k + 1], bias=ada[:, l, j_shift * 8 + k:j_shift * 8 + k + 1])


def layer(st, l):
    kb, OP, nc, inp = st['kb'], st['OP'], st['nc'], st['inp']
    xT, ada, lnp = st['xT'], st['ada'], st['lnp']
    ident_bf, ident_f, ones_bf, ones_f, mask01 = st['ident_bf'], st['ident_f'], st['ones_bf'], st['ones_f'], st['mask01']
    psX = st['psX']
    HTK = ['hT%d' % k for k in range(8)]
    park = st['park']
    dbg, DBG = st['dbg'], st['DBG']
    w_in, b_in = inp['w_in'], inp['b_in']
    cc1_src, cc1_dst, cc2_src, cc2_dst = st['cc1_src'], st['cc1_dst'], st['cc2_src'], st['cc2_dst']
    if l + 1 < st['n_layers']:
        st['cast_layer'](l + 1)

    def evac(i, ps_ap, out_ap, R, W, bias=None, func=None):
        if func is not None or (bias is not None and i % 2 == 0):
            kw = dict(out=out_ap, in_=ps_ap, func=func or AF.Identity)
            if bias is not None:
                kw['bias'] = bias
            OP('act', 'activation', R=R, W=W, **kw)
        elif bias is not None:
            OP('dve', 'tensor_scalar_add', R=R, W=W, out=out_ap, in0=ps_ap, scalar1=bias)
        elif i % 2 == 0:
            OP('act', 'activation', R=R, W=W, out=out_ap, in_=ps_ap, func=AF.Copy)
        else:
            OP('dve', 'tensor_copy', R=R, W=W, out=out_ap, in_=ps_ap)

    with contextlib.ExitStack() as mix:
        uT = kb.sbuf("uT", [128, 32, 4, 64], BF16, mix)
        xlo = kb.sbuf("xlo", [128, 32, 128], BF16, mix)

        with contextlib.ExitStack() as ph:
            wA = kb.sbuf("wA", [128, 8, 1024], BF16, ph)
            hT = kb.sbuf("hT", [128, 8, TT], BF16, ph)
            Xs5 = kb.sbuf("Xs5", [64, 32, 8, 16], BF16, ph)
            brow_f = kb.sbuf("brow_f", [1, 1024], F32, ph)
            brow = kb.sbuf("brow", [1, 1024], BF16, ph)
            bfm = kb.sbuf("bfm", [128, 8], F32, ph)
            ckv_f = kb.sbuf("ckv_f", [128, 2, TT], F32, ph)
            sq = kb.sbuf("sq", [128, 2, TT], BF16, ph)
            rstd = kb.sbuf("rstd", [128, TT], F32, ph)
            ckvn = kb.sbuf("ckvn", [128, 2, TT], BF16, ph)
            cs = kb.sbuf("cs", [32, 2, TT], F32, ph)
            kt1 = kb.sbuf("kt1", [32, 2, TT], F32, ph)
            kpe = kb.sbuf("kpe", [32, TT], BF16, ph)
            gu = kb.sbuf("gu", [128, 4, TT], BF16, ph)
            gv = kb.sbuf("gv", [128, TT], F32, ph)
            junk = kb.sbuf("junk", [128, TT], BF16, ph)
            stat = kb.sbuf("stat", [128, 8], F32, ph)
            vn = kb.sbuf("vn", [128, TT], F32, ph)
            vnb = kb.sbuf("vnb", [128, TT], BF16, ph)
            ysg = kb.sbuf("ysg", [128, 4, TT], BF16, ph)
            lng = kb.sbuf("lng", [128, 2, 512], F32, ph)
            ws_nat = kb.sbuf("ws_nat", [128, 4, 128], F32, ph)
            wsT = kb.sbuf("wsT", [128, 4, 128], BF16, ph)
            bs_f = kb.sbuf("bs_f", [1, 3, 512], F32, ph)
            bs_hl = kb.sbuf("bs_hl", [1, 2, 512], BF16, ph)

            def wsrc(c0, c1):
                return w_in[l][:, c0:c1].rearrange("(kt p) c -> p kt c", p=128)
            kb.dma('pool', wA[:, :, 0:512], wsrc(0, 512), W=['wA'])
            kb.dma('pool', wA[:, :, 512:800], wsrc(896, 1184), R=[], W=['wA2'])
            OP('dve', 'tensor_scalar_mul', R=['wA2'], W=['wA3'], out=wA[:, :, 800:816], in0=wA[:, :, 784:800], scalar1=-1.0)
            OP('dve', 'tensor_copy', R=['wA2'], W=['wA4'], out=wA[:, :, 816:832], in_=wA[:, :, 768:784])
            WA1 = ['wA', 'wA2', 'wA3', 'wA4']
            kb.dma('sp', brow_f[:, 0:512], b_in[l:l + 1, 0:512], W=['brow_f'])
            kb.dma('sp', brow_f[:, 512:1024], b_in[l:l + 1, 1696:2208], W=['brow_f2'])
            OP('dve', 'tensor_copy', R=['brow_f', 'brow_f2'], W=['brow'], out=brow[:], in_=brow_f[:])
            kb.dma('sp', bfm[:, 0:2], _fm(b_in[l, 896:1152]), W=['bfm0'], allow_slow_non_contiguous=True)
            kb.dma('sp', bfm[:, 2:6], _fm(b_in[l, 1184:1696]), W=['bfm1'], allow_slow_non_contiguous=True)
            kb.dma('sp', bfm[0:32, 6:7], b_in[l, 1152:1184].rearrange("(p o) -> p o", o=1), W=['bfm2'])
            kb.dma('sp', bfm[0:16, 7:8], b_in[l, 1168:1184].rearrange("(p o) -> p o", o=1), W=['bfm3'])
            kb.dma('sp', bfm[16:32, 7:8], b_in[l, 1152:1168].rearrange("(p o) -> p o", o=1), W=['bfm4'])
            OP('dve', 'tensor_scalar_mul', R=['bfm3'], W=['bfm3'], out=bfm[0:16, 7:8], in0=bfm[0:16, 7:8], scalar1=-1.0)
            BF = ['bfm0', 'bfm1', 'bfm2', 'bfm3', 'bfm4']

            for slot in range(NSLOT):
                sl = slice(slot * TT, (slot + 1) * TT)
                _make_hT(st, l, slot, hT, 0, 1)
                kb.dma('sp', cs[:, 0, :], st['ropec'][:, sl], W=['cs0'])
                kb.dma('sp', cs[:, 1, :], st['ropes'][:, sl], W=['cs1'])
                for s_lo in range(8):
                    ps, pk = kb.next_ps()
                    kb.mm(ps[0:64, :], [(hT[:, k, s_lo:TT:8], wA[:, k, 0:512]) for k in range(8)]
                          + [(ones_bf[0:1, 0:64], brow[0:1, 0:512])], R=HTK + ['brow', 'ones_bf'] + WA1, W=[pk])
                    evac(s_lo, ps[0:64, :].rearrange("p (g c) -> p g c", c=16), Xs5[:, :, 7 - s_lo, :], [pk], ['Xs5_%d' % s_lo])
                for g8 in range(4):
                    ps, pk = kb.next_ps()
                    for gi in range(8):
                        g = g8 * 8 + gi
                        kb.mm(ps[:, gi * 64:(gi + 1) * 64],
                              [(Xs5[:, g, :, :].rearrange("p a b -> p (a b)"), ident_bf[0:64, 0:64])],
                              R=['Xs5_%d' % i for i in range(8)] + ['ident_bf'], W=[pk])
                    evac(g8, ps[:, :].rearrange("p (g nl th) -> p g nl th", g=8, nl=16),
                         uT[:, g8 * 8:(g8 + 1) * 8, :, slot * 16:(slot + 1) * 16].rearrange("p g th nl -> p g nl th"),
                         [pk], ['uT'])
                for m in range(2):
                    ps, pk = kb.next_ps()
                    kb.mm(ps[:, :], [(wA[:, k, 512 + m * 128:512 + (m + 1) * 128], hT[:, k, :]) for k in range(8)],
                          R=HTK + WA1, W=[pk])
                    OP('act', 'activation', R=[pk, 'bfm0'], W=['ckv_f'], out=ckv_f[:, m, :], in_=ps[:, :], func=AF.Identity,
                       bias=bfm[:, m:m + 1])
                    OP('act', 'activation', R=[pk, 'bfm0'], W=['sq'], out=sq[:, m, :], in_=ps[:, :], func=AF.Square,
                       bias=bfm[:, m:m + 1])
                ps, pk = kb.next_ps()
                kb.mm(ps[:, :], [(ones_bf[:, :], sq[:, m, :]) for m in range(2)], R=['sq', 'ones_bf'], W=[pk])
                OP('dve', 'tensor_scalar', R=[pk], W=['rstd'], out=rstd[:], in0=ps[:, :], scalar1=1.0 / 256, scalar2=RMS_EPS,
                   op0=ALU.mult, op1=ALU.add)
                OP('act', 'activation', R=['rstd'], W=['rstd'], out=rstd[:], in_=rstd[:], func=AF.Sqrt)
                OP('dve', 'reciprocal', R=['rstd'], W=['rstd'], out=rstd[:], in_=rstd[:])
                for m in range(2):
                    OP('dve', 'tensor_tensor', R=['ckv_f', 'rstd'], W=['ckvn'], out=ckvn[:, m, :], in0=ckv_f[:, m, :],
                       in1=rstd[:], op=ALU.mult)
                    kb.dma('sp', cc1_src.ap()[m * 128:(m + 1) * 128, sl], ckvn[:, m, :], R=['ckvn'],
                           W=['cc1s_%d_%d' % (slot, m)])
                psa, pka = kb.next_ps()
                kb.mm(psa[0:32, :], [(wA[:, k, 768:800], hT[:, k, :]) for k in range(8)], R=HTK + WA1, W=[pka])
                psb, pkb = kb.next_ps()
                kb.mm(psb[0:32, :], [(wA[:, k, 800:832], hT[:, k, :]) for k in range(8)], R=HTK + WA1, W=[pkb])
                OP('dve', 'scalar_tensor_tensor', R=[pka, 'cs0'] + BF, W=['kt1a'], out=kt1[:, 0, :], in0=psa[0:32, :],
                   scalar=bfm[0:32, 6:7], in1=cs[:, 0, :], op0=ALU.add, op1=ALU.mult)
                OP('dve', 'scalar_tensor_tensor', R=[pkb, 'cs1'] + BF, W=['kt1b'], out=kt1[:, 1, :], in0=psb[0:32, :],
                   scalar=bfm[0:32, 7:8], in1=cs[:, 1, :], op0=ALU.add, op1=ALU.mult)
                OP('dve', 'tensor_tensor', R=['kt1a', 'kt1b'], W=['kpe'], out=kpe[:], in0=kt1[:, 0, :], in1=kt1[:, 1, :],
                   op=ALU.add)
                kb.dma('sp', cc1_src.ap()[256:288, sl], kpe[:], R=['kpe'], W=['cc1s_%d_k' % slot])
            CC1K = ['cc1s_%d_%s' % (s_, m_) for s_ in range(NSLOT) for m_ in ('0', '1', 'k')]

            kb.dma('pool', wA[:, :, 0:1024], wsrc(1184, 2208), W=['wA', 'wA2', 'wA3', 'wA4'])
            kb.dma('sp', lng[:, 0, :], inp['sgu_ln_g'][l:l + 1, :].partition_broadcast(128), W=['lng0'])
            kb.dma('sp', lng[:, 1, :], inp['sgu_ln_b'][l:l + 1, :].partition_broadcast(128), W=['lng1'])
            kb.dma('sp', ws_nat[:], inp['sgu_w_s'][l].rearrange("g i j -> i g j"), W=['ws_nat'])
            OP('dve', 'memset', R=['ws_nat'], W=['ws_nat'], ap=ws_nat[0:64, :, 64:128], constant=0.0)
            ps, pk = kb.next_ps()
            for g in range(4):
                kb.mm(ps[:, g * 128:(g + 1) * 128], [(ws_nat[:, g, :], ident_f[:, :])], R=['ws_nat', 'ident_f'], W=[pk])
            OP('dve', 'tensor_copy', R=[pk], W=['wsT'], out=wsT[:].rearrange("p g i -> p (g i)"), in_=ps[:, :])
            kb.dma('sp', bs_f[:, 0, :], inp['sgu_b_s'][l:l + 1].rearrange("o g i -> o (g i)"), W=['bs_f'])
            OP('dve', 'tensor_copy', R=['bs_f'], W=['bs_hl0'], out=bs_hl[:, 0, :], in_=bs_f[:, 0, :])
            OP('dve', 'tensor_copy', R=['bs_hl0'], W=['bs_f1'], out=bs_f[:, 1, :], in_=bs_hl[:, 0, :])
            OP('dve', 'tensor_tensor', R=['bs_f', 'bs_f1'], W=['bs_f2'], out=bs_f[:, 2, :], in0=bs_f[:, 0, :],
               in1=bs_f[:, 1, :], op=ALU.subtract)
            OP('dve', 'tensor_copy', R=['bs_f2'], W=['bs_hl1'], out=bs_hl[:, 1, :], in_=bs_f[:, 2, :])

            for slot in range(NSLOT):
                sl = slice(slot * TT, (slot + 1) * TT)
                _make_hT(st, l, slot, hT, 0, 1)
                for m in range(4):
                    ps, pk = kb.next_ps()
                    kb.mm(ps[:, :], [(wA[:, k, m * 128:(m + 1) * 128], hT[:, k, :]) for k in range(8)], R=HTK + ['wA'], W=[pk])
                    OP('act', 'activation', R=[pk, 'bfm1'], W=['gu%d' % m], out=gu[:, m, :], in_=ps[:, :], func=AF.Gelu_apprx_tanh,
                       bias=bfm[:, 2 + m:3 + m])
                for sub in range(4):
                    ss = slice(sub * 128, (sub + 1) * 128)
                    ps, pk = kb.next_ps()
                    kb.mm(ps[:, :], [(hT[:, k, ss], wA[:, k, 512:1024]) for k in range(8)]
                          + [(ones_bf[0:1, 0:128], brow[0:1, 512:1024])], R=HTK + ['wA', 'brow', 'ones_bf'], W=[pk])
                    OP('pool', 'memset', W=['stat0', 'stat1'], ap=stat[:, 0:2], constant=0.0)
                    OP('act', 'activation', R=[pk], W=['gv', 'stat0'], out=gv[:], in_=ps[:, :], func=AF.Gelu_apprx_tanh,
                       accum_out=stat[:, 0:1])
                    OP('act', 'activation', R=['gv'], W=['junk', 'stat1'], out=junk[:], in_=gv[:], func=AF.Square,
                       accum_out=stat[:, 1:2])
                    OP('dve', 'tensor_scalar_mul', R=['stat0'], W=['stat2'], out=stat[:, 2:3], in0=stat[:, 0:1], scalar1=1.0 / 512)
                    OP('dve', 'tensor_tensor', R=['stat2'], W=['stat3'], out=stat[:, 3:4], in0=stat[:, 2:3], in1=stat[:, 2:3],
                       op=ALU.mult)
                    OP('dve', 'scalar_tensor_tensor', R=['stat1', 'stat3'], W=['stat4'], out=stat[:, 4:5], in0=stat[:, 1:2],
                       scalar=1.0 / 512, in1=stat[:, 3:4], op0=ALU.mult, op1=ALU.subtract)
                    OP('dve', 'tensor_scalar', R=['stat4'], W=['stat4'], out=stat[:, 4:5], in0=stat[:, 4:5], scalar1=0.0,
                       scalar2=LN_EPS, op0=ALU.max, op1=ALU.add)
                    OP('act', 'activation', R=['stat4'], W=['stat4'], out=stat[:, 4:5], in_=stat[:, 4:5], func=AF.Sqrt)
                    OP('dve', 'reciprocal', R=['stat4'], W=['stat5'], out=stat[:, 5:6], in_=stat[:, 4:5])
                    OP('dve', 'scalar_tensor_tensor', R=['stat2', 'stat5'], W=['stat6'], out=stat[:, 6:7], in0=stat[:, 2:3],
                       scalar=-1.0, in1=stat[:, 5:6], op0=ALU.mult, op1=ALU.mult)
                    OP('dve', 'tensor_scalar', R=['gv', 'stat5', 'stat6'], W=['vn'], out=vn[:], in0=gv[:], scalar1=stat[:, 5:6],
                       scalar2=stat[:, 6:7], op0=ALU.mult, op1=ALU.add)
                    OP('pool', 'tensor_tensor', R=['vn', 'lng0'], W=['vn'], out=vn[:], in0=vn[:], in1=lng[:, 0, :], op=ALU.mult)
                    OP('pool', 'tensor_tensor', R=['vn', 'lng1'], W=['vnb'], out=vnb[:], in0=vn[:], in1=lng[:, 1, :], op=ALU.add)
                    ps2, pk2 = kb.next_ps()
                    for g in range(4):
                        gs = slice(g * 128, (g + 1) * 128)
                        kb.mm(ps2[:, gs], [(vnb[:, gs], wsT[:, g, :]), (ones_bf[0:1, 0:128], bs_hl[0:1, 0, gs]),
                                           (ones_bf[0:1, 0:128], bs_hl[0:1, 1, gs])],
                              R=['vnb', 'wsT', 'bs_hl0', 'bs_hl1', 'ones_bf'], W=[pk2])
                    OP('dve', 'tensor_tensor', R=[pk2] + ['gu%d' % i for i in range(4)], W=['ysg%d' % sub], out=ysg[:, :, ss],
                       in0=ps2[:, :].rearrange("p (g i) -> p g i", g=4), in1=gu[:, :, ss], op=ALU.mult)
                kb.dma('sp', park['sgu'][:, :, sl], ysg[:], R=['ysg%d' % i for i in range(4)], W=['park_sgu%d' % slot])
            if 'a1' in dbg:
                kb.dma('sp', DBG('uT', [128, 32 * 256], BF16)[:, :], uT[:].rearrange("p g t n -> p (g t n)"), R=['uT'], W=['dbg_uT'])
                kb.dma('sp', DBG('cc1', [288, NTOK], BF16)[:, :], cc1_src.ap()[:, :], R=CC1K, W=['dbg_cc1'])
                kb.dma('sp', DBG('sgu', [128, 4 * NTOK], BF16)[:, :], park['sgu'].rearrange("p m t -> p (m t)"),
                       R=['park_sgu%d' % i for i in range(4)], W=['dbg_sgu'])
            kb.barrier()
        if 'a1' in dbg:
            return
        _s5_phase(st, l, uT, xlo, CC1K)
    if any(k.startswith('s5') for k in dbg):
        return
    _mla_phase(st, l)
    if 'mla' in dbg:
        return
    _a3_phase(st, l)
    _ffn_phase(st, l)


def _core_tokens(o):
    return np.concatenate([np.arange(G * TT, (G + 1) * TT) for G in TILES[o]])


def _const_tables(o):
    inv_freq = (1.0 / (np.float32(10000.0) ** (np.arange(0, 32, 2, dtype=np.float32) / np.float32(32)))).astype(np.float32)
    idx = _core_tokens(o)
    ang = (idx.astype(np.float32)[:, None] * inv_freq[None, :]).astype(np.float32)
    cos = np.cos(ang).astype(np.float32)
    sin = np.sin(ang).astype(np.float32)
    ropec = np.ascontiguousarray(np.concatenate([cos, cos], 1).T)
    ropes = np.ascontiguousarray(np.concatenate([sin, sin], 1).T)
    qm = np.zeros((8, NSLOT, 2, TT), np.float32)
    q = np.arange(TT)
    for s in range(NSLOT):
        G = TILES[o][s]
        for k in range(2):
            for c in range(8):
                qm[c, s, k, :] = np.where((2 * s + k) * 8 + c > G * 8 + q // 64, NEG, 0.0)
    e8 = (np.arange(TT)[None, :] // 64 == np.arange(8)[:, None]).astype(np.float32)
    egrid = np.concatenate([np.arange(33), np.arange(39) - 7.0]).astype(np.float32)
    selv = np.zeros((128, 8), np.float32)
    for s in range(NSLOT):
        selv[:, s * 2 + (TILES[o][s] - 2 * s)] = 1.0
    return dict(ropec=ropec, ropes=ropes, qmask=qm.reshape(8, -1).astype(ml_dtypes.bfloat16),
                e8=e8.astype(ml_dtypes.bfloat16), egrid=np.ascontiguousarray(np.broadcast_to(egrid, (128, 72))),
                selv=selv)


WEIGHT_KEYS = ['w_ada', 'b_ada', 'w_in', 'b_in', 's5_lambda_re', 's5_lambda_im', 's5_log_dt', 's5_b_re', 's5_b_im',
               's5_c_re', 's5_c_im', 's5_d', 's5_w_glu', 's5_b_glu', 'mla_q_norm', 'mla_w_q_up', 'mla_kv_norm',
               'mla_w_kv_up', 'sgu_ln_g', 'sgu_ln_b', 'sgu_w_s', 'sgu_b_s', 'w_branch', 'w_out', 'ln1_g', 'ln1_b',
               'ffn_w_in', 'ffn_w_out', 'ln2_g', 'ln2_b']


def make_in_maps(inputs, cores, n_layers=DEPTH):
    x = np.asarray(inputs['x'], np.float32)
    c = np.asarray(inputs['c'], np.float32)
    shared = {k: np.ascontiguousarray(np.asarray(inputs[k], np.float32)[:n_layers]) for k in WEIGHT_KEYS}
    maps = []
    for core in cores:
        b, o = core // 2, core % 2
        idx = _core_tokens(o)
        m = dict(shared)
        m['xT'] = np.ascontiguousarray(x[b, idx, :].T)
        m['cvec'] = np.ascontiguousarray(c[b].reshape(8, 128).T)
        m.update(_const_tables(o))
        maps.append(m)
    return maps


def kernel(**inputs):
    nc, _ = build_program(DEPTH)
    cores = list(range(8))
    maps = make_in_maps(inputs, cores)
    res = run_bass_kernel_spmd(nc, maps, core_ids=cores)
    out = np.zeros((NB, S, D), np.float32)
    for core in cores:
        b, o = core // 2, core % 2
        out[b, _core_tokens(o), :] = np.asarray(res.results[core]['outT'], np.float32).T
    return out


def _bc(ap2d, n, axis):
    a = ap2d.shape[1]
    if axis == 2:
        return ap2d.unsqueeze(2).to_broadcast([128, a, n])
    return ap2d.unsqueeze(1).to_broadcast([128, n, a])


def _s5_phase(st, l, uT, xlo, CC1K):
    kb, OP, nc, inp = st['kb'], st['OP'], st['nc'], st['inp']
    ident_bf, ident_f, mask01 = st['ident_bf'], st['ident_f'], st['mask01']
    psX, park, dbg, DBG = st['psX'], st['park'], st['dbg'], st['DBG']
    cc1_src, cc1_dst, cc2_src, cc2_dst = st['cc1_src'], st['cc1_dst'], st['cc2_src'], st['cc2_dst']
    PI = math.pi
    kb.allgather(cc1_src, cc1_dst, R=CC1K, W=['cc1_dst'])
    with contextlib.ExitStack() as s5:
        s5t = kb.sbuf("s5t", [128, 4, 32, 39], F32, s5)
        s5b = kb.sbuf("s5b", [128, 2, 32, 16], F32, s5)
        s5c = kb.sbuf("s5c", [128, 2, 32, 16], F32, s5)
        d_rep = kb.sbuf("d_rep", [128, 32], F32, s5)
        scn = kb.sbuf("scn", [128, 3, 32], F32, s5)
        selv = kb.sbuf("selv", [128, 8], F32, s5)
        Uown = kb.sbuf("Uown", [128, 32, 64], BF16, s5)
        Er_re, Er_im, Ey_re, Ey_im = (s5t[:, i] for i in range(4))
        kb.dma('sp', selv[:], st['selv_in'][:, :], W=['selv'])
        with contextlib.ExitStack() as tb:
            egrid = kb.sbuf("egrid", [128, 72], F32, tb)
            lam = kb.sbuf("lam", [128, 2, 32], F32, tb)
            dts = kb.sbuf("dts", [128, 32], F32, tb)
            ld = kb.sbuf("ld", [128, 2, 32], F32, tb)
            tA = kb.sbuf("tA", [128, 32, 39], F32, tb)
            tB = kb.sbuf("tB", [128, 32, 39], F32, tb)
            tC = kb.sbuf("tC", [128, 32, 39], F32, tb)
            tI = kb.sbuf("tI", [128, 32, 39], mybir.dt.int32, tb)
            sm = kb.sbuf("sm", [128, 12, 32], F32, tb)
            Fs = kb.sbuf("Fs", [128, 3, 32], F32, tb)
            Bri = kb.sbuf("Bri", [128, 2, 32, 16], F32, tb)
            tb1 = kb.sbuf("tb1", [128, 32, 16], F32, tb)
            Cnat = kb.sbuf("Cnat", [128, 2, 4, 128], F32, tb)
            kb.dma('sp', egrid[:], st['egrid_in'][:, :], W=['egrid'])
            for j, src in enumerate((inp['s5_lambda_re'], inp['s5_lambda_im'])):
                for h in range(2):
                    kb.dma('sp', lam[h * 64:(h + 1) * 64, j, :], src[l].rearrange("g p -> p g"), W=['lam%d%d' % (j, h)],
                           allow_slow_non_contiguous=True)
            LAM = ['lam00', 'lam01', 'lam10', 'lam11']
            kb.dma('sp', dts[:], inp['s5_log_dt'][l:l + 1, :].partition_broadcast(128), W=['dts'])
            OP('act', 'activation', R=['dts'], W=['dts'], out=dts[:], in_=dts[:], func=AF.Exp)
            for j in range(2):
                OP('dve', 'tensor_tensor', R=LAM + ['dts'], W=['ld%d' % j], out=ld[:, j, :], in0=lam[:, j, :], in1=dts[:],
                   op=ALU.mult)
            for (Ere, Eim, k0, K) in ((Er_re, Er_im, 0, 33), (Ey_re, Ey_im, 33, 39)):
                eg = _bc(egrid[:, k0:k0 + K], 32, 1)
                OP('dve', 'tensor_tensor', R=['ld0', 'egrid'], W=['tA'], out=tA[:, :, 0:K], in0=_bc(ld[:, 0, :], K, 2), in1=eg,
                   op=ALU.mult)
                OP('act', 'activation', R=['tA'], W=['tA'], out=tA[:, :, 0:K], in_=tA[:, :, 0:K], func=AF.Exp)
                OP('dve', 'tensor_tensor', R=['ld1', 'egrid'], W=['tB'], out=tB[:, :, 0:K], in0=_bc(ld[:, 1, :], K, 2), in1=eg,
                   op=ALU.mult)
                for (dst, off) in ((Eim, 0.0), (Ere, 0.25)):
                    if off:
                        OP('dve', 'tensor_scalar_add', R=['tB'], W=['tB'], out=tB[:, :, 0:K], in0=tB[:, :, 0:K], scalar1=0.5 * PI)
                    OP('dve', 'tensor_scalar_mul', R=['tB'], W=['tC'], out=tC[:, :, 0:K], in0=tB[:, :, 0:K], scalar1=1.0 / (2 * PI))
                    OP('dve', 'tensor_copy', R=['tC'], W=['tI'], out=tI[:, :, 0:K], in_=tC[:, :, 0:K])
                    OP('dve', 'tensor_copy', R=['tI'], W=['tC'], out=tC[:, :, 0:K], in_=tI[:, :, 0:K])
                    OP('dve', 'scalar_tensor_tensor', R=['tC', 'tB'], W=['tC'], out=tC[:, :, 0:K], in0=tC[:, :, 0:K], scalar=-2 * PI,
                       in1=tB[:, :, 0:K], op0=ALU.mult, op1=ALU.add)
                    OP('act', 'activation', R=['tC'], W=['tC'], out=tC[:, :, 0:K], in_=tC[:, :, 0:K], func=AF.Sin)
                    OP('dve', 'tensor_tensor', R=['tA', 'tC'], W=['s5t'], out=dst[:, :, 0:K], in0=tA[:, :, 0:K], in1=tC[:, :, 0:K],
                       op=ALU.mult)
            a_re, a_im = Ey_re[:, :, 8], Ey_im[:, :, 8]
            lr, li = lam[:, 0, :], lam[:, 1, :]

            def S(i):
                return sm[:, i, :]

            def TT_(o, a, b, op):
                OP('dve', 'tensor_tensor', R=['s5t', 'sm'] + LAM, W=['sm'], out=o, in0=a, in1=b, op=op)
            TT_(S(0), lr, lr, ALU.mult)
            TT_(S(1), li, li, ALU.mult)
            TT_(S(0), S(0), S(1), ALU.add)
            OP('dve', 'reciprocal', R=['sm'], W=['sm'], out=S(0), in_=S(0))
            OP('dve', 'tensor_scalar_add', R=['s5t', 'sm'], W=['sm'], out=S(2), in0=a_re, scalar1=-1.0)
            TT_(S(3), S(2), lr, ALU.mult)
            TT_(S(4), a_im, li, ALU.mult)
            TT_(S(3), S(3), S(4), ALU.add)
            TT_(S(3), S(3), S(0), ALU.mult)
            TT_(S(5), a_im, lr, ALU.mult)
            TT_(S(6), S(2), li, ALU.mult)
            TT_(S(5), S(5), S(6), ALU.subtract)
            TT_(S(5), S(5), S(0), ALU.mult)
            OP('dve', 'tensor_scalar_mul', R=['sm'], W=['sm'], out=S(7), in0=S(3), scalar1=-1.0)
            OP('dve', 'tensor_scalar_mul', R=['sm'], W=['sm'], out=S(8), in0=S(5), scalar1=-1.0)
            OP('dve', 'tensor_copy', R=['sm'], W=['Fs'], out=Fs[0:64, 0, :], in_=sm[0:64, 3, :])
            OP('dve', 'tensor_copy', R=['sm', 'Fs'], W=['Fs'], out=Fs[64:128, 0, :], in_=sm[64:128, 8, :])
            OP('dve', 'tensor_copy', R=['sm', 'Fs'], W=['Fs'], out=Fs[0:64, 1, :], in_=sm[0:64, 8, :])
            OP('dve', 'tensor_copy', R=['sm', 'Fs'], W=['Fs'], out=Fs[64:128, 1, :], in_=sm[64:128, 7, :])
            OP('dve', 'tensor_scalar_mul', R=['Fs'], W=['Fs'], out=Fs[:, 2, :], in0=Fs[:, 0, :], scalar1=-1.0)
            for j, src in enumerate((inp['s5_b_re'], inp['s5_b_im'])):
                for h in range(2):
                    kb.dma('sp', Bri[h * 64:(h + 1) * 64, j], src[l].rearrange("g p c -> p g c"), W=['Bri%d%d' % (j, h)])
            BRI = ['Bri00', 'Bri01', 'Bri10', 'Bri11']
            for o_, (fa, fb) in enumerate(((0, 1), (1, 2))):
                OP('dve', 'tensor_tensor', R=BRI + ['Fs'], W=['tb1'], out=tb1[:], in0=Bri[:, 0], in1=_bc(Fs[:, fa, :], 16, 2),
                   op=ALU.mult)
                OP('dve', 'tensor_tensor', R=BRI + ['Fs'], W=['s5b%d' % o_], out=s5b[:, o_], in0=Bri[:, 1],
                   in1=_bc(Fs[:, fb, :], 16, 2), op=ALU.mult)
                OP('dve', 'tensor_tensor', R=['tb1', 's5b%d' % o_], W=['s5b%d' % o_], out=s5b[:, o_], in0=s5b[:, o_], in1=tb1[:],
                   op=ALU.add)
            cre = inp['s5_c_re'][l].rearrange("(t gl) c p -> (gl c) t p", t=4)
            cim = inp['s5_c_im'][l].rearrange("(t gl) c p -> (gl c) t p", t=4)
            kb.dma('sp', Cnat[:, 0, :, 0:64], cre, W=['Cn0'])
            kb.dma('sp', Cnat[:, 0, :, 64:128], cim, W=['Cn1'])
            kb.dma('sp', Cnat[:, 1, :, 0:64], cim, W=['Cn2'])
            kb.dma('sp', Cnat[:, 1, :, 64:128], cre, W=['Cn3'])
            for v in range(2):
                ps, pk = kb.next_ps()
                for t in range(4):
                    kb.mm(ps[:, t * 128:(t + 1) * 128], [(Cnat[:, v, t, :], ident_f[:, :])],
                          R=['Cn0', 'Cn1', 'Cn2', 'Cn3', 'ident_f'], W=[pk])
                dst = s5c[:, v].rearrange("p g c -> p (g c)")
                if v == 0:
                    OP('dve', 'tensor_copy', R=[pk], W=['s5c0'], out=dst, in_=ps[:, :])
                else:
                    OP('dve', 'tensor_scalar_mul', R=[pk], W=['s5c1a'], out=dst[0:64, :], in0=ps[0:64, :], scalar1=-1.0)
                    OP('act', 'activation', R=[pk], W=['s5c1b'], out=dst[64:128, :], in_=ps[64:128, :], func=AF.Copy)
            S5C = ['s5c0', 's5c1a', 's5c1b']
            for s_ in range(8):
                kb.dma('sp', d_rep[s_ * 16:(s_ + 1) * 16, :], inp['s5_d'][l].rearrange("(g c) -> c g", c=16), W=['d_rep%d' % s_],
                       allow_slow_non_contiguous=True)
            DREP = ['d_rep%d' % s_ for s_ in range(8)]
            OP('dve', 'tensor_copy', R=['s5t'], W=['scn'], out=scn[:, 0, :], in_=Er_re[:, :, 32])
            OP('dve', 'tensor_copy', R=['s5t', 'scn'], W=['scn'], out=scn[0:64, 1, :], in_=Er_im[0:64, :, 32])
            OP('dve', 'tensor_scalar_mul', R=['s5t', 'scn'], W=['scn'], out=scn[64:128, 1, :], in0=Er_im[64:128, :, 32], scalar1=-1.0)
            OP('dve', 'tensor_scalar_mul', R=['scn'], W=['scn'], out=scn[:, 2, :], in0=scn[:, 1, :], scalar1=-1.0)
            if 's5t' in dbg:
                kb.dma('sp', DBG('s5t', [128, 4 * 32 * 39])[:, :], s5t[:].rearrange("p a g k -> p (a g k)"), R=['s5t'], W=['dbg1'])
                kb.dma('sp', DBG('s5b', [128, 2 * 512])[:, :], s5b[:].rearrange("p a g k -> p (a g k)"), R=['s5b0', 's5b1'], W=['dbg2'])
                kb.dma('sp', DBG('s5c', [128, 2 * 512])[:, :], s5c[:].rearrange("p a g k -> p (a g k)"), R=S5C, W=['dbg3'])
                kb.dma('sp', DBG('scn', [128, 96])[:, :], scn[:].rearrange("p a g -> p (a g)"), R=['scn'], W=['dbg4'])
            kb.barrier()
        if 's5t' in dbg:
            return

        with contextlib.ExitStack() as p1:
            Xrev = [kb.sbuf("Xrev%d" % i, [128, 33, 16], BF16, p1) for i in range(2)]
            t1 = kb.sbuf("t1", [128, 39, 16], F32, p1)
            t2 = kb.sbuf("t2", [128, 39, 16], F32, p1)
            Gt = [kb.sbuf("Gt%d" % i, [128, 4, 128], BF16, p1) for i in range(2)]
            Gs = [kb.sbuf("Gs%d" % i, [128, 4, 128], BF16, p1) for i in range(2)]
            IUo = kb.sbuf("IUo", [128, 64, 32], F32, p1)
            IVo = kb.sbuf("IVo", [128, 64, 32], F32, p1)
            P1E = os.environ.get('S5_ENG', 'pool')
            NG1 = int(os.environ.get("S5_NG", "32"))
            if NG1 < 32:
                OP('dve', 'memset', W=['IUo7'], ap=IUo[:], constant=0.0)
                OP('dve', 'memset', W=['IVo7'], ap=IVo[:], constant=0.0)
            for g in range(NG1):
                i = g % 2
                OP(P1E, 'tensor_tensor', R=['s5t', 's5b0'], W=['t1'], out=t1[:, 0:33, :], in0=_bc(Er_re[:, g, 0:33], 16, 2),
                   in1=_bc(s5b[:, 0, g, :], 33, 1), op=ALU.mult)
                OP(P1E, 'tensor_tensor', R=['s5t', 's5b1'], W=['t2'], out=t2[:, 0:33, :], in0=_bc(Er_im[:, g, 0:33], 16, 2),
                   in1=_bc(s5b[:, 1, g, :], 33, 1), op=ALU.mult)
                OP(P1E, 'tensor_tensor', R=['t1', 't2'], W=['Xrev%d' % i], out=Xrev[i][:], in0=t1[:, 0:33, :],
                   in1=t2[:, 0:33, :], op=ALU.add)
                OP('act', 'activation', R=['Xrev%d' % i], W=['xlo%d' % g], out=xlo[:, g, :],
                   in_=Xrev[i][:, 0:8, :].rearrange("p a b -> p (a b)"), func=AF.Copy)
                ps, pk = kb.next_ps()
                for sh in range(4):
                    kb.mm(ps[:, sh * 128:(sh + 1) * 128],
                          [(Xrev[i][:, 25 - 8 * sh:33 - 8 * sh, :].rearrange("p a b -> p (a b)"), ident_bf[:, :])],
                          R=['Xrev%d' % i, 'ident_bf'], W=[pk])
                ps3 = ps[:, :].rearrange("p (s j) -> p s j", s=4)
                OP('act', 'activation', R=[pk], W=['Gt%d' % i], out=Gt[i][:], in_=ps3, func=AF.Copy)
                OP('dve', 'tensor_copy', R=[pk], W=['Gs%da' % i], out=Gs[i][:, :, 0:64], in_=ps3[:, :, 64:128])
                OP('dve', 'tensor_copy', R=[pk], W=['Gs%db' % i], out=Gs[i][:, :, 64:128], in_=ps3[:, :, 0:64])
                gi = g % 8
                kb.mm(psX[0][0][:, gi * 64:(gi + 1) * 64], [(Gt[i][:, sh, :], uT[:, g, sh, :]) for sh in range(4)],
                      R=['Gt%d' % i, 'uT'], W=[psX[0][1]])
                kb.mm(psX[1][0][:, gi * 64:(gi + 1) * 64], [(Gs[i][:, sh, :], uT[:, g, sh, :]) for sh in range(4)],
                      R=['Gs%da' % i, 'Gs%db' % i, 'uT'], W=[psX[1][1]])
                if gi == 7:
                    for (pp, dst, key) in ((psX[0], IUo, 'IUo'), (psX[1], IVo, 'IVo')):
                        OP('act' if key == 'IUo' else 'dve', 'activation' if key == 'IUo' else 'tensor_copy', R=[pp[1]],
                           W=[key + str(g)], out=dst[:, :, g - 7:g + 1].rearrange("p n g -> p g n"),
                           in_=pp[0][:, :].rearrange("p (g n) -> p g n", g=8), **({'func': AF.Copy} if key == 'IUo' else {}))
            if os.environ.get("S5_STOP") == "1":
                kb.dma('sp', DBG('IUo', [128, 2048])[:, :], IUo[:].rearrange("p n g -> p (n g)"), R=['IUo7'], W=['dbgq'])
                kb.barrier()
                return
            kb.dma('sp', cc2_src.ap()[0:128, :], IUo[:].rearrange("p n g -> p (n g)"), R=['IUo%d' % g for g in (7, 15, 23, 31)],
                   W=['cc2s0'])
            kb.dma('sp', cc2_src.ap()[128:256, :], IVo[:].rearrange("p n g -> p (n g)"), R=['IVo%d' % g for g in (7, 15, 23, 31)],
                   W=['cc2s1'])
            kb.allgather(cc2_src, cc2_dst, R=['cc2s0', 'cc2s1'], W=['cc2_dst'])
            if 's5p1' in dbg:
                kb.dma('sp', DBG('cc2', [512, 2048])[:, :], cc2_dst.ap()[:, :], R=['cc2_dst'], W=['dbg5'])
            kb.barrier()
        if 's5p1' in dbg:
            return

        with contextlib.ExitStack() as sc:
            U = kb.sbuf("U", [128, 128, 32], F32, sc)
            V = kb.sbuf("V", [128, 128, 32], F32, sc)
            sa = kb.sbuf("sa", [128, 2, 32], F32, sc)
            sb_ = kb.sbuf("sb_", [128, 2, 32], F32, sc)
            selt = kb.sbuf("selt", [128, 16, 32], F32, sc)
            for r in range(2):
                for slot in range(NSLOT):
                    G = TILES[r][slot]
                    for (dst, roff, key) in ((U, 0, 'U'), (V, 128, 'V')):
                        kb.dma('sp', dst[:, G * 16:(G + 1) * 16, :],
                               cc2_dst.ap()[r * 256 + roff:r * 256 + roff + 128, slot * 512:(slot + 1) * 512]
                               .rearrange("p (n g) -> p n g", g=32), R=['cc2_dst'], W=['%s%d' % (key, G)])
            for n in range(1, 128):
                a, b = (n - 1) // 16, n // 16
                Ua, Va, Ub, Vb = 'U%d' % a, 'V%d' % a, 'U%d' % b, 'V%d' % b
                OP('dve', 'tensor_tensor', R=[Va, 'scn'], W=['sa0'], out=sa[:, 0, :], in0=V[:, n - 1, :], in1=scn[:, 1, :], op=ALU.mult)
                OP('dve', 'tensor_tensor', R=[Ua, 'scn'], W=['sa1'], out=sa[:, 1, :], in0=U[:, n - 1, :], in1=scn[:, 0, :], op=ALU.mult)
                OP('pool', 'tensor_tensor', R=[Ua, 'scn'], W=['sb0'], out=sb_[:, 0, :], in0=U[:, n - 1, :], in1=scn[:, 2, :], op=ALU.mult)
                OP('pool', 'tensor_tensor', R=[Va, 'scn'], W=['sb1'], out=sb_[:, 1, :], in0=V[:, n - 1, :], in1=scn[:, 0, :], op=ALU.mult)
                OP('dve', 'tensor_tensor', R=['sa0', Ub], W=[Ub], out=U[:, n, :], in0=U[:, n, :], in1=sa[:, 0, :], op=ALU.add)
                OP('dve', 'tensor_tensor', R=['sa1', Ub], W=[Ub], out=U[:, n, :], in0=U[:, n, :], in1=sa[:, 1, :], op=ALU.add)
                OP('pool', 'tensor_tensor', R=['sb0', Vb], W=[Vb], out=V[:, n, :], in0=V[:, n, :], in1=sb_[:, 0, :], op=ALU.add)
                OP('pool', 'tensor_tensor', R=['sb1', Vb], W=[Vb], out=V[:, n, :], in0=V[:, n, :], in1=sb_[:, 1, :], op=ALU.add)
            UK = ['U%d' % i for i in range(8)]
            for s_ in range(NSLOT):
                T0, T1 = 2 * s_, 2 * s_ + 1
                if T0 == 0:
                    OP('dve', 'memset', W=['selt'], ap=selt[:, 0, :], constant=0.0)
                    OP('dve', 'tensor_scalar_mul', R=UK + ['selv', 'selt'], W=['selt'], out=selt[:, 1:16, :], in0=U[:, 0:15, :],
                       scalar1=selv[:, 0:1])
                else:
                    OP('dve', 'tensor_scalar_mul', R=UK + ['selv'], W=['selt'], out=selt[:], in0=U[:, T0 * 16 - 1:T0 * 16 + 15, :],
                       scalar1=selv[:, 2 * s_:2 * s_ + 1])
                OP('dve', 'scalar_tensor_tensor', R=UK + ['selv', 'selt'], W=['Uown'],
                   out=Uown[:, :, s_ * 16:(s_ + 1) * 16].rearrange("p g n -> p n g"), in0=U[:, T1 * 16 - 1:T1 * 16 + 15, :],
                   scalar=selv[:, 2 * s_ + 1:2 * s_ + 2], in1=selt[:], op0=ALU.mult, op1=ALU.add)
            if 's5scan' in dbg:
                kb.dma('sp', DBG('Uown', [128, 2048], BF16)[:, :], Uown[:].rearrange("p g n -> p (g n)"), R=['Uown'], W=['dbg6'])
            kb.barrier()
        if 's5scan' in dbg:
            return

        with contextlib.ExitStack() as p2:
            Yx = [kb.sbuf("Yx%d" % i, [128, 39, 16], BF16, p2) for i in range(2)]
            t1 = kb.sbuf("t1b", [128, 39, 16], F32, p2)
            t2 = kb.sbuf("t2b", [128, 39, 16], F32, p2)
            Wg = [kb.sbuf("Wg%d" % i, [128, 512], BF16, p2) for i in range(2)]
            tmpw = kb.sbuf("tmpw", [128, 128], F32, p2)
            zz = [kb.sbuf("zz%d" % i, [128, 256], BF16, p2) for i in range(2)]
            Ztok = kb.sbuf("Ztok", [128, 2, 8, 512], BF16, p2)
            zT = kb.sbuf("zT", [128, 4, NTOK], BF16, p2)
            wglu = kb.sbuf("wglu", [128, 4, 512], BF16, p2)
            bglu = kb.sbuf("bglu", [128, 4], F32, p2)
            sg = kb.sbuf("sg", [128, TT], F32, p2)
            ys5 = kb.sbuf("ys5", [128, 4, TT], BF16, p2)
            kb.dma('pool', wglu[:], inp['s5_w_glu'][l].rearrange("(kt p) c -> p kt c", p=128), W=['wglu'])
            kb.dma('sp', bglu[:], _fm(inp['s5_b_glu'][l]), W=['bglu'], allow_slow_non_contiguous=True)
            psZ = None
            for g in range(32):
                i = g % 2
                OP('pool', 'tensor_tensor', R=['s5t', 's5c0'], W=['t1'], out=t1[:], in0=_bc(Ey_re[:, g, :], 16, 2),
                   in1=_bc(s5c[:, 0, g, :], 39, 1), op=ALU.mult)
                OP('pool', 'tensor_tensor', R=['s5t', 's5c1a', 's5c1b'], W=['t2'], out=t2[:], in0=_bc(Ey_im[:, g, :], 16, 2),
                   in1=_bc(s5c[:, 1, g, :], 39, 1), op=ALU.mult)
                OP('pool', 'tensor_tensor', R=['t1', 't2'], W=['Yx%d' % i], out=Yx[i][:], in0=t1[:], in1=t2[:], op=ALU.add)
                psW, pkW = kb.next_ps()
                kb.mm(psW[:, :], [(xlo[:, g, :], Yx[i][:, 0:32, :].rearrange("p a b -> p (a b)"))], R=['xlo%d' % g, 'Yx%d' % i], W=[pkW])
                OP('dve', 'tensor_tensor', R=[pkW, 'mask01'], W=['tmpw'], out=tmpw[:], in0=psW[:, 0:128], in1=mask01[:], op=ALU.mult)
                OP('dve', 'scalar_tensor_tensor', R=['tmpw', 'jmat'] + ['d_rep%d' % s_ for s_ in range(8)], W=['Wg%da' % i],
                   out=Wg[i][:, 0:128], in0=st['jmat'][:, :], scalar=d_rep[:, g:g + 1], in1=tmpw[:], op0=ALU.mult, op1=ALU.add)
                OP('act', 'activation', R=[pkW], W=['Wg%db' % i], out=Wg[i][:, 128:512], in_=psW[:, 128:512], func=AF.Copy)
                psO, pkO = kb.next_ps()
                items = [(psO[:, 0:256], Wg[i][:, 0:128], uT[:, g].rearrange("p t n -> p (t n)"), True, False)]
                for d_ in range(1, 4):
                    items.append((psO[:, d_ * 64:256], Wg[i][:, d_ * 128:(d_ + 1) * 128],
                                  uT[:, g, 0:4 - d_, :].rearrange("p t n -> p (t n)"), False, False))
                for th in range(4):
                    items.append((psO[:, th * 64:(th + 1) * 64], Yx[i][:, 7 + 8 * th:15 + 8 * th, :].rearrange("p a b -> p (a b)"),
                                  Uown[:, g, :], False, th == 3))
                kb.mmg(items, R=['Wg%da' % i, 'Wg%db' % i, 'uT', 'Yx%d' % i, 'Uown'], W=[pkO])
                OP('act', 'activation', R=[pkO], W=['zz%d' % i], out=zz[i][:], in_=psO[:, 0:256], func=AF.Gelu_apprx_tanh)
                if i == 0:
                    psZ, pkZ = kb.next_ps()
                for mt in range(2):
                    kb.mm(psZ[:, (i * 2 + mt) * 128:(i * 2 + mt + 1) * 128], [(zz[i][:, mt * 128:(mt + 1) * 128], ident_bf[:, :])],
                          R=['zz%d' % i, 'ident_bf'], W=[pkZ])
                if i == 1:
                    pz = psZ[:, :].rearrange("p (gl mt t c) -> p gl mt t c", gl=2, mt=2, t=8)
                    for mt in range(2):
                        evac_e = 'act' if mt == 0 else 'dve'
                        kw = dict(out=Ztok[:, mt, :, (g - 1) * 16:(g + 1) * 16].rearrange("p t (gl c) -> p gl t c", c=16),
                                  in_=pz[:, :, mt])
                        if evac_e == 'act':
                            OP('act', 'activation', R=[pkZ], W=['Ztok%d_%d' % (mt, g)], func=AF.Copy, **kw)
                        else:
                            OP('dve', 'tensor_copy', R=[pkZ], W=['Ztok%d_%d' % (mt, g)], **kw)
            ZK = ['Ztok%d_%d' % (mt, g) for mt in range(2) for g in range(1, 32, 2)]
            cnt = 0
            for mt in range(2):
                for s_lo in range(8):
                    ps, pk = kb.next_ps()
                    for ct in range(4):
                        kb.mm(ps[:, ct * 128:(ct + 1) * 128], [(Ztok[:, mt, s_lo, ct * 128:(ct + 1) * 128], ident_bf[:, :])],
                              R=ZK + ['ident_bf'], W=[pk])
                    for ct in range(4):
                        dst = zT[:, ct, :].rearrange("p (slot nl th s) -> p th slot nl s", slot=4, nl=16, th=4, s=8)
                        dst = dst[:, 2 * mt:2 * mt + 2, :, :, s_lo]
                        src = ps[:, ct * 128:(ct + 1) * 128].rearrange("p (th slot nl) -> p th slot nl", th=2, slot=4)
                        cnt += 1
                        if cnt % 2:
                            OP('act', 'activation', R=[pk], W=['zT%d_%d_%d' % (ct, mt, s_lo)], out=dst, in_=src, func=AF.Copy)
                        else:
                            OP('dve', 'tensor_copy', R=[pk], W=['zT%d_%d_%d' % (ct, mt, s_lo)], out=dst, in_=src)
            ZT = ['zT%d_%d_%d' % (ct, mt, s_lo) for ct in range(4) for mt in range(2) for s_lo in range(8)]
            for slot in range(NSLOT):
                sl = slice(slot * TT, (slot + 1) * TT)
                for m in range(4):
                    ps, pk = kb.next_ps()
                    kb.mm(ps[:, :], [(wglu[:, k, m * 128:(m + 1) * 128], zT[:, k, sl]) for k in range(4)], R=ZT + ['wglu'], W=[pk])
                    OP('act', 'activation', R=[pk, 'bglu'], W=['sg'], out=sg[:], in_=ps[:, :], func=AF.Sigmoid, bias=bglu[:, m:m + 1])
                    OP('dve', 'tensor_tensor', R=['sg'] + ZT, W=['ys5_%d' % m], out=ys5[:, m, :], in0=zT[:, m, sl], in1=sg[:], op=ALU.mult)
                kb.dma('sp', park['s5'][:, :, sl], ys5[:], R=['ys5_%d' % m for m in range(4)], W=['park_s5%d' % slot])
            if 's5' in dbg:
                kb.dma('sp', DBG('s5', [128, 4 * NTOK], BF16)[:, :], park['s5'].rearrange("p m t -> p (m t)"),
                       R=['park_s5%d' % i for i in range(4)], W=['dbg_s5'])
            kb.barrier()


def _mla_phase(st, l):
    kb, OP, nc, inp = st['kb'], st['OP'], st['nc'], st['inp']
    ones_bf, ones_f, e8, qmask = st['ones_bf'], st['ones_f'], st['e8'], st['qmask']
    psX, park, dbg, DBG = st['psX'], st['park'], st['dbg'], st['DBG']
    cc1_dst = st['cc1_dst']
    w_in, b_in = inp['w_in'], inp['b_in']
    HTK = ['hT%d' % k for k in range(8)]
    SCALE = 96.0 ** -0.5
    with contextlib.ExitStack() as ph:
        cqnT = kb.sbuf("cqnT", [128, 3, NTOK], BF16, ph)
        wqA = kb.sbuf("wqA", [128, 3, 8, 96], BF16, ph)
        wqB = kb.sbuf("wqB", [128, 3, 8, 32], BF16, ph)
        wkn = kb.sbuf("wkn", [128, 2, 8, 96], BF16, ph)
        wkvv = kb.sbuf("wkvv", [128, 2, 8, 64], BF16, ph)
        qg = kb.sbuf("qg", [128, 5], F32, ph)
        bcq = kb.sbuf("bcq", [128, 3], F32, ph)
        kb.dma('sp', qg[:, 0:3], _fm(inp['mla_q_norm'][l]), W=['qg0'], allow_slow_non_contiguous=True)
        kb.dma('sp', qg[:, 3:5], _fm(inp['mla_kv_norm'][l]), W=['qg1'], allow_slow_non_contiguous=True)
        kb.dma('sp', bcq[:], _fm(b_in[l, 512:896]), W=['bcq'], allow_slow_non_contiguous=True)
        with contextlib.ExitStack() as wp:
            wq_raw = kb.sbuf("wq_raw", [128, 3, 768], F32, wp)
            wkv_raw = kb.sbuf("wkv_raw", [128, 2, 1024], F32, wp)
            kb.dma('sp', wq_raw[:], inp['mla_w_q_up'][l].rearrange("(kt p) c -> p kt c", p=128), W=['wq_raw'])
            kb.dma('sp', wkv_raw[:], inp['mla_w_kv_up'][l].rearrange("(kt p) c -> p kt c", p=128), W=['wkv_raw'])
            OP('pool', 'memset', W=['wkn'], ap=wkn[:], constant=0.0)
            for kt in range(3):
                src = wq_raw[:, kt, :].rearrange("p (h c) -> p h c", c=96)
                g_ = qg[:, kt:kt + 1]
                OP('dve', 'tensor_scalar_mul', R=['wq_raw', 'qg0'], W=['wqA'], out=wqA[:, kt, :, 0:32], in0=src[:, :, 64:96], scalar1=g_)
                OP('dve', 'tensor_scalar_mul', R=['wq_raw', 'qg0'], W=['wqA'], out=wqA[:, kt, :, 32:96], in0=src[:, :, 0:64], scalar1=g_)
                OP('dve', 'tensor_scalar', R=['wq_raw', 'qg0'], W=['wqB'], out=wqB[:, kt, :, 0:16], in0=src[:, :, 80:96], scalar1=g_,
                   scalar2=-1.0, op0=ALU.mult, op1=ALU.mult)
                OP('dve', 'tensor_scalar_mul', R=['wq_raw', 'qg0'], W=['wqB'], out=wqB[:, kt, :, 16:32], in0=src[:, :, 64:80], scalar1=g_)
            for kt in range(2):
                src = wkv_raw[:, kt, :].rearrange("p (h c) -> p h c", c=128)
                g_ = qg[:, 3 + kt:4 + kt]
                OP('dve', 'tensor_scalar_mul', R=['wkv_raw', 'qg1', 'wkn'], W=['wkn'], out=wkn[:, kt, :, 32:96], in0=src[:, :, 0:64], scalar1=g_)
                OP('dve', 'tensor_scalar_mul', R=['wkv_raw', 'qg1'], W=['wkvv'], out=wkvv[:, kt, :, :], in0=src[:, :, 64:128], scalar1=g_)
            kb.barrier()
        with contextlib.ExitStack() as cqp:
            hT = kb.sbuf("hTm", [128, 8, TT], BF16, cqp)
            wcq = kb.sbuf("wcq", [128, 8, 384], BF16, cqp)
            cq_f = kb.sbuf("cq_f", [128, 3, TT], F32, cqp)
            sq = kb.sbuf("sqm", [128, 3, TT], BF16, cqp)
            rstd = kb.sbuf("rstdm", [128, TT], F32, cqp)
            kb.dma('pool', wcq[:], w_in[l][:, 512:896].rearrange("(kt p) c -> p kt c", p=128), W=['wcq'])
            for slot in range(NSLOT):
                sl = slice(slot * TT, (slot + 1) * TT)
                _make_hT(st, l, slot, hT, 0, 1)
                for m in range(3):
                    ps, pk = kb.next_ps()
                    kb.mm(ps[:, :], [(wcq[:, k, m * 128:(m + 1) * 128], hT[:, k, :]) for k in range(8)], R=HTK + ['wcq'], W=[pk])
                    OP('act', 'activation', R=[pk, 'bcq'], W=['cq_f%d' % m], out=cq_f[:, m, :], in_=ps[:, :], func=AF.Identity,
                       bias=bcq[:, m:m + 1])
                    OP('act', 'activation', R=[pk, 'bcq'], W=['sq%d' % m], out=sq[:, m, :], in_=ps[:, :], func=AF.Square,
                       bias=bcq[:, m:m + 1])
                ps, pk = kb.next_ps()
                kb.mm(ps[:, :], [(ones_bf[:, :], sq[:, m, :]) for m in range(3)], R=['sq0', 'sq1', 'sq2', 'ones_bf'], W=[pk])
                OP('dve', 'tensor_scalar', R=[pk], W=['rstd'], out=rstd[:], in0=ps[:, :], scalar1=1.0 / 384, scalar2=RMS_EPS,
                   op0=ALU.mult, op1=ALU.add)
                OP('act', 'activation', R=['rstd'], W=['rstd'], out=rstd[:], in_=rstd[:], func=AF.Sqrt)
                OP('dve', 'reciprocal', R=['rstd'], W=['rstd'], out=rstd[:], in_=rstd[:])
                for m in range(3):
                    OP('dve' if m != 1 else 'pool', 'tensor_tensor', R=['cq_f%d' % m, 'rstd'], W=['cqn%d_%d' % (m, slot)],
                       out=cqnT[:, m, sl], in0=cq_f[:, m, :], in1=rstd[:], op=ALU.mult)
            kb.barrier()
        ckvT = kb.sbuf("ckvT", [128, 2, S], BF16, ph)
        KT = [kb.sbuf("KT%d" % i, [96, S], BF16, ph) for i in range(2)]
        QT = [kb.sbuf("QT%d" % i, [96, NTOK], BF16, ph) for i in range(2)]
        Vp = [kb.sbuf("Vp%d" % i, [128, 32, 128], BF16, ph) for i in range(2)]
        pT = [kb.sbuf("pT%d" % i, [128, TT], BF16, ph) for i in range(4)]
        csq = kb.sbuf("csq", [32, 2, NTOK], F32, ph)
        qt1 = kb.sbuf("qt1", [32, 2, TT], F32, ph)
        rden = kb.sbuf("rden", [128, TT], F32, ph)
        bc_sb = kb.sbuf("bc_sb", [128, TT], F32, ph)
        yt = [kb.sbuf("yt%d" % i, [128, TT], BF16, ph) for i in range(2)]
        kb.dma('sp', csq[:, 0, :], st['ropec'][:, :], W=['csq0'])
        kb.dma('sp', csq[:, 1, :], st['ropes'][:, :], W=['csq1'])
        for r in range(2):
            for slot in range(NSLOT):
                G = TILES[r][slot]
                gs = slice(G * TT, (G + 1) * TT)
                sl = slice(slot * TT, (slot + 1) * TT)
                for m in range(2):
                    kb.dma('sp', ckvT[:, m, gs], cc1_dst.ap()[r * 288 + m * 128:r * 288 + (m + 1) * 128, sl], R=['cc1_dst'],
                           W=['ckvT%d_%d' % (m, G)])
                for i in range(2):
                    kb.dma('sp', KT[i][0:32, gs], cc1_dst.ap()[r * 288 + 256:r * 288 + 288, sl], R=['cc1_dst'], W=['KTk%d_%d' % (i, G)])
        CKV = ['ckvT%d_%d' % (m, G) for m in range(2) for G in range(8)]
        for i in range(2):
            OP('pool', 'memset', W=['Vp%d' % i], ap=Vp[i][:], constant=0.0)
        OP('pool', 'memset', R=['Vp0'], W=['Vp0'], ap=Vp[0][:, :, 64:65], constant=1.0)
        OP('pool', 'memset', R=['Vp1'], W=['Vp1'], ap=Vp[1][:, :, 32:33], constant=1.0)
        CQN = ['cqn%d_%d' % (m, s_) for m in range(3) for s_ in range(NSLOT)]
        ecnt = 0
        for h in range(8):
            i = h % 2
            KT_, QT_, Vp_ = KT[i], QT[i], Vp[i]
            voff = 0 if i == 0 else 64
            for kc in range(8):
                cs_ = slice(kc * TT, (kc + 1) * TT)
                ps, pk = kb.next_ps()
                kb.mm(ps[0:96, :], [(wkn[:, kt, h, :], ckvT[:, kt, cs_]) for kt in range(2)], R=CKV + ['wkn'], W=[pk])
                OP('act', 'activation', R=[pk], W=['KTn%d_%d' % (i, kc)], out=KT_[32:64, cs_], in_=ps[32:64, :], func=AF.Copy)
                OP('dve', 'tensor_copy', R=[pk], W=['KTm%d_%d' % (i, kc)], out=KT_[64:96, cs_], in_=ps[64:96, :])
            for k8 in range(4):
                ps, pk = kb.next_ps()
                for j in range(8):
                    kt = k8 * 8 + j
                    kb.mm(ps[:, j * 64:(j + 1) * 64], [(ckvT[:, m, kt * 128:(kt + 1) * 128], wkvv[:, m, h, :]) for m in range(2)],
                          R=CKV + ['wkvv'], W=[pk])
                ecnt += 1
                kw = dict(out=Vp_[:, k8 * 8:(k8 + 1) * 8, voff:voff + 64], in_=ps[:, :].rearrange("p (j d) -> p j d", d=64))
                if ecnt % 2:
                    OP('act', 'activation', R=[pk], W=['Vp%d' % i], func=AF.Copy, **kw)
                else:
                    OP('dve', 'tensor_copy', R=[pk], W=['Vp%d' % i], **kw)
            for slot in range(NSLOT):
                sl = slice(slot * TT, (slot + 1) * TT)
                psa, pka = kb.next_ps()
                kb.mm(psa[0:96, :], [(wqA[:, kt, h, :], cqnT[:, kt, sl]) for kt in range(3)], R=CQN + ['wqA'], W=[pka])
                psb, pkb = kb.next_ps()
                kb.mm(psb[0:32, :], [(wqB[:, kt, h, :], cqnT[:, kt, sl]) for kt in range(3)], R=CQN + ['wqB'], W=[pkb])
                OP('act', 'activation', R=[pka], W=['QTn%d_%d' % (i, slot)], out=QT_[32:64, sl], in_=psa[32:64, :], func=AF.Copy)
                OP('act', 'activation', R=[pka], W=['QTm%d_%d' % (i, slot)], out=QT_[64:96, sl], in_=psa[64:96, :], func=AF.Copy)
                OP('dve', 'tensor_tensor', R=[pka, 'csq0'], W=['qt1a'], out=qt1[:, 0, :], in0=psa[0:32, :], in1=csq[:, 0, sl], op=ALU.mult)
                OP('dve', 'tensor_tensor', R=[pkb, 'csq1'], W=['qt1b'], out=qt1[:, 1, :], in0=psb[0:32, :], in1=csq[:, 1, sl], op=ALU.mult)
                OP('pool', 'tensor_tensor', R=['qt1a', 'qt1b'], W=['QTr%d_%d' % (i, slot)], out=QT_[0:32, sl], in0=qt1[:, 0, :],
                   in1=qt1[:, 1, :], op=ALU.add)
            KTK = ['KTn%d_%d' % (i, kc) for kc in range(8)] + ['KTm%d_%d' % (i, kc) for kc in range(8)] + ['KTk%d_%d' % (i, G) for G in range(8)]
            for slot in range(NSLOT):
                sl = slice(slot * TT, (slot + 1) * TT)
                o_ps, o_k = psX[slot % 2]
                nkt = 8 * (slot + 1)
                QK = ['QTn%d_%d' % (i, slot), 'QTm%d_%d' % (i, slot), 'QTr%d_%d' % (i, slot)]
                for kt in range(nkt):
                    t512 = kt // 4
                    pairs = [(KT_[0:96, kt * 128:(kt + 1) * 128], QT_[0:96, sl])]
                    if t512 >= 2 * slot:
                        cand = t512 - 2 * slot
                        pairs.append((e8[0:8, (kt % 4) * 128:(kt % 4 + 1) * 128],
                                      qmask[0:8, (slot * 2 + cand) * TT:(slot * 2 + cand + 1) * TT]))
                    ps, pk = kb.next_ps()
                    kb.mm(ps[:, :], pairs, R=KTK + QK + ['e8', 'qmask'], W=[pk])
                    pt = pT[kt % 4]
                    OP('act', 'activation', R=[pk], W=['pT%d' % (kt % 4)], out=pt[:], in_=ps[:, :], func=AF.Exp, scale=SCALE)
                    kb.mmg([(o_ps[:, :], Vp_[:, kt, :], pt[:], kt == 0, kt == nkt - 1)], R=['pT%d' % (kt % 4), 'Vp%d' % i], W=[o_k])
                row = 64 if i == 0 else 32
                rows = slice(0, 64) if i == 0 else slice(64, 128)
                y_ = yt[slot % 2]
                OP('dve', 'reciprocal', R=[o_k], W=['rden'], out=rden[row:row + 1, :], in_=o_ps[row:row + 1, :])
                ps, pk = kb.next_ps()
                kb.mm(ps[:, :], [(ones_f[row:row + 1, :], rden[row:row + 1, :])], R=['rden', 'ones_f'], W=[pk])
                OP('act', 'activation', R=[pk], W=['bc_sb'], out=bc_sb[rows, :], in_=ps[rows, :], func=AF.Copy)
                OP('dve', 'tensor_tensor', R=[o_k, 'bc_sb'], W=['yt%d' % (slot % 2)], out=y_[rows, :], in0=o_ps[rows, :], in1=bc_sb[rows, :],
                   op=ALU.mult)
                kb.dma('sp', park['mla'][rows, h // 2, sl], y_[rows, :], R=['yt%d' % (slot % 2)], W=['park_mla%d_%d' % (h, slot)])
        if 'mla' in dbg:
            kb.dma('sp', DBG('mla', [128, 4 * NTOK], BF16)[:, :], park['mla'].rearrange("p m t -> p (m t)"),
                   R=['park_mla%d_%d' % (h, s_) for h in range(8) for s_ in range(NSLOT)], W=['dbg_mla'])
        kb.barrier()


def _resid_ln(st, l, slot, m, ps, pk, gj, zb, zq):
    OP, xT, ada = st['OP'], st['xT'], st['ada']
    sl = slice(slot * TT, (slot + 1) * TT)
    xk = 'xT%d' % m
    OP('dve', 'scalar_tensor_tensor', R=[pk, xk, 'ada'], W=[xk], out=xT[:, m, sl], in0=ps[:, :], scalar=ada[:, l, gj * 8 + m:gj * 8 + m + 1],
       in1=xT[:, m, sl], op0=ALU.mult, op1=ALU.add)
    OP('act', 'activation', R=[xk], W=['zb%d' % m], out=zb[:, m, :], in_=xT[:, m, sl], func=AF.Copy)
    OP('act', 'activation', R=[xk], W=['zq%d' % m], out=zq[:, m, :], in_=xT[:, m, sl], func=AF.Square)


def _a3_phase(st, l):
    kb, OP, nc, inp = st['kb'], st['OP'], st['nc'], st['inp']
    park = st['park']
    wsc_g, wsc_br, wsc_o = st['wsc_g'], st['wsc_br'], st['wsc_o']
    HTK = ['hT%d' % k for k in range(8)]
    PK = ('s5', 'mla', 'sgu')
    with contextlib.ExitStack() as ph:
        hT = kb.sbuf("hT3", [128, 8, TT], BF16, ph)
        yb = kb.sbuf("yb", [128, 3, 4, TT], BF16, ph)
        gw = [kb.sbuf("gw%d" % i, [128, 3, 8, 128], BF16, ph) for i in range(2)]
        bw = [kb.sbuf("bw%d" % i, [128, 3, 4, 128], BF16, ph) for i in range(2)]
        ow = [kb.sbuf("ow%d" % i, [128, 8, 128], BF16, ph) for i in range(2)]
        bg = kb.sbuf("bg", [128, 24], F32, ph)
        gt = kb.sbuf("gt", [128, 3, TT], F32, ph)
        acc = kb.sbuf("acc", [128, TT], F32, ph)
        tmp = kb.sbuf("tmp3", [128, TT], F32, ph)
        mg = kb.sbuf("mg", [128, 8, TT], BF16, ph)
        zb = kb.sbuf("zb", [128, 8, TT], BF16, ph)
        zq = kb.sbuf("zq", [128, 8, TT], BF16, ph)
        mean_sb = kb.sbuf("mean_sb", [128, TT], F32, ph)
        r_sb = kb.sbuf("r_sb", [128, TT], F32, ph)
        tmpf = kb.sbuf("tmpf", [128, 2, TT], F32, ph)
        kb.dma('sp', bg[:], _fm(inp['b_in'][l, 2208:5280]), W=['bg'], allow_slow_non_contiguous=True)
        for slot in range(NSLOT):
            sl = slice(slot * TT, (slot + 1) * TT)
            for b, k_ in enumerate(PK):
                kb.dma('sp', yb[:, b], park[k_][:, :, sl], W=['yb%d' % b])
            _make_hT(st, l, slot, hT, 0, 1)
            for m in range(8):
                i = m % 2
                for b in range(3):
                    kb.dma('sp', gw[i][:, b], wsc_g[l][b][m], R=['wsc_g%d_%d' % (l, b)], W=['gw%d_%d' % (i, b)])
                    kb.dma('sp', bw[i][:, b], wsc_br[l][b][m], R=['wsc_br%d_%d' % (l, b)], W=['bw%d_%d' % (i, b)])
                for b in range(3):
                    psg, pkg = kb.next_ps()
                    kb.mm(psg[:, :], [(gw[i][:, b, k, :], hT[:, k, :]) for k in range(8)], R=HTK + ['gw%d_%d' % (i, b)], W=[pkg])
                    OP('act', 'activation', R=[pkg, 'bg'], W=['gt%d' % b], out=gt[:, b, :], in_=psg[:, :], func=AF.Sigmoid,
                       bias=bg[:, b * 8 + m:b * 8 + m + 1])
                    psp, pkp = kb.next_ps()
                    kb.mm(psp[:, :], [(bw[i][:, b, k, :], yb[:, b, k, :]) for k in range(4)], R=['yb%d' % b, 'bw%d_%d' % (i, b)], W=[pkp])
                    if b == 0:
                        OP('dve', 'tensor_tensor', R=[pkp, 'gt0'], W=['acc'], out=acc[:], in0=psp[:, :], in1=gt[:, 0, :], op=ALU.mult)
                    else:
                        OP('dve', 'tensor_tensor', R=[pkp, 'gt%d' % b], W=['tmp'], out=tmp[:], in0=psp[:, :], in1=gt[:, b, :], op=ALU.mult)
                        if b == 1:
                            OP('pool', 'tensor_tensor', R=['acc', 'tmp'], W=['acc'], out=acc[:], in0=acc[:], in1=tmp[:], op=ALU.add)
                        else:
                            OP('pool', 'tensor_tensor', R=['acc', 'tmp'], W=['mg%d' % m], out=mg[:, m, :], in0=acc[:], in1=tmp[:], op=ALU.add)
            MG = ['mg%d' % m for m in range(8)]
            for m in range(8):
                i = m % 2
                kb.dma('sp', ow[i][:], wsc_o[l][m], R=['wsc_o%d' % l], W=['ow%d' % i])
                ps, pk = kb.next_ps()
                kb.mm(ps[:, :], [(ow[i][:, k, :], mg[:, k, :]) for k in range(8)], R=MG + ['ow%d' % i], W=[pk])
                _resid_ln(st, l, slot, m, ps, pk, 2, zb, zq)
            _ln_slot(st, l, slot, 2, 0, None, (zb, zq, mean_sb, r_sb, tmpf))
        kb.barrier()


def _ffn_phase(st, l):
    kb, OP, nc, inp = st['kb'], st['OP'], st['nc'], st['inp']
    wsc_fi, wsc_fo = st['wsc_fi'], st['wsc_fo']
    HTK = ['hT%d' % k for k in range(8)]
    with contextlib.ExitStack() as ph:
        hT = kb.sbuf("hT4", [128, 8, TT], BF16, ph)
        fw = [kb.sbuf("fw%d" % i, [128, 2, 8, 128], BF16, ph) for i in range(2)]
        fo = [kb.sbuf("fo%d" % i, [128, 22, 128], BF16, ph) for i in range(2)]
        sa = [kb.sbuf("sa%d" % i, [128, TT], F32, ph) for i in range(2)]
        gT = kb.sbuf("gT", [128, 22, TT], BF16, ph)
        zb = kb.sbuf("zb4", [128, 8, TT], BF16, ph)
        zq = kb.sbuf("zq4", [128, 8, TT], BF16, ph)
        mean_sb = kb.sbuf("mean_sb4", [128, TT], F32, ph)
        r_sb = kb.sbuf("r_sb4", [128, TT], F32, ph)
        tmpf = kb.sbuf("tmpf4", [128, 2, TT], F32, ph)
        for slot in range(NSLOT):
            _make_hT(st, l, slot, hT, 3, 4)
            for j in range(22):
                i = j % 2
                for h in range(2):
                    kb.dma('sp', fw[i][:, h], wsc_fi[l][h][j], R=['wsc_fi%d_%d' % (l, h)], W=['fw%d_%d' % (i, h)])
                psa, pka = kb.next_ps()
                kb.mm(psa[:, :], [(fw[i][:, 0, k, :], hT[:, k, :]) for k in range(8)], R=HTK + ['fw%d_0' % i], W=[pka])
                psb, pkb = kb.next_ps()
                kb.mm(psb[:, :], [(fw[i][:, 1, k, :], hT[:, k, :]) for k in range(8)], R=HTK + ['fw%d_1' % i], W=[pkb])
                OP('act', 'activation', R=[pka], W=['sa%d' % i], out=sa[i][:], in_=psa[:, :], func=AF.Silu)
                OP('dve', 'tensor_tensor', R=[pkb, 'sa%d' % i], W=['gT%d' % j], out=gT[:, j, :], in0=psb[:, :], in1=sa[i][:], op=ALU.mult)
            GT = ['gT%d' % j for j in range(22)]
            for m in range(8):
                i = m % 2
                kb.dma('sp', fo[i][:], wsc_fo[l][m], R=['wsc_fo%d' % l], W=['fo%d' % i])
                ps, pk = kb.next_ps()
                kb.mm(ps[:, :], [(fo[i][:, j, :], gT[:, j, :]) for j in range(22)], R=GT + ['fo%d' % i], W=[pk])
                _resid_ln(st, l, slot, m, ps, pk, 5, zb, zq)
            _ln_slot(st, l, slot, 5, 2, None, (zb, zq, mean_sb, r_sb, tmpf))
        kb.barrier()
```

```python
import contextlib
import math
import os

import ml_dtypes
import numpy as np

import concourse.bass as bass
import concourse.mybir as mybir
from concourse.bass_utils import run_bass_kernel_spmd

F32 = mybir.dt.float32
BF16 = mybir.dt.bfloat16
AF = mybir.ActivationFunctionType
ALU = mybir.AluOpType
AX = mybir.AxisListType

D = 1024
S = 4096
NB = 4
DEPTH = 4
TT = 512
NSLOT = 4
NTOK = NSLOT * TT
INW = 5280
FF = 2816
ALPHA = (2 * DEPTH) ** 0.25
LN_EPS = 1e-5
RMS_EPS = 1e-6
TILES = {0: [0, 3, 4, 7], 1: [1, 2, 5, 6]}
NEG = -30000.0
PAIRS = [[0, 1], [2, 3], [4, 5], [6, 7]]


class KB:
    NDMA = 28

    def __init__(self, nc, es):
        self.nc = nc
        self.es = es
        self.eng = {'pe': nc.tensor, 'act': nc.scalar, 'dve': nc.vector, 'pool': nc.gpsimd, 'sp': nc.sync}
        self.sem = {}
        for n in ['pe', 'act', 'dve', 'pool', 'cc']:
            self.sem[n] = es.enter_context(nc.semaphore('s_' + n))
        for i in range(self.NDMA):
            self.sem['d%d' % i] = es.enter_context(nc.semaphore('s_d%d' % i))
        self.cnt = {k: 0 for k in self.sem}
        self.known = {e: {} for e in self.eng}
        self.res = {}
        self.dma_rr = 0
        self.ninst = 0
        self.ps_rr = 0
        self.psb = []

    def sbuf(self, name, shape, dt, es=None):
        self.uid = getattr(self, 'uid', 0) + 1
        return (es or self.es).enter_context(self.nc.sbuf_tensor('sb%d_%s' % (self.uid, name), list(shape), dt))

    def psum(self, name, shape, dt):
        return self.es.enter_context(self.nc.psum_tensor(name, list(shape), dt))

    def next_ps(self):
        t = self.psb[self.ps_rr % len(self.psb)]
        self.ps_rr += 1
        return t

    def _wait(self, e, s, v):
        if v <= 0 or (e == 'pe' and s == 'pe'):
            return
        kn = self.known[e]
        if kn.get(s, 0) >= v:
            return
        self.eng[e].wait_ge(self.sem[s], v)
        kn[s] = v
        self.ninst += 1

    @staticmethod
    def _psfix(R, W):
        R = list(R)
        W = list(W)
        W += [k for k in R if k.startswith('ps')]
        R = [k for k in R if not k.startswith('ps')]
        return R, W

    def _deps(self, e, reads, writes):
        toks = {}

        def add(s, v):
            if toks.get(s, 0) < v:
                toks[s] = v
        for r in reads:
            st = self.res.get(r)
            if st is not None and st[0] is not None:
                add(*st[0])
        for w in writes:
            st = self.res.get(w)
            if st is not None:
                if st[0] is not None:
                    add(*st[0])
                for s, v in st[1].items():
                    add(s, v)
        for s, v in toks.items():
            self._wait(e, s, v)

    def _commit(self, tok, reads, writes):
        s, v = tok
        for r in reads:
            st = self.res.get(r)
            if st is None:
                st = [None, {}]
                self.res[r] = st
            if st[1].get(s, 0) < v:
                st[1][s] = v
        for w in writes:
            self.res[w] = [tok, {}]

    def op(self, e, meth, R=(), W=(), **kw):
        R, W = self._psfix(R, W)
        self._deps(e, R, W)
        inst = getattr(self.eng[e], meth)(**kw)
        self.cnt[e] += 1
        inst.then_inc(self.sem[e], 1)
        self._commit((e, self.cnt[e]), R, W)
        self.ninst += 1

    def mm(self, out, pairs, R=(), W=()):
        self._deps('pe', R, W)
        n = len(pairs)
        inst = None
        for i, (lt, rh) in enumerate(pairs):
            inst = self.nc.tensor.matmul(out, lhsT=lt, rhs=rh, start=(i == 0), stop=(i == n - 1))
            self.ninst += 1
        self.cnt['pe'] += 1
        inst.then_inc(self.sem['pe'], 1)
        self._commit(('pe', self.cnt['pe']), R, W)

    def dma(self, q, out, in_, R=(), W=(), **kw):
        s = 'd%d' % self.dma_rr
        self.dma_rr = (self.dma_rr + 1) % self.NDMA
        self._wait(q, s, self.cnt[s])
        self._deps(q, R, W)
        inst = self.eng[q].dma_start(out=out, in_=in_, **kw)
        self.cnt[s] += 16
        inst.then_inc(self.sem[s], 16)
        self._commit((s, self.cnt[s]), R, W)
        self.ninst += 1

    def bg_dma(self, q, group, out, in_, **kw):
        s = 'bg_' + group
        if s not in self.sem:
            self.sem[s] = self.es.enter_context(self.nc.semaphore('s_' + s))
            self.cnt[s] = 0
        inst = self.eng[q].dma_start(out=out, in_=in_, **kw)
        self.cnt[s] += 16
        inst.then_inc(self.sem[s], 16)
        self.res[group] = [(s, self.cnt[s]), {}]
        self.ninst += 1

    def mmg(self, items, R=(), W=()):
        self._deps('pe', R, W)
        inst = None
        for (out, lt, rh, st, sp) in items:
            inst = self.nc.tensor.matmul(out, lhsT=lt, rhs=rh, start=st, stop=sp, skip_group_check=True)
            self.ninst += 1
        self.cnt['pe'] += 1
        inst.then_inc(self.sem['pe'], 1)
        self._commit(('pe', self.cnt['pe']), R, W)

    def allgather(self, src_t, dst_t, R=(), W=()):
        self._deps('pool', R, W)
        inst = self.nc.gpsimd.collective_compute("AllGather", ALU.bypass, replica_groups=PAIRS,
                                                 ins=[src_t.ap().opt()], outs=[dst_t.ap().opt()])
        self.cnt['cc'] += 1
        inst.then_inc(self.sem['cc'], 1)
        self._commit(('cc', self.cnt['cc']), R, W)
        self.ninst += 1

    def barrier(self):
        for e in self.eng:
            for s, v in self.cnt.items():
                if not s.startswith('bg_'):
                    self._wait(e, s, v)
        self.res = {k: [st[0], {}] for k, st in self.res.items()
                    if st[0] is not None and st[0][0].startswith('bg_')}

    def finish(self):
        for s, v in self.cnt.items():
            self._wait('sp', s, v)


def _fm(ap1d):
    return ap1d.rearrange("(t p) -> p t", p=128)


def build_program(n_layers=DEPTH, dbg=()):
    nc = bass.Bass("TRN2", target_bir_lowering=False)
    inp = {}

    def I(name, shape, dt=F32):
        inp[name] = nc.dram_tensor(name, list(shape), dt, kind="ExternalInput").ap()
        return inp[name]
    xT_in = I("xT", [D, NTOK])
    cvec = I("cvec", [128, 8])
    ropec = I("ropec", [32, NTOK])
    ropes = I("ropes", [32, NTOK])
    qmask_in = I("qmask", [8, NSLOT * 2 * TT], BF16)
    e8_in = I("e8", [8, TT], BF16)
    egrid_in = I("egrid", [128, 72])
    selv_in = I("selv", [128, 8])
    w_ada = I("w_ada", [n_layers, D, 6 * D])
    b_ada = I("b_ada", [n_layers, 6 * D])
    w_in = I("w_in", [n_layers, D, INW])
    b_in = I("b_in", [n_layers, INW])
    lam_re = I("s5_lambda_re", [n_layers, 32, 64])
    lam_im = I("s5_lambda_im", [n_layers, 32, 64])
    log_dt = I("s5_log_dt", [n_layers, 32])
    s5_b_re = I("s5_b_re", [n_layers, 32, 64, 16])
    s5_b_im = I("s5_b_im", [n_layers, 32, 64, 16])
    s5_c_re = I("s5_c_re", [n_layers, 32, 16, 64])
    s5_c_im = I("s5_c_im", [n_layers, 32, 16, 64])
    s5_d = I("s5_d", [n_layers, 512])
    s5_w_glu = I("s5_w_glu", [n_layers, 512, 512])
    s5_b_glu = I("s5_b_glu", [n_layers, 512])
    q_norm = I("mla_q_norm", [n_layers, 384])
    w_q_up = I("mla_w_q_up", [n_layers, 384, 768])
    kv_norm = I("mla_kv_norm", [n_layers, 256])
    w_kv_up = I("mla_w_kv_up", [n_layers, 256, 1024])
    sgu_ln_g = I("sgu_ln_g", [n_layers, 512])
    sgu_ln_b = I("sgu_ln_b", [n_layers, 512])
    sgu_w_s = I("sgu_w_s", [n_layers, 4, 128, 128])
    sgu_b_s = I("sgu_b_s", [n_layers, 4, 128])
    w_branch = I("w_branch", [n_layers, 3, 512, D])
    w_out = I("w_out", [n_layers, D, D])
    ln1_g = I("ln1_g", [n_layers, D])
    ln1_b = I("ln1_b", [n_layers, D])
    ffn_w_in = I("ffn_w_in", [n_layers, D, 2 * FF])
    ffn_w_out = I("ffn_w_out", [n_layers, FF, D])
    ln2_g = I("ln2_g", [n_layers, D])
    ln2_b = I("ln2_b", [n_layers, D])
    outT = nc.dram_tensor("outT", [D, NTOK], F32, kind="ExternalOutput").ap()
    dbg_out = {}

    def DBG(name, shape, dt=F32):
        dbg_out[name] = nc.dram_tensor("dbg_" + name, list(shape), dt, kind="ExternalOutput").ap()
        return dbg_out[name]

    def SCR(name, shape, dt=BF16):
        return nc.dram_tensor(name, list(shape), dt)
    wsc_g = [[SCR("wsc_g%d_%d" % (l, b), [8, 128, 8, 128]).ap() for b in range(3)] for l in range(n_layers)]
    wsc_br = [[SCR("wsc_br%d_%d" % (l, b), [8, 128, 4, 128]).ap() for b in range(3)] for l in range(n_layers)]
    wsc_o = [SCR("wsc_o%d" % l, [8, 128, 8, 128]).ap() for l in range(n_layers)]
    wsc_fi = [[SCR("wsc_fi%d_%d" % (l, h), [22, 128, 8, 128]).ap() for h in range(2)] for l in range(n_layers)]
    wsc_fo = [SCR("wsc_fo%d" % l, [8, 128, 22, 128]).ap() for l in range(n_layers)]
    park = {k: SCR("park_" + k, [128, 4, NTOK]).ap() for k in ("s5", "mla", "sgu")}
    cc1_src = SCR("cc1_src", [288, NTOK])
    cc1_dst = SCR("cc1_dst", [576, NTOK])
    cc2_src = SCR("cc2_src", [256, 64 * 32], F32)
    cc2_dst = SCR("cc2_dst", [512, 64 * 32], F32)

    with contextlib.ExitStack() as es:
        kb = KB(nc, es)
        kb.psb = [(kb.psum("psA%d" % i, [128, 512], F32), "psA%d" % i) for i in range(6)]
        psX = [(kb.psum("psX%d" % i, [128, 512], F32), "psX%d" % i) for i in range(2)]

        xT = kb.sbuf("xTres", [128, 8, NTOK], F32)
        ident_bf = kb.sbuf("ident_bf", [128, 128], BF16)
        ident_f = kb.sbuf("ident_f", [128, 128], F32)
        ones_bf = kb.sbuf("ones_bf", [128, 128], BF16)
        ones_f = kb.sbuf("ones_f", [128, 128], F32)
        mask01 = kb.sbuf("mask01", [128, 128], F32)
        jmat = kb.sbuf("jmat", [128, 128], F32)
        e8 = kb.sbuf("e8", [8, TT], BF16)
        qmask = kb.sbuf("qmask", [8, NSLOT * 2 * TT], BF16)
        cact = kb.sbuf("cact", [128, 8], BF16)
        ada = kb.sbuf("ada", [128, n_layers, 48], F32)
        lnp = kb.sbuf("lnp", [128, n_layers, 4, 8], F32)

        def OP(e, meth, R=(), W=(), **kw):
            kb.op(e, meth, R, W, **kw)

        OP('pool', 'memset', W=['ident_f'], ap=ident_f[:], constant=1.0)
        OP('pool', 'affine_select', R=['ident_f'], W=['ident_f'], out=ident_f[:], in_=ident_f[:], pattern=[[-1, 128]],
           compare_op=ALU.is_equal, fill=0.0, base=0, channel_multiplier=1)
        OP('pool', 'tensor_copy', R=['ident_f'], W=['ident_bf'], out=ident_bf[:], in_=ident_f[:])
        OP('pool', 'memset', W=['ones_bf'], ap=ones_bf[:], constant=1.0)
        OP('pool', 'memset', W=['ones_f'], ap=ones_f[:], constant=1.0)
        OP('pool', 'memset', W=['mask01'], ap=mask01[:], constant=1.0)
        OP('pool', 'affine_select', R=['mask01'], W=['mask01'], out=mask01[:].rearrange("p (a b) -> p a b", b=16),
           in_=mask01[:].rearrange("p (a b) -> p a b", b=16), pattern=[[16, 8], [0, 16]],
           compare_op=ALU.is_ge, fill=0.0, base=-112, channel_multiplier=1)
        OP('pool', 'memset', W=['jmat'], ap=jmat[:], constant=1.0)
        OP('pool', 'affine_select', R=['jmat'], W=['jmat'], out=jmat[:].rearrange("p (a b) -> p a b", b=16),
           in_=jmat[:].rearrange("p (a b) -> p a b", b=16), pattern=[[16, 8], [-1, 16]],
           compare_op=ALU.is_equal, fill=0.0, base=-112, channel_multiplier=1)
        kb.dma('sp', e8[:], e8_in[:, :], W=['e8'])
        kb.dma('sp', qmask[:], qmask_in[:, :], W=['qmask'])
        for k in range(8):
            kb.dma('sp', xT[:, k, :], xT_in[k * 128:(k + 1) * 128, :], W=['xT%d' % k])
        XK = ['xT%d' % k for k in range(8)]

        def cast_mt(dst, src2d, KT, NM, key):
            for m in range(NM):
                kb.bg_dma('pool', key, dst[m], src2d[:, m * 128:(m + 1) * 128].rearrange("(kt p) c -> p kt c", p=128))

        def cast_layer(l):
            for b in range(3):
                cast_mt(wsc_g[l][b], w_in[l][:, 2208 + b * 1024: 2208 + (b + 1) * 1024], 8, 8, 'wsc_g%d_%d' % (l, b))
                cast_mt(wsc_br[l][b], w_branch[l][b], 4, 8, 'wsc_br%d_%d' % (l, b))
            cast_mt(wsc_o[l], w_out[l], 8, 8, 'wsc_o%d' % l)
            for h in range(2):
                cast_mt(wsc_fi[l][h], ffn_w_in[l][:, h * FF:(h + 1) * FF], 8, 22, 'wsc_fi%d_%d' % (l, h))
            cast_mt(wsc_fo[l], ffn_w_out[l], 22, 8, 'wsc_fo%d' % l)

        with contextlib.ExitStack() as ph:
            cv = kb.sbuf("cv", [128, 8], F32, ph)
            wada = [kb.sbuf("wada%d" % i, [128, 8, 512], BF16, ph) for i in range(2)]
            arow = kb.sbuf("arow", [1, 6 * D], F32, ph)
            brow = kb.sbuf("brow", [1, 6 * D], F32, ph)
            kb.dma('sp', cv[:], cvec[:, :], W=['cv'])
            OP('act', 'activation', R=['cv'], W=['cact'], out=cact[:], in_=cv[:], func=AF.Silu)
            for l in range(n_layers):
                kb.dma('sp', brow[:], b_ada[l:l + 1, :], W=['brow'])
                for cb in range(12):
                    wb = wada[cb % 2]
                    wk = 'wada%d' % (cb % 2)
                    kb.dma('pool', wb[:], w_ada[l][:, cb * 512:(cb + 1) * 512].rearrange("(kt p) c -> p kt c", p=128),
                           W=[wk])
                    ps, pk = kb.next_ps()
                    kb.mm(ps[0:1, :], [(cact[:, kt:kt + 1], wb[:, kt, :]) for kt in range(8)], R=['cact', wk], W=[pk])
                    OP('dve', 'tensor_tensor', R=[pk, 'brow'], W=['arow'], out=arow[:, cb * 512:(cb + 1) * 512],
                       in0=ps[0:1, :], in1=brow[:, cb * 512:(cb + 1) * 512], op=ALU.add)
                for j in (1, 4):
                    OP('dve', 'tensor_scalar_add', R=['arow'], W=['arow'], out=arow[:, j * D:(j + 1) * D],
                       in0=arow[:, j * D:(j + 1) * D], scalar1=1.0)
                for j in (2, 5):
                    OP('dve', 'tensor_scalar', R=['arow'], W=['arow'], out=arow[:, j * D:(j + 1) * D],
                       in0=arow[:, j * D:(j + 1) * D], scalar1=1.0, scalar2=1.0 / ALPHA, op0=ALU.add, op1=ALU.mult)
                ps, pk = kb.next_ps()
                for t in range(48):
                    kb.mm(ps[:, t:t + 1], [(arow[0:1, t * 128:(t + 1) * 128], ones_f[0:1, 0:1])],
                          R=['arow', 'ones_f'], W=[pk])
                OP('dve', 'tensor_copy', R=[pk], W=['ada'], out=ada[:, l, :], in_=ps[:, 0:48])
                for j, src in enumerate((ln1_g, ln1_b, ln2_g, ln2_b)):
                    kb.dma('sp', lnp[:, l, j, :], _fm(src[l]), W=['lnp'], allow_slow_non_contiguous=True)
            kb.barrier()

        state = dict(nc=nc, kb=kb, OP=OP, inp=inp, xT=xT, XK=XK, ada=ada, lnp=lnp, psX=psX,
                     ident_bf=ident_bf, ident_f=ident_f, ones_bf=ones_bf, ones_f=ones_f, mask01=mask01, jmat=jmat,
                     e8=e8, qmask=qmask, park=park, wsc_g=wsc_g, wsc_br=wsc_br, wsc_o=wsc_o, wsc_fi=wsc_fi,
                     wsc_fo=wsc_fo, cc1_src=cc1_src, cc1_dst=cc1_dst, cc2_src=cc2_src, cc2_dst=cc2_dst,
                     ropec=ropec, ropes=ropes, dbg=dbg, DBG=DBG, egrid_in=egrid_in, selv_in=selv_in)
        state['cast_layer'] = cast_layer
        state['n_layers'] = n_layers
        for l in range(n_layers):
            layer(state, l)

        for k in range(8):
            kb.dma('sp', outT[k * 128:(k + 1) * 128, :], xT[:, k, :], R=['xT%d' % k], W=['out%d' % k])
        kb.finish()
        print("instructions:", kb.ninst)
    return nc, list(dbg_out.keys())


def _ln_slot(st, l, slot, gi, pj, zk, ph_bufs):
    kb, OP, xT, lnp = st['kb'], st['OP'], st['xT'], st['lnp']
    ones_bf = st['ones_bf']
    zb, zq, mean_sb, r_sb, tmpf = ph_bufs
    sl = slice(slot * TT, (slot + 1) * TT)
    ps_s, pks = kb.next_ps()
    kb.mm(ps_s[:, :], [(ones_bf[:, :], zb[:, m, :]) for m in range(8)], R=['zb%d' % m for m in range(8)] + ['ones_bf'], W=[pks])
    ps_q, pkq = kb.next_ps()
    kb.mm(ps_q[:, :], [(ones_bf[:, :], zq[:, m, :]) for m in range(8)], R=['zq%d' % m for m in range(8)] + ['ones_bf'], W=[pkq])
    OP('act', 'activation', R=[pks], W=['mean_sb'], out=mean_sb[:], in_=ps_s[:, :], func=AF.Copy, scale=1.0 / D)
    OP('dve', 'tensor_tensor', R=['mean_sb'], W=['r_sb'], out=r_sb[:], in0=mean_sb[:], in1=mean_sb[:], op=ALU.mult)
    OP('dve', 'scalar_tensor_tensor', R=[pkq, 'r_sb'], W=['r_sb'], out=r_sb[:], in0=ps_q[:, :], scalar=1.0 / D,
       in1=r_sb[:], op0=ALU.mult, op1=ALU.subtract)
    OP('dve', 'tensor_scalar', R=['r_sb'], W=['r_sb'], out=r_sb[:], in0=r_sb[:], scalar1=0.0,
       scalar2=LN_EPS / (ALPHA * ALPHA), op0=ALU.max, op1=ALU.add)
    OP('act', 'activation', R=['r_sb'], W=['r_sb'], out=r_sb[:], in_=r_sb[:], func=AF.Sqrt)
    OP('dve', 'reciprocal', R=['r_sb'], W=['r_sb'], out=r_sb[:], in_=r_sb[:])
    for m in range(8):
        xk = 'xT%d' % m
        e1 = 'dve' if m % 2 == 0 else 'pool'
        OP(e1, 'tensor_tensor', R=[xk, 'mean_sb'], W=['tmpf%d' % (m % 2)], out=tmpf[:, m % 2, :], in0=xT[:, m, sl],
           in1=mean_sb[:], op=ALU.subtract)
        OP(e1, 'tensor_tensor', R=['tmpf%d' % (m % 2), 'r_sb'], W=['tmpf%d' % (m % 2)], out=tmpf[:, m % 2, :],
           in0=tmpf[:, m % 2, :], in1=r_sb[:], op=ALU.mult)
        OP('act', 'activation', R=['tmpf%d' % (m % 2), 'lnp'], W=[xk], out=xT[:, m, sl], in_=tmpf[:, m % 2, :],
           func=AF.Identity, scale=lnp[:, l, pj, m:m + 1], bias=lnp[:, l, pj + 1, m:m + 1])


def _make_hT(st, l, slot, hT, j_shift, j_scale, pref='hT'):
    kb, OP, xT, ada = st['kb'], st['OP'], st['xT'], st['ada']
    sl = slice(slot * TT, (slot + 1) * TT)
    for k in range(8):
        OP('act', 'activation', R=['xT%d' % k, 'ada'], W=[pref + '%d' % k], out=hT[:, k, :], in_=xT[:, k, sl], func=AF.Identity,
           scale=ada[:, l, j_scale * 8 + k:j_scale * 8 + k + 1], bias=ada[:, l, j_shift * 8 + k:j_shift * 8 + k + 1])


def layer(st, l):
    kb, OP, nc, inp = st['kb'], st['OP'], st['nc'], st['inp']
    xT, ada, lnp = st['xT'], st['ada'], st['lnp']
    ident_bf, ident_f, ones_bf, ones_f, mask01 = st['ident_bf'], st['ident_f'], st['ones_bf'], st['ones_f'], st['mask01']
    psX = st['psX']
    HTK = ['hT%d' % k for k in range(8)]
    park = st['park']
    dbg, DBG = st['dbg'], st['DBG']
    w_in, b_in = inp['w_in'], inp['b_in']
    cc1_src, cc1_dst, cc2_src, cc2_dst = st['cc1_src'], st['cc1_dst'], st['cc2_src'], st['cc2_dst']

    def evac(i, ps_ap, out_ap, R, W, bias=None, func=None):
        if func is not None or (bias is not None and i % 2 == 0):
            kw = dict(out=out_ap, in_=ps_ap, func=func or AF.Identity)
            if bias is not None:
                kw['bias'] = bias
            OP('act', 'activation', R=R, W=W, **kw)
        elif bias is not None:
            OP('dve', 'tensor_scalar_add', R=R, W=W, out=out_ap, in0=ps_ap, scalar1=bias)
        elif i % 2 == 0:
            OP('act', 'activation', R=R, W=W, out=out_ap, in_=ps_ap, func=AF.Copy)
        else:
            OP('dve', 'tensor_copy', R=R, W=W, out=out_ap, in_=ps_ap)

    with contextlib.ExitStack() as mix:
        uT = kb.sbuf("uT", [128, 32, 4, 64], BF16, mix)
        xlo = kb.sbuf("xlo", [128, 32, 128], BF16, mix)

        with contextlib.ExitStack() as ph:
            wA = kb.sbuf("wA", [128, 8, 1024], BF16, ph)
            hTs = [kb.sbuf("hT_%d" % i, [128, 8, TT], BF16, ph) for i in range(2)]
            Xs5 = kb.sbuf("Xs5", [64, 32, 8, 16], BF16, ph)
            brow_f = kb.sbuf("brow_f", [1, 1024], F32, ph)
            brow = kb.sbuf("brow", [1, 1024], BF16, ph)
            bfm = kb.sbuf("bfm", [128, 8], F32, ph)
            ckv_f = kb.sbuf("ckv_f", [128, 2, TT], F32, ph)
            sq = kb.sbuf("sq", [128, 2, TT], BF16, ph)
            rstd = kb.sbuf("rstd", [128, TT], F32, ph)
            ckvn = kb.sbuf("ckvn", [128, 2, TT], BF16, ph)
            cs = kb.sbuf("cs", [32, 2, TT], F32, ph)
            kt1 = kb.sbuf("kt1", [32, 2, TT], F32, ph)
            kpe = kb.sbuf("kpe", [32, TT], BF16, ph)
            gu = kb.sbuf("gu", [128, 4, TT], BF16, ph)
            gvs = [kb.sbuf("gv%d" % i, [128, TT], F32, ph) for i in range(2)]
            junk = kb.sbuf("junk", [128, TT], BF16, ph)
            stats = [kb.sbuf("stat%d" % i, [128, 8], F32, ph) for i in range(2)]
            vns = [kb.sbuf("vn%d" % i, [128, TT], F32, ph) for i in range(2)]
            vnbs = [kb.sbuf("vnb%d" % i, [128, TT], BF16, ph) for i in range(2)]
            ysg = kb.sbuf("ysg", [128, 4, TT], BF16, ph)
            lng = kb.sbuf("lng", [128, 2, 512], F32, ph)
            ws_nat = kb.sbuf("ws_nat", [128, 4, 128], F32, ph)
            wsT = kb.sbuf("wsT", [128, 4, 128], BF16, ph)
            bs_f = kb.sbuf("bs_f", [1, 3, 512], F32, ph)
            bs_hl = kb.sbuf("bs_hl", [1, 2, 512], BF16, ph)

            def wsrc(c0, c1):
                return w_in[l][:, c0:c1].rearrange("(kt p) c -> p kt c", p=128)
            kb.dma('pool', wA[:, :, 0:512], wsrc(0, 512), W=['wA'])
            kb.dma('pool', wA[:, :, 512:800], wsrc(896, 1184), R=[], W=['wA2'])
            OP('dve', 'tensor_scalar_mul', R=['wA2'], W=['wA3'], out=wA[:, :, 800:816], in0=wA[:, :, 784:800], scalar1=-1.0)
            OP('dve', 'tensor_copy', R=['wA2'], W=['wA4'], out=wA[:, :, 816:832], in_=wA[:, :, 768:784])
            WA1 = ['wA', 'wA2', 'wA3', 'wA4']
            kb.dma('sp', brow_f[:, 0:512], b_in[l:l + 1, 0:512], W=['brow_f'])
            kb.dma('sp', brow_f[:, 512:1024], b_in[l:l + 1, 1696:2208], W=['brow_f2'])
            OP('dve', 'tensor_copy', R=['brow_f', 'brow_f2'], W=['brow'], out=brow[:], in_=brow_f[:])
            kb.dma('sp', bfm[:, 0:2], _fm(b_in[l, 896:1152]), W=['bfm0'], allow_slow_non_contiguous=True)
            kb.dma('sp', bfm[:, 2:6], _fm(b_in[l, 1184:1696]), W=['bfm1'], allow_slow_non_contiguous=True)
            kb.dma('sp', bfm[0:32, 6:7], b_in[l, 1152:1184].rearrange("(p o) -> p o", o=1), W=['bfm2'])
            kb.dma('sp', bfm[0:16, 7:8], b_in[l, 1168:1184].rearrange("(p o) -> p o", o=1), W=['bfm3'])
            kb.dma('sp', bfm[16:32, 7:8], b_in[l, 1152:1168].rearrange("(p o) -> p o", o=1), W=['bfm4'])
            OP('dve', 'tensor_scalar_mul', R=['bfm3'], W=['bfm3'], out=bfm[0:16, 7:8], in0=bfm[0:16, 7:8], scalar1=-1.0)
            BF = ['bfm0', 'bfm1', 'bfm2', 'bfm3', 'bfm4']

            _make_hT(st, l, 0, hTs[0], 0, 1, 'hTa')
            for slot in range(NSLOT):
                sl = slice(slot * TT, (slot + 1) * TT)
                hT = hTs[slot % 2]
                HTK = [('hTa' if slot % 2 == 0 else 'hTb') + '%d' % k for k in range(8)]
                kb.dma('sp', cs[:, 0, :], st['ropec'][:, sl], W=['cs0'])
                kb.dma('sp', cs[:, 1, :], st['ropes'][:, sl], W=['cs1'])
                for s_lo in range(8):
                    ps, pk = kb.next_ps()
                    kb.mm(ps[0:64, :], [(hT[:, k, s_lo:TT:8], wA[:, k, 0:512]) for k in range(8)]
                          + [(ones_bf[0:1, 0:64], brow[0:1, 0:512])], R=HTK + ['brow', 'ones_bf'] + WA1, W=[pk])
                    evac(s_lo, ps[0:64, :].rearrange("p (g c) -> p g c", c=16), Xs5[:, :, 7 - s_lo, :], [pk], ['Xs5_%d' % s_lo])
                if slot + 1 < NSLOT:
                    _make_hT(st, l, slot + 1, hTs[(slot + 1) % 2], 0, 1, 'hTb' if slot % 2 == 0 else 'hTa')
                for g8 in range(4):
                    ps, pk = kb.next_ps()
                    for gi in range(8):
                        g = g8 * 8 + gi
                        kb.mm(ps[:, gi * 64:(gi + 1) * 64],
                              [(Xs5[:, g, :, :].rearrange("p a b -> p (a b)"), ident_bf[0:64, 0:64])],
                              R=['Xs5_%d' % i for i in range(8)] + ['ident_bf'], W=[pk])
                    evac(g8, ps[:, :].rearrange("p (g nl th) -> p g nl th", g=8, nl=16),
                         uT[:, g8 * 8:(g8 + 1) * 8, :, slot * 16:(slot + 1) * 16].rearrange("p g th nl -> p g nl th"),
                         [pk], ['uT'])
                for m in range(2):
                    ps, pk = kb.next_ps()
                    kb.mm(ps[:, :], [(wA[:, k, 512 + m * 128:512 + (m + 1) * 128], hT[:, k, :]) for k in range(8)],
                          R=HTK + WA1, W=[pk])
                    OP('act', 'activation', R=[pk, 'bfm0'], W=['ckv_f'], out=ckv_f[:, m, :], in_=ps[:, :], func=AF.Identity,
                       bias=bfm[:, m:m + 1])
                    OP('act', 'activation', R=[pk, 'bfm0'], W=['sq'], out=sq[:, m, :], in_=ps[:, :], func=AF.Square,
                       bias=bfm[:, m:m + 1])
                ps, pk = kb.next_ps()
                kb.mm(ps[:, :], [(ones_bf[:, :], sq[:, m, :]) for m in range(2)], R=['sq', 'ones_bf'], W=[pk])
                OP('dve', 'tensor_scalar', R=[pk], W=['rstd'], out=rstd[:], in0=ps[:, :], scalar1=1.0 / 256, scalar2=RMS_EPS,
                   op0=ALU.mult, op1=ALU.add)
                OP('act', 'activation', R=['rstd'], W=['rstd'], out=rstd[:], in_=rstd[:], func=AF.Sqrt)
                OP('dve', 'reciprocal', R=['rstd'], W=['rstd'], out=rstd[:], in_=rstd[:])
                for m in range(2):
                    OP('dve', 'tensor_tensor', R=['ckv_f', 'rstd'], W=['ckvn'], out=ckvn[:, m, :], in0=ckv_f[:, m, :],
                       in1=rstd[:], op=ALU.mult)
                    kb.dma('sp', cc1_src.ap()[m * 128:(m + 1) * 128, sl], ckvn[:, m, :], R=['ckvn'],
                           W=['cc1s_%d_%d' % (slot, m)])
                psa, pka = kb.next_ps()
                kb.mm(psa[0:32, :], [(wA[:, k, 768:800], hT[:, k, :]) for k in range(8)], R=HTK + WA1, W=[pka])
                psb, pkb = kb.next_ps()
                kb.mm(psb[0:32, :], [(wA[:, k, 800:832], hT[:, k, :]) for k in range(8)], R=HTK + WA1, W=[pkb])
                OP('dve', 'scalar_tensor_tensor', R=[pka, 'cs0'] + BF, W=['kt1a'], out=kt1[:, 0, :], in0=psa[0:32, :],
                   scalar=bfm[0:32, 6:7], in1=cs[:, 0, :], op0=ALU.add, op1=ALU.mult)
                OP('dve', 'scalar_tensor_tensor', R=[pkb, 'cs1'] + BF, W=['kt1b'], out=kt1[:, 1, :], in0=psb[0:32, :],
                   scalar=bfm[0:32, 7:8], in1=cs[:, 1, :], op0=ALU.add, op1=ALU.mult)
                OP('dve', 'tensor_tensor', R=['kt1a', 'kt1b'], W=['kpe'], out=kpe[:], in0=kt1[:, 0, :], in1=kt1[:, 1, :],
                   op=ALU.add)
                kb.dma('sp', cc1_src.ap()[256:288, sl], kpe[:], R=['kpe'], W=['cc1s_%d_k' % slot])
            CC1K = ['cc1s_%d_%s' % (s_, m_) for s_ in range(NSLOT) for m_ in ('0', '1', 'k')]

            kb.dma('pool', wA[:, :, 0:1024], wsrc(1184, 2208), W=['wA', 'wA2', 'wA3', 'wA4'])
            kb.dma('sp', lng[:, 0, :], inp['sgu_ln_g'][l:l + 1, :].partition_broadcast(128), W=['lng0'])
            kb.dma('sp', lng[:, 1, :], inp['sgu_ln_b'][l:l + 1, :].partition_broadcast(128), W=['lng1'])
            kb.dma('sp', ws_nat[:], inp['sgu_w_s'][l].rearrange("g i j -> i g j"), W=['ws_nat'])
            OP('dve', 'memset', R=['ws_nat'], W=['ws_nat'], ap=ws_nat[0:64, :, 64:128], constant=0.0)
            ps, pk = kb.next_ps()
            for g in range(4):
                kb.mm(ps[:, g * 128:(g + 1) * 128], [(ws_nat[:, g, :], ident_f[:, :])], R=['ws_nat', 'ident_f'], W=[pk])
            OP('dve', 'tensor_copy', R=[pk], W=['wsT'], out=wsT[:].rearrange("p g i -> p (g i)"), in_=ps[:, :])
            kb.dma('sp', bs_f[:, 0, :], inp['sgu_b_s'][l:l + 1].rearrange("o g i -> o (g i)"), W=['bs_f'])
            OP('dve', 'tensor_copy', R=['bs_f'], W=['bs_hl0'], out=bs_hl[:, 0, :], in_=bs_f[:, 0, :])
            OP('dve', 'tensor_copy', R=['bs_hl0'], W=['bs_f1'], out=bs_f[:, 1, :], in_=bs_hl[:, 0, :])
            OP('dve', 'tensor_tensor', R=['bs_f', 'bs_f1'], W=['bs_f2'], out=bs_f[:, 2, :], in0=bs_f[:, 0, :],
               in1=bs_f[:, 1, :], op=ALU.subtract)
            OP('dve', 'tensor_copy', R=['bs_f2'], W=['bs_hl1'], out=bs_hl[:, 1, :], in_=bs_f[:, 2, :])

            _make_hT(st, l, 0, hTs[0], 0, 1, 'hTa')
            for slot in range(NSLOT):
                sl = slice(slot * TT, (slot + 1) * TT)
                hT = hTs[slot % 2]
                HTK = [('hTa' if slot % 2 == 0 else 'hTb') + '%d' % k for k in range(8)]
                for m in range(4):
                    ps, pk = kb.next_ps()
                    kb.mm(ps[:, :], [(wA[:, k, m * 128:(m + 1) * 128], hT[:, k, :]) for k in range(8)], R=HTK + ['wA'], W=[pk])
                    OP('act', 'activation', R=[pk, 'bfm1'], W=['gu%d' % m], out=gu[:, m, :], in_=ps[:, :], func=AF.Gelu_apprx_tanh,
                       bias=bfm[:, 2 + m:3 + m])
                if slot + 1 < NSLOT:
                    _make_hT(st, l, slot + 1, hTs[(slot + 1) % 2], 0, 1, 'hTb' if slot % 2 == 0 else 'hTa')
                for sub in range(4):
                    ss = slice(sub * 128, (sub + 1) * 128)
                    sp_ = sub % 2
                    gv, stat, vn, vnb = gvs[sp_], stats[sp_], vns[sp_], vnbs[sp_]
                    X = '_%d' % sp_
                    ps, pk = kb.next_ps()
                    kb.mm(ps[:, :], [(hT[:, k, ss], wA[:, k, 512:1024]) for k in range(8)]
                          + [(ones_bf[0:1, 0:128], brow[0:1, 512:1024])], R=HTK + ['wA', 'brow', 'ones_bf'], W=[pk])
                    OP('pool', 'memset', W=['stat0' + X, 'stat1' + X], ap=stat[:, 0:2], constant=0.0)
                    OP('act', 'activation', R=[pk], W=['gv' + X, 'stat0' + X], out=gv[:], in_=ps[:, :], func=AF.Gelu_apprx_tanh,
                       accum_out=stat[:, 0:1])
                    OP('act', 'activation', R=['gv' + X], W=['junk', 'stat1' + X], out=junk[:], in_=gv[:], func=AF.Square,
                       accum_out=stat[:, 1:2])
                    OP('dve', 'tensor_scalar_mul', R=['stat0' + X], W=['stat2' + X], out=stat[:, 2:3], in0=stat[:, 0:1], scalar1=1.0 / 512)
                    OP('dve', 'tensor_tensor', R=['stat2' + X], W=['stat3' + X], out=stat[:, 3:4], in0=stat[:, 2:3], in1=stat[:, 2:3],
                       op=ALU.mult)
                    OP('dve', 'scalar_tensor_tensor', R=['stat1' + X, 'stat3' + X], W=['stat4' + X], out=stat[:, 4:5], in0=stat[:, 1:2],
                       scalar=1.0 / 512, in1=stat[:, 3:4], op0=ALU.mult, op1=ALU.subtract)
                    OP('dve', 'tensor_scalar', R=['stat4' + X], W=['stat4' + X], out=stat[:, 4:5], in0=stat[:, 4:5], scalar1=0.0,
                       scalar2=LN_EPS, op0=ALU.max, op1=ALU.add)
                    OP('act', 'activation', R=['stat4' + X], W=['stat4' + X], out=stat[:, 4:5], in_=stat[:, 4:5], func=AF.Sqrt)
                    OP('dve', 'reciprocal', R=['stat4' + X], W=['stat5' + X], out=stat[:, 5:6], in_=stat[:, 4:5])
                    OP('dve', 'scalar_tensor_tensor', R=['stat2' + X, 'stat5' + X], W=['stat6' + X], out=stat[:, 6:7], in0=stat[:, 2:3],
                       scalar=-1.0, in1=stat[:, 5:6], op0=ALU.mult, op1=ALU.mult)
                    OP('dve', 'tensor_scalar', R=['gv' + X, 'stat5' + X, 'stat6' + X], W=['vn' + X], out=vn[:], in0=gv[:], scalar1=stat[:, 5:6],
                       scalar2=stat[:, 6:7], op0=ALU.mult, op1=ALU.add)
                    OP('pool', 'tensor_tensor', R=['vn' + X, 'lng0'], W=['vn' + X], out=vn[:], in0=vn[:], in1=lng[:, 0, :], op=ALU.mult)
                    OP('pool', 'tensor_tensor', R=['vn' + X, 'lng1'], W=['vnb' + X], out=vnb[:], in0=vn[:], in1=lng[:, 1, :], op=ALU.add)
                    ps2, pk2 = kb.next_ps()
                    for g in range(4):
                        gs = slice(g * 128, (g + 1) * 128)
                        kb.mm(ps2[:, gs], [(vnb[:, gs], wsT[:, g, :]), (ones_bf[0:1, 0:128], bs_hl[0:1, 0, gs]),
                                           (ones_bf[0:1, 0:128], bs_hl[0:1, 1, gs])],
                              R=['vnb' + X, 'wsT', 'bs_hl0', 'bs_hl1', 'ones_bf'], W=[pk2])
                    OP('dve', 'tensor_tensor', R=[pk2] + ['gu%d' % i for i in range(4)], W=['ysg%d' % sub], out=ysg[:, :, ss],
                       in0=ps2[:, :].rearrange("p (g i) -> p g i", g=4), in1=gu[:, :, ss], op=ALU.mult)
                kb.dma('sp', park['sgu'][:, :, sl], ysg[:], R=['ysg%d' % i for i in range(4)], W=['park_sgu%d' % slot])
            if 'a1' in dbg:
                kb.dma('sp', DBG('uT', [128, 32 * 256], BF16)[:, :], uT[:].rearrange("p g t n -> p (g t n)"), R=['uT'], W=['dbg_uT'])
                kb.dma('sp', DBG('cc1', [288, NTOK], BF16)[:, :], cc1_src.ap()[:, :], R=CC1K, W=['dbg_cc1'])
                kb.dma('sp', DBG('sgu', [128, 4 * NTOK], BF16)[:, :], park['sgu'].rearrange("p m t -> p (m t)"),
                       R=['park_sgu%d' % i for i in range(4)], W=['dbg_sgu'])
            kb.barrier()
        if 'a1' in dbg:
            return
        _s5_phase(st, l, uT, xlo, CC1K)
    if any(k.startswith('s5') for k in dbg):
        return
    _mla_phase(st, l)
    if 'mla' in dbg:
        return
    _a3_phase(st, l)
    _ffn_phase(st, l)


def _core_tokens(o):
    return np.concatenate([np.arange(G * TT, (G + 1) * TT) for G in TILES[o]])


def _const_tables(o):
    inv_freq = (1.0 / (np.float32(10000.0) ** (np.arange(0, 32, 2, dtype=np.float32) / np.float32(32)))).astype(np.float32)
    idx = _core_tokens(o)
    ang = (idx.astype(np.float32)[:, None] * inv_freq[None, :]).astype(np.float32)
    cos = np.cos(ang).astype(np.float32)
    sin = np.sin(ang).astype(np.float32)
    ropec = np.ascontiguousarray(np.concatenate([cos, cos], 1).T)
    ropes = np.ascontiguousarray(np.concatenate([sin, sin], 1).T)
    qm = np.zeros((8, NSLOT, 2, TT), np.float32)
    q = np.arange(TT)
    for s in range(NSLOT):
        G = TILES[o][s]
        for k in range(2):
            for c in range(8):
                qm[c, s, k, :] = np.where((2 * s + k) * 8 + c > G * 8 + q // 64, NEG, 0.0)
    e8 = (np.arange(TT)[None, :] // 64 == np.arange(8)[:, None]).astype(np.float32)
    egrid = np.concatenate([np.arange(33), np.arange(39) - 7.0]).astype(np.float32)
    selv = np.zeros((128, 8), np.float32)
    for s in range(NSLOT):
        selv[:, s * 2 + (TILES[o][s] - 2 * s)] = 1.0
    return dict(ropec=ropec, ropes=ropes, qmask=qm.reshape(8, -1).astype(ml_dtypes.bfloat16),
                e8=e8.astype(ml_dtypes.bfloat16), egrid=np.ascontiguousarray(np.broadcast_to(egrid, (128, 72))),
                selv=selv)


WEIGHT_KEYS = ['w_ada', 'b_ada', 'w_in', 'b_in', 's5_lambda_re', 's5_lambda_im', 's5_log_dt', 's5_b_re', 's5_b_im',
               's5_c_re', 's5_c_im', 's5_d', 's5_w_glu', 's5_b_glu', 'mla_q_norm', 'mla_w_q_up', 'mla_kv_norm',
               'mla_w_kv_up', 'sgu_ln_g', 'sgu_ln_b', 'sgu_w_s', 'sgu_b_s', 'w_branch', 'w_out', 'ln1_g', 'ln1_b',
               'ffn_w_in', 'ffn_w_out', 'ln2_g', 'ln2_b']


def make_in_maps(inputs, cores, n_layers=DEPTH):
    x = np.asarray(inputs['x'], np.float32)
    c = np.asarray(inputs['c'], np.float32)
    shared = {k: np.ascontiguousarray(np.asarray(inputs[k], np.float32)[:n_layers]) for k in WEIGHT_KEYS}
    maps = []
    for core in cores:
        b, o = core // 2, core % 2
        idx = _core_tokens(o)
        m = dict(shared)
        m['xT'] = np.ascontiguousarray(x[b, idx, :].T)
        m['cvec'] = np.ascontiguousarray(c[b].reshape(8, 128).T)
        m.update(_const_tables(o))
        maps.append(m)
    return maps


def kernel(**inputs):
    nc, _ = build_program(DEPTH)
    cores = list(range(8))
    maps = make_in_maps(inputs, cores)
    res = run_bass_kernel_spmd(nc, maps, core_ids=cores)
    out = np.zeros((NB, S, D), np.float32)
    for core in cores:
        b, o = core // 2, core % 2
        out[b, _core_tokens(o), :] = np.asarray(res.results[core]['outT'], np.float32).T
    return out


def _bc(ap2d, n, axis):
    a = ap2d.shape[1]
    if axis == 2:
        return ap2d.unsqueeze(2).to_broadcast([128, a, n])
    return ap2d.unsqueeze(1).to_broadcast([128, n, a])


def _s5_phase(st, l, uT, xlo, CC1K):
    kb, OP, nc, inp = st['kb'], st['OP'], st['nc'], st['inp']
    ident_bf, ident_f, mask01 = st['ident_bf'], st['ident_f'], st['mask01']
    psX, park, dbg, DBG = st['psX'], st['park'], st['dbg'], st['DBG']
    cc1_src, cc1_dst, cc2_src, cc2_dst = st['cc1_src'], st['cc1_dst'], st['cc2_src'], st['cc2_dst']
    PI = math.pi
    kb.allgather(cc1_src, cc1_dst, R=CC1K, W=['cc1_dst'])
    with contextlib.ExitStack() as s5:
        s5t = kb.sbuf("s5t", [128, 4, 32, 39], F32, s5)
        s5b = kb.sbuf("s5b", [128, 2, 32, 16], F32, s5)
        s5c = kb.sbuf("s5c", [128, 2, 32, 16], F32, s5)
        d_rep = kb.sbuf("d_rep", [128, 32], F32, s5)
        scn = kb.sbuf("scn", [128, 3, 32], F32, s5)
        selv = kb.sbuf("selv", [128, 8], F32, s5)
        Uown = kb.sbuf("Uown", [128, 32, 64], BF16, s5)
        Er_re, Er_im, Ey_re, Ey_im = (s5t[:, i] for i in range(4))
        kb.dma('sp', selv[:], st['selv_in'][:, :], W=['selv'])
        with contextlib.ExitStack() as tb:
            egrid = kb.sbuf("egrid", [128, 72], F32, tb)
            lam = kb.sbuf("lam", [128, 2, 32], F32, tb)
            dts = kb.sbuf("dts", [128, 32], F32, tb)
            ld = kb.sbuf("ld", [128, 2, 32], F32, tb)
            tA = kb.sbuf("tA", [128, 32, 39], F32, tb)
            tB = kb.sbuf("tB", [128, 32, 39], F32, tb)
            tC = kb.sbuf("tC", [128, 32, 39], F32, tb)
            tI = kb.sbuf("tI", [128, 32, 39], mybir.dt.int32, tb)
            sm = kb.sbuf("sm", [128, 12, 32], F32, tb)
            Fs = kb.sbuf("Fs", [128, 3, 32], F32, tb)
            Bri = kb.sbuf("Bri", [128, 2, 32, 16], F32, tb)
            tb1 = kb.sbuf("tb1", [128, 32, 16], F32, tb)
            Cnat = kb.sbuf("Cnat", [128, 2, 4, 128], F32, tb)
            kb.dma('sp', egrid[:], st['egrid_in'][:, :], W=['egrid'])
            for j, src in enumerate((inp['s5_lambda_re'], inp['s5_lambda_im'])):
                for h in range(2):
                    kb.dma('sp', lam[h * 64:(h + 1) * 64, j, :], src[l].rearrange("g p -> p g"), W=['lam%d%d' % (j, h)],
                           allow_slow_non_contiguous=True)
            LAM = ['lam00', 'lam01', 'lam10', 'lam11']
            kb.dma('sp', dts[:], inp['s5_log_dt'][l:l + 1, :].partition_broadcast(128), W=['dts'])
            OP('act', 'activation', R=['dts'], W=['dts'], out=dts[:], in_=dts[:], func=AF.Exp)
            for j in range(2):
                OP('dve', 'tensor_tensor', R=LAM + ['dts'], W=['ld%d' % j], out=ld[:, j, :], in0=lam[:, j, :], in1=dts[:],
                   op=ALU.mult)
            for (Ere, Eim, k0, K) in ((Er_re, Er_im, 0, 33), (Ey_re, Ey_im, 33, 39)):
                eg = _bc(egrid[:, k0:k0 + K], 32, 1)
                OP('dve', 'tensor_tensor', R=['ld0', 'egrid'], W=['tA'], out=tA[:, :, 0:K], in0=_bc(ld[:, 0, :], K, 2), in1=eg,
                   op=ALU.mult)
                OP('act', 'activation', R=['tA'], W=['tA'], out=tA[:, :, 0:K], in_=tA[:, :, 0:K], func=AF.Exp)
                OP('dve', 'tensor_tensor', R=['ld1', 'egrid'], W=['tB'], out=tB[:, :, 0:K], in0=_bc(ld[:, 1, :], K, 2), in1=eg,
                   op=ALU.mult)
                for (dst, off) in ((Eim, 0.0), (Ere, 0.25)):
                    if off:
                        OP('dve', 'tensor_scalar_add', R=['tB'], W=['tB'], out=tB[:, :, 0:K], in0=tB[:, :, 0:K], scalar1=0.5 * PI)
                    OP('dve', 'tensor_scalar_mul', R=['tB'], W=['tC'], out=tC[:, :, 0:K], in0=tB[:, :, 0:K], scalar1=1.0 / (2 * PI))
                    OP('dve', 'tensor_copy', R=['tC'], W=['tI'], out=tI[:, :, 0:K], in_=tC[:, :, 0:K])
                    OP('dve', 'tensor_copy', R=['tI'], W=['tC'], out=tC[:, :, 0:K], in_=tI[:, :, 0:K])
                    OP('dve', 'scalar_tensor_tensor', R=['tC', 'tB'], W=['tC'], out=tC[:, :, 0:K], in0=tC[:, :, 0:K], scalar=-2 * PI,
                       in1=tB[:, :, 0:K], op0=ALU.mult, op1=ALU.add)
                    OP('act', 'activation', R=['tC'], W=['tC'], out=tC[:, :, 0:K], in_=tC[:, :, 0:K], func=AF.Sin)
                    OP('dve', 'tensor_tensor', R=['tA', 'tC'], W=['s5t'], out=dst[:, :, 0:K], in0=tA[:, :, 0:K], in1=tC[:, :, 0:K],
                       op=ALU.mult)
            a_re, a_im = Ey_re[:, :, 8], Ey_im[:, :, 8]
            lr, li = lam[:, 0, :], lam[:, 1, :]

            def S(i):
                return sm[:, i, :]

            def TT_(o, a, b, op):
                OP('dve', 'tensor_tensor', R=['s5t', 'sm'] + LAM, W=['sm'], out=o, in0=a, in1=b, op=op)
            TT_(S(0), lr, lr, ALU.mult)
            TT_(S(1), li, li, ALU.mult)
            TT_(S(0), S(0), S(1), ALU.add)
            OP('dve', 'reciprocal', R=['sm'], W=['sm'], out=S(0), in_=S(0))
            OP('dve', 'tensor_scalar_add', R=['s5t', 'sm'], W=['sm'], out=S(2), in0=a_re, scalar1=-1.0)
            TT_(S(3), S(2), lr, ALU.mult)
            TT_(S(4), a_im, li, ALU.mult)
            TT_(S(3), S(3), S(4), ALU.add)
            TT_(S(3), S(3), S(0), ALU.mult)
            TT_(S(5), a_im, lr, ALU.mult)
            TT_(S(6), S(2), li, ALU.mult)
            TT_(S(5), S(5), S(6), ALU.subtract)
            TT_(S(5), S(5), S(0), ALU.mult)
            OP('dve', 'tensor_scalar_mul', R=['sm'], W=['sm'], out=S(7), in0=S(3), scalar1=-1.0)
            OP('dve', 'tensor_scalar_mul', R=['sm'], W=['sm'], out=S(8), in0=S(5), scalar1=-1.0)
            OP('dve', 'tensor_copy', R=['sm'], W=['Fs'], out=Fs[0:64, 0, :], in_=sm[0:64, 3, :])
            OP('dve', 'tensor_copy', R=['sm', 'Fs'], W=['Fs'], out=Fs[64:128, 0, :], in_=sm[64:128, 8, :])
            OP('dve', 'tensor_copy', R=['sm', 'Fs'], W=['Fs'], out=Fs[0:64, 1, :], in_=sm[0:64, 8, :])
            OP('dve', 'tensor_copy', R=['sm', 'Fs'], W=['Fs'], out=Fs[64:128, 1, :], in_=sm[64:128, 7, :])
            OP('dve', 'tensor_scalar_mul', R=['Fs'], W=['Fs'], out=Fs[:, 2, :], in0=Fs[:, 0, :], scalar1=-1.0)
            for j, src in enumerate((inp['s5_b_re'], inp['s5_b_im'])):
                for h in range(2):
                    kb.dma('sp', Bri[h * 64:(h + 1) * 64, j], src[l].rearrange("g p c -> p g c"), W=['Bri%d%d' % (j, h)])
            BRI = ['Bri00', 'Bri01', 'Bri10', 'Bri11']
            for o_, (fa, fb) in enumerate(((0, 1), (1, 2))):
                OP('dve', 'tensor_tensor', R=BRI + ['Fs'], W=['tb1'], out=tb1[:], in0=Bri[:, 0], in1=_bc(Fs[:, fa, :], 16, 2),
                   op=ALU.mult)
                OP('dve', 'tensor_tensor', R=BRI + ['Fs'], W=['s5b%d' % o_], out=s5b[:, o_], in0=Bri[:, 1],
                   in1=_bc(Fs[:, fb, :], 16, 2), op=ALU.mult)
                OP('dve', 'tensor_tensor', R=['tb1', 's5b%d' % o_], W=['s5b%d' % o_], out=s5b[:, o_], in0=s5b[:, o_], in1=tb1[:],
                   op=ALU.add)
            cre = inp['s5_c_re'][l].rearrange("(t gl) c p -> (gl c) t p", t=4)
            cim = inp['s5_c_im'][l].rearrange("(t gl) c p -> (gl c) t p", t=4)
            kb.dma('sp', Cnat[:, 0, :, 0:64], cre, W=['Cn0'])
            kb.dma('sp', Cnat[:, 0, :, 64:128], cim, W=['Cn1'])
            kb.dma('sp', Cnat[:, 1, :, 0:64], cim, W=['Cn2'])
            kb.dma('sp', Cnat[:, 1, :, 64:128], cre, W=['Cn3'])
            for v in range(2):
                ps, pk = kb.next_ps()
                for t in range(4):
                    kb.mm(ps[:, t * 128:(t + 1) * 128], [(Cnat[:, v, t, :], ident_f[:, :])],
                          R=['Cn0', 'Cn1', 'Cn2', 'Cn3', 'ident_f'], W=[pk])
                dst = s5c[:, v].rearrange("p g c -> p (g c)")
                if v == 0:
                    OP('dve', 'tensor_copy', R=[pk], W=['s5c0'], out=dst, in_=ps[:, :])
                else:
                    OP('dve', 'tensor_scalar_mul', R=[pk], W=['s5c1a'], out=dst[0:64, :], in0=ps[0:64, :], scalar1=-1.0)
                    OP('act', 'activation', R=[pk], W=['s5c1b'], out=dst[64:128, :], in_=ps[64:128, :], func=AF.Copy)
            S5C = ['s5c0', 's5c1a', 's5c1b']
            for s_ in range(8):
                kb.dma('sp', d_rep[s_ * 16:(s_ + 1) * 16, :], inp['s5_d'][l].rearrange("(g c) -> c g", c=16), W=['d_rep%d' % s_],
                       allow_slow_non_contiguous=True)
            DREP = ['d_rep%d' % s_ for s_ in range(8)]
            OP('dve', 'tensor_copy', R=['s5t'], W=['scn'], out=scn[:, 0, :], in_=Er_re[:, :, 32])
            OP('dve', 'tensor_copy', R=['s5t', 'scn'], W=['scn'], out=scn[0:64, 1, :], in_=Er_im[0:64, :, 32])
            OP('dve', 'tensor_scalar_mul', R=['s5t', 'scn'], W=['scn'], out=scn[64:128, 1, :], in0=Er_im[64:128, :, 32], scalar1=-1.0)
            OP('dve', 'tensor_scalar_mul', R=['scn'], W=['scn'], out=scn[:, 2, :], in0=scn[:, 1, :], scalar1=-1.0)
            if 's5t' in dbg:
                kb.dma('sp', DBG('s5t', [128, 4 * 32 * 39])[:, :], s5t[:].rearrange("p a g k -> p (a g k)"), R=['s5t'], W=['dbg1'])
                kb.dma('sp', DBG('s5b', [128, 2 * 512])[:, :], s5b[:].rearrange("p a g k -> p (a g k)"), R=['s5b0', 's5b1'], W=['dbg2'])
                kb.dma('sp', DBG('s5c', [128, 2 * 512])[:, :], s5c[:].rearrange("p a g k -> p (a g k)"), R=S5C, W=['dbg3'])
                kb.dma('sp', DBG('scn', [128, 96])[:, :], scn[:].rearrange("p a g -> p (a g)"), R=['scn'], W=['dbg4'])
            kb.barrier()
        if 's5t' in dbg:
            return

        with contextlib.ExitStack() as p1:
            Xrev = [kb.sbuf("Xrev%d" % i, [128, 33, 16], BF16, p1) for i in range(4)]
            t1s = [kb.sbuf("t1_%d" % i, [128, 39, 16], F32, p1) for i in range(2)]
            t2s = [kb.sbuf("t2_%d" % i, [128, 39, 16], F32, p1) for i in range(2)]
            Gt = [kb.sbuf("Gt%d" % i, [128, 4, 128], BF16, p1) for i in range(4)]
            Gs = [kb.sbuf("Gs%d" % i, [128, 4, 128], BF16, p1) for i in range(4)]
            IUo = kb.sbuf("IUo", [128, 64, 32], F32, p1)
            IVo = kb.sbuf("IVo", [128, 64, 32], F32, p1)
            P1E = os.environ.get('S5_ENG', 'pool')
            NG1 = int(os.environ.get("S5_NG", "32"))
            if NG1 < 32:
                OP('dve', 'memset', W=['IUo7'], ap=IUo[:], constant=0.0)
                OP('dve', 'memset', W=['IVo7'], ap=IVo[:], constant=0.0)
            for g in range(NG1):
                i = g % 4
                t1, t2 = t1s[g % 2], t2s[g % 2]
                T1K, T2K = 't1_%d' % (g % 2), 't2_%d' % (g % 2)
                OP(P1E, 'tensor_tensor', R=['s5t', 's5b0'], W=[T1K], out=t1[:, 0:33, :], in0=_bc(Er_re[:, g, 0:33], 16, 2),
                   in1=_bc(s5b[:, 0, g, :], 33, 1), op=ALU.mult)
                OP('dve', 'tensor_tensor', R=['s5t', 's5b1'], W=[T2K], out=t2[:, 0:33, :], in0=_bc(Er_im[:, g, 0:33], 16, 2),
                   in1=_bc(s5b[:, 1, g, :], 33, 1), op=ALU.mult)
                OP('dve', 'tensor_tensor', R=[T1K, T2K], W=['Xrev%d' % i], out=Xrev[i][:], in0=t1[:, 0:33, :],
                   in1=t2[:, 0:33, :], op=ALU.add)
                OP('act', 'activation', R=['Xrev%d' % i], W=['xlo%d' % g], out=xlo[:, g, :],
                   in_=Xrev[i][:, 0:8, :].rearrange("p a b -> p (a b)"), func=AF.Copy)
                ps, pk = kb.next_ps()
                for sh in range(4):
                    kb.mm(ps[:, sh * 128:(sh + 1) * 128],
                          [(Xrev[i][:, 25 - 8 * sh:33 - 8 * sh, :].rearrange("p a b -> p (a b)"), ident_bf[:, :])],
                          R=['Xrev%d' % i, 'ident_bf'], W=[pk])
                ps3 = ps[:, :].rearrange("p (s j) -> p s j", s=4)
                OP('act', 'activation', R=[pk], W=['Gt%d' % i], out=Gt[i][:], in_=ps3, func=AF.Copy)
                OP('dve', 'tensor_copy', R=[pk], W=['Gs%da' % i], out=Gs[i][:, :, 0:64], in_=ps3[:, :, 64:128])
                OP('dve', 'tensor_copy', R=[pk], W=['Gs%db' % i], out=Gs[i][:, :, 64:128], in_=ps3[:, :, 0:64])
                gi = g % 8
                kb.mm(psX[0][0][:, gi * 64:(gi + 1) * 64], [(Gt[i][:, sh, :], uT[:, g, sh, :]) for sh in range(4)],
                      R=['Gt%d' % i, 'uT'], W=[psX[0][1]])
                kb.mm(psX[1][0][:, gi * 64:(gi + 1) * 64], [(Gs[i][:, sh, :], uT[:, g, sh, :]) for sh in range(4)],
                      R=['Gs%da' % i, 'Gs%db' % i, 'uT'], W=[psX[1][1]])
                if gi == 7:
                    for (pp, dst, key) in ((psX[0], IUo, 'IUo'), (psX[1], IVo, 'IVo')):
                        OP('act' if key == 'IUo' else 'dve', 'activation' if key == 'IUo' else 'tensor_copy', R=[pp[1]],
                           W=[key + str(g)], out=dst[:, :, g - 7:g + 1].rearrange("p n g -> p g n"),
                           in_=pp[0][:, :].rearrange("p (g n) -> p g n", g=8), **({'func': AF.Copy} if key == 'IUo' else {}))
            if os.environ.get("S5_STOP") == "1":
                kb.dma('sp', DBG('IUo', [128, 2048])[:, :], IUo[:].rearrange("p n g -> p (n g)"), R=['IUo7'], W=['dbgq'])
                kb.barrier()
                return
            kb.dma('sp', cc2_src.ap()[0:128, :], IUo[:].rearrange("p n g -> p (n g)"), R=['IUo%d' % g for g in (7, 15, 23, 31)],
                   W=['cc2s0'])
            kb.dma('sp', cc2_src.ap()[128:256, :], IVo[:].rearrange("p n g -> p (n g)"), R=['IVo%d' % g for g in (7, 15, 23, 31)],
                   W=['cc2s1'])
            kb.allgather(cc2_src, cc2_dst, R=['cc2s0', 'cc2s1'], W=['cc2_dst'])
            if 's5p1' in dbg:
                kb.dma('sp', DBG('cc2', [512, 2048])[:, :], cc2_dst.ap()[:, :], R=['cc2_dst'], W=['dbg5'])
            kb.barrier()
        if 's5p1' in dbg:
            return

        with contextlib.ExitStack() as sc:
            U = kb.sbuf("U", [128, 128, 32], F32, sc)
            V = kb.sbuf("V", [128, 128, 32], F32, sc)
            sa = kb.sbuf("sa", [128, 2, 8, 32], F32, sc)
            sb_ = kb.sbuf("sb_", [128, 2, 8, 32], F32, sc)
            PT = kb.sbuf("PT", [128, 4, 16, 32], F32, sc)
            ptmp = kb.sbuf("ptmp", [128, 2, 8, 32], F32, sc)
            ftmp = kb.sbuf("ftmp", [128, 2, 7, 15, 32], F32, sc)
            selt = kb.sbuf("selt", [128, 16, 32], F32, sc)
            for r in range(2):
                for slot in range(NSLOT):
                    G = TILES[r][slot]
                    for (dst, roff, key) in ((U, 0, 'U'), (V, 128, 'V')):
                        kb.dma('sp', dst[:, G * 16:(G + 1) * 16, :],
                               cc2_dst.ap()[r * 256 + roff:r * 256 + roff + 128, slot * 512:(slot + 1) * 512]
                               .rearrange("p (n g) -> p n g", g=32), R=['cc2_dst'], W=['%s%d' % (key, G)])
            LD = ['U%d' % i for i in range(8)] + ['V%d' % i for i in range(8)]
            U4 = U[:].rearrange("p (b j) g -> p b j g", j=16)
            V4 = V[:].rearrange("p (b j) g -> p b j g", j=16)

            def b8(ap2):
                return ap2.unsqueeze(1).to_broadcast([128, 8, 32])
            for j in range(1, 16):
                ku, kv, kup, kvp = 'Ul%d' % (j % 2), 'Vl%d' % (j % 2), 'Ul%d' % ((j - 1) % 2), 'Vl%d' % ((j - 1) % 2)
                OP('dve', 'tensor_tensor', R=LD + [kvp, 'scn'], W=['sa0'], out=sa[:, 0], in0=V4[:, :, j - 1, :], in1=b8(scn[:, 1, :]), op=ALU.mult)
                OP('dve', 'tensor_tensor', R=LD + [kup, 'scn'], W=['sa1'], out=sa[:, 1], in0=U4[:, :, j - 1, :], in1=b8(scn[:, 0, :]), op=ALU.mult)
                OP('pool', 'tensor_tensor', R=LD + [kup, 'scn'], W=['sb0'], out=sb_[:, 0], in0=U4[:, :, j - 1, :], in1=b8(scn[:, 2, :]), op=ALU.mult)
                OP('pool', 'tensor_tensor', R=LD + [kvp, 'scn'], W=['sb1'], out=sb_[:, 1], in0=V4[:, :, j - 1, :], in1=b8(scn[:, 0, :]), op=ALU.mult)
                OP('dve', 'tensor_tensor', R=LD + ['sa0', ku], W=[ku], out=U4[:, :, j, :], in0=U4[:, :, j, :], in1=sa[:, 0], op=ALU.add)
                OP('dve', 'tensor_tensor', R=LD + ['sa1', ku], W=[ku], out=U4[:, :, j, :], in0=U4[:, :, j, :], in1=sa[:, 1], op=ALU.add)
                OP('pool', 'tensor_tensor', R=LD + ['sb0', kv], W=[kv], out=V4[:, :, j, :], in0=V4[:, :, j, :], in1=sb_[:, 0], op=ALU.add)
                OP('pool', 'tensor_tensor', R=LD + ['sb1', kv], W=[kv], out=V4[:, :, j, :], in0=V4[:, :, j, :], in1=sb_[:, 1], op=ALU.add)
            ALLUV = ['Ul0', 'Ul1', 'Vl0', 'Vl1']
            Pre, Pim, PB = PT[:, 0], PT[:, 1], PT[:, 2]
            OP('act', 'activation', R=['s5t'], W=['PT'], out=Pre[:, 0, :], in_=s5t[:, 0, :, 32], func=AF.Copy)
            OP('act', 'activation', R=['s5t', 'PT'], W=['PT'], out=Pim[:, 0, :], in_=s5t[:, 1, :, 32], func=AF.Copy)
            n_ = 1
            while n_ < 16:
                yr = Pre[:, n_ - 1, :].unsqueeze(1).to_broadcast([128, n_, 32])
                yi = Pim[:, n_ - 1, :].unsqueeze(1).to_broadcast([128, n_, 32])
                xr, xi = Pre[:, 0:n_, :], Pim[:, 0:n_, :]
                zr, zi = Pre[:, n_:2 * n_, :], Pim[:, n_:2 * n_, :]
                ta, tb_ = ptmp[:, 0, 0:n_, :], ptmp[:, 1, 0:n_, :]
                OP('dve', 'tensor_tensor', R=['PT'], W=['pta'], out=ta, in0=xr, in1=yr, op=ALU.mult)
                OP('dve', 'tensor_tensor', R=['PT'], W=['ptb'], out=tb_, in0=xi, in1=yi, op=ALU.mult)
                OP('dve', 'tensor_tensor', R=['pta', 'ptb', 'PT'], W=['PT'], out=zr, in0=ta, in1=tb_, op=ALU.subtract)
                OP('dve', 'tensor_tensor', R=['PT', 'pta'], W=['pta'], out=ta, in0=xr, in1=yi, op=ALU.mult)
                OP('dve', 'tensor_tensor', R=['PT', 'ptb'], W=['ptb'], out=tb_, in0=xi, in1=yr, op=ALU.mult)
                OP('dve', 'tensor_tensor', R=['pta', 'ptb', 'PT'], W=['PT'], out=zi, in0=ta, in1=tb_, op=ALU.add)
                n_ *= 2
            OP('dve', 'tensor_copy', R=['PT'], W=['PTb'], out=PT[0:64, 2], in_=PT[0:64, 1])
            OP('dve', 'tensor_scalar_mul', R=['PT', 'PTb'], W=['PTb'], out=PT[64:128, 2], in0=PT[64:128, 1], scalar1=-1.0)
            OP('dve', 'tensor_scalar_mul', R=['PTb'], W=['PTn'], out=PT[:, 3, 15, :], in0=PT[:, 2, 15, :], scalar1=-1.0)
            for b in range(1, 8):
                r0, r1 = (b - 1) * 16 + 15, b * 16 + 15
                OP('dve', 'tensor_tensor', R=ALLUV + ['PTb', 'Vc'], W=['sa0'], out=sa[:, 0, 0, :], in0=V[:, r0, :], in1=PB[:, 15, :], op=ALU.mult)
                OP('dve', 'tensor_tensor', R=ALLUV + ['PT', 'Uc'], W=['sa1'], out=sa[:, 1, 0, :], in0=U[:, r0, :], in1=Pre[:, 15, :], op=ALU.mult)
                OP('pool', 'tensor_tensor', R=ALLUV + ['PTn', 'Uc'], W=['sb0'], out=sb_[:, 0, 0, :], in0=U[:, r0, :], in1=PT[:, 3, 15, :], op=ALU.mult)
                OP('pool', 'tensor_tensor', R=ALLUV + ['PT', 'Vc'], W=['sb1'], out=sb_[:, 1, 0, :], in0=V[:, r0, :], in1=Pre[:, 15, :], op=ALU.mult)
                OP('dve', 'tensor_tensor', R=ALLUV + ['sa0', 'Uc'], W=['Uc'], out=U[:, r1, :], in0=U[:, r1, :], in1=sa[:, 0, 0, :], op=ALU.add)
                OP('dve', 'tensor_tensor', R=ALLUV + ['sa1', 'Uc'], W=['Uc'], out=U[:, r1, :], in0=U[:, r1, :], in1=sa[:, 1, 0, :], op=ALU.add)
                OP('pool', 'tensor_tensor', R=ALLUV + ['sb0', 'Vc'], W=['Vc'], out=V[:, r1, :], in0=V[:, r1, :], in1=sb_[:, 0, 0, :], op=ALU.add)
                OP('pool', 'tensor_tensor', R=ALLUV + ['sb1', 'Vc'], W=['Vc'], out=V[:, r1, :], in0=V[:, r1, :], in1=sb_[:, 1, 0, :], op=ALU.add)
            shp = [128, 7, 15, 32]
            tu = U4[:, 0:7, 15, :].unsqueeze(2).to_broadcast(shp)
            tv = V4[:, 0:7, 15, :].unsqueeze(2).to_broadcast(shp)
            OP('dve', 'tensor_tensor', R=ALLUV + ['Uc', 'PT'], W=['ft0'], out=ftmp[:, 0], in0=tu, in1=Pre[:, 0:15, :].unsqueeze(1).to_broadcast(shp), op=ALU.mult)
            OP('pool', 'tensor_tensor', R=ALLUV + ['Vc', 'PTb'], W=['ft1'], out=ftmp[:, 1], in0=tv, in1=PB[:, 0:15, :].unsqueeze(1).to_broadcast(shp), op=ALU.mult)
            OP('dve', 'tensor_tensor', R=ALLUV + ['Uc', 'ft0'], W=['Uf'], out=U4[:, 1:8, 0:15, :], in0=U4[:, 1:8, 0:15, :], in1=ftmp[:, 0], op=ALU.add)
            OP('dve', 'tensor_tensor', R=ALLUV + ['Uc', 'Uf', 'ft1'], W=['Uf'], out=U4[:, 1:8, 0:15, :], in0=U4[:, 1:8, 0:15, :], in1=ftmp[:, 1], op=ALU.add)
            UK = ['U%d' % i for i in range(8)] + ['Ul0', 'Ul1', 'Uc', 'Uf']
            for s_ in range(NSLOT):
                T0, T1 = 2 * s_, 2 * s_ + 1
                if T0 == 0:
                    OP('dve', 'memset', W=['selt'], ap=selt[:, 0, :], constant=0.0)
                    OP('dve', 'tensor_scalar_mul', R=UK + ['selv', 'selt'], W=['selt'], out=selt[:, 1:16, :], in0=U[:, 0:15, :],
                       scalar1=selv[:, 0:1])
                else:
                    OP('dve', 'tensor_scalar_mul', R=UK + ['selv'], W=['selt'], out=selt[:], in0=U[:, T0 * 16 - 1:T0 * 16 + 15, :],
                       scalar1=selv[:, 2 * s_:2 * s_ + 1])
                OP('dve', 'scalar_tensor_tensor', R=UK + ['selv', 'selt'], W=['Uown'],
                   out=Uown[:, :, s_ * 16:(s_ + 1) * 16].rearrange("p g n -> p n g"), in0=U[:, T1 * 16 - 1:T1 * 16 + 15, :],
                   scalar=selv[:, 2 * s_ + 1:2 * s_ + 2], in1=selt[:], op0=ALU.mult, op1=ALU.add)
            if 's5scan' in dbg:
                kb.dma('sp', DBG('Uown', [128, 2048], BF16)[:, :], Uown[:].rearrange("p g n -> p (g n)"), R=['Uown'], W=['dbg6'])
            kb.barrier()
        if 's5scan' in dbg:
            return

        with contextlib.ExitStack() as p2:
            Yx = [kb.sbuf("Yx%d" % i, [128, 39, 16], BF16, p2) for i in range(4)]
            t1s = [kb.sbuf("t1b%d" % i, [128, 39, 16], F32, p2) for i in range(2)]
            t2s = [kb.sbuf("t2b%d" % i, [128, 39, 16], F32, p2) for i in range(2)]
            Wg = [kb.sbuf("Wg%d" % i, [128, 512], BF16, p2) for i in range(4)]
            tmpws = [kb.sbuf("tmpw%d" % i, [128, 128], F32, p2) for i in range(2)]
            zz = [kb.sbuf("zz%d" % i, [128, 256], BF16, p2) for i in range(4)]
            Ztok = kb.sbuf("Ztok", [128, 2, 8, 512], BF16, p2)
            zT = kb.sbuf("zT", [128, 4, NTOK], BF16, p2)
            wglu = kb.sbuf("wglu", [128, 4, 512], BF16, p2)
            bglu = kb.sbuf("bglu", [128, 4], F32, p2)
            sg = kb.sbuf("sg", [128, TT], F32, p2)
            ys5 = kb.sbuf("ys5", [128, 4, TT], BF16, p2)
            kb.dma('pool', wglu[:], inp['s5_w_glu'][l].rearrange("(kt p) c -> p kt c", p=128), W=['wglu'])
            kb.dma('sp', bglu[:], _fm(inp['s5_b_glu'][l]), W=['bglu'], allow_slow_non_contiguous=True)
            psZ = None
            for g in range(32):
                i = g % 4
                ip = g % 2
                t1, t2, tmpw = t1s[ip], t2s[ip], tmpws[ip]
                T1K, T2K, TWK = 't1b%d' % ip, 't2b%d' % ip, 'tmpw%d' % ip
                OP('pool', 'tensor_tensor', R=['s5t', 's5c0'], W=[T1K], out=t1[:], in0=_bc(Ey_re[:, g, :], 16, 2),
                   in1=_bc(s5c[:, 0, g, :], 39, 1), op=ALU.mult)
                OP('dve', 'tensor_tensor', R=['s5t', 's5c1a', 's5c1b'], W=[T2K], out=t2[:], in0=_bc(Ey_im[:, g, :], 16, 2),
                   in1=_bc(s5c[:, 1, g, :], 39, 1), op=ALU.mult)
                OP('dve', 'tensor_tensor', R=[T1K, T2K], W=['Yx%d' % i], out=Yx[i][:], in0=t1[:], in1=t2[:], op=ALU.add)
                psW, pkW = kb.next_ps()
                kb.mm(psW[:, :], [(xlo[:, g, :], Yx[i][:, 0:32, :].rearrange("p a b -> p (a b)"))], R=['xlo%d' % g, 'Yx%d' % i], W=[pkW])
                OP('dve', 'tensor_tensor', R=[pkW, 'mask01'], W=[TWK], out=tmpw[:], in0=psW[:, 0:128], in1=mask01[:], op=ALU.mult)
                OP('dve', 'scalar_tensor_tensor', R=[TWK, 'jmat'] + ['d_rep%d' % s_ for s_ in range(8)], W=['Wg%da' % i],
                   out=Wg[i][:, 0:128], in0=st['jmat'][:, :], scalar=d_rep[:, g:g + 1], in1=tmpw[:], op0=ALU.mult, op1=ALU.add)
                OP('act', 'activation', R=[pkW], W=['Wg%db' % i], out=Wg[i][:, 128:512], in_=psW[:, 128:512], func=AF.Copy)
                psO, pkO = kb.next_ps()
                items = [(psO[:, 0:256], Wg[i][:, 0:128], uT[:, g].rearrange("p t n -> p (t n)"), True, False)]
                for d_ in range(1, 4):
                    items.append((psO[:, d_ * 64:256], Wg[i][:, d_ * 128:(d_ + 1) * 128],
                                  uT[:, g, 0:4 - d_, :].rearrange("p t n -> p (t n)"), False, False))
                for th in range(4):
                    items.append((psO[:, th * 64:(th + 1) * 64], Yx[i][:, 7 + 8 * th:15 + 8 * th, :].rearrange("p a b -> p (a b)"),
                                  Uown[:, g, :], False, th == 3))
                kb.mmg(items, R=['Wg%da' % i, 'Wg%db' % i, 'uT', 'Yx%d' % i, 'Uown'], W=[pkO])
                OP('act', 'activation', R=[pkO], W=['zz%d' % i], out=zz[i][:], in_=psO[:, 0:256], func=AF.Gelu_apprx_tanh)
                if ip == 0:
                    psZ, pkZ = kb.next_ps()
                for mt in range(2):
                    kb.mm(psZ[:, (ip * 2 + mt) * 128:(ip * 2 + mt + 1) * 128], [(zz[i][:, mt * 128:(mt + 1) * 128], ident_bf[:, :])],
                          R=['zz%d' % i, 'ident_bf'], W=[pkZ])
                if ip == 1:
                    pz = psZ[:, :].rearrange("p (gl mt t c) -> p gl mt t c", gl=2, mt=2, t=8)
                    for mt in range(2):
                        evac_e = 'act' if mt == 0 else 'dve'
                        kw = dict(out=Ztok[:, mt, :, (g - 1) * 16:(g + 1) * 16].rearrange("p t (gl c) -> p gl t c", c=16),
                                  in_=pz[:, :, mt])
                        if evac_e == 'act':
                            OP('act', 'activation', R=[pkZ], W=['Ztok%d_%d' % (mt, g)], func=AF.Copy, **kw)
                        else:
                            OP('dve', 'tensor_copy', R=[pkZ], W=['Ztok%d_%d' % (mt, g)], **kw)
            ZK = ['Ztok%d_%d' % (mt, g) for mt in range(2) for g in range(1, 32, 2)]
            cnt = 0
            for mt in range(2):
                for s_lo in range(8):
                    ps, pk = kb.next_ps()
                    for ct in range(4):
                        kb.mm(ps[:, ct * 128:(ct + 1) * 128], [(Ztok[:, mt, s_lo, ct * 128:(ct + 1) * 128], ident_bf[:, :])],
                              R=ZK + ['ident_bf'], W=[pk])
                    for ct in range(4):
                        dst = zT[:, ct, :].rearrange("p (slot nl th s) -> p th slot nl s", slot=4, nl=16, th=4, s=8)
                        dst = dst[:, 2 * mt:2 * mt + 2, :, :, s_lo]
                        src = ps[:, ct * 128:(ct + 1) * 128].rearrange("p (th slot nl) -> p th slot nl", th=2, slot=4)
                        cnt += 1
                        if cnt % 2:
                            OP('act', 'activation', R=[pk], W=['zT%d_%d_%d' % (ct, mt, s_lo)], out=dst, in_=src, func=AF.Copy)
                        else:
                            OP('dve', 'tensor_copy', R=[pk], W=['zT%d_%d_%d' % (ct, mt, s_lo)], out=dst, in_=src)
            ZT = ['zT%d_%d_%d' % (ct, mt, s_lo) for ct in range(4) for mt in range(2) for s_lo in range(8)]
            for slot in range(NSLOT):
                sl = slice(slot * TT, (slot + 1) * TT)
                for m in range(4):
                    ps, pk = kb.next_ps()
                    kb.mm(ps[:, :], [(wglu[:, k, m * 128:(m + 1) * 128], zT[:, k, sl]) for k in range(4)], R=ZT + ['wglu'], W=[pk])
                    OP('act', 'activation', R=[pk, 'bglu'], W=['sg'], out=sg[:], in_=ps[:, :], func=AF.Sigmoid, bias=bglu[:, m:m + 1])
                    OP('dve', 'tensor_tensor', R=['sg'] + ZT, W=['ys5_%d' % m], out=ys5[:, m, :], in0=zT[:, m, sl], in1=sg[:], op=ALU.mult)
                kb.dma('sp', park['s5'][:, :, sl], ys5[:], R=['ys5_%d' % m for m in range(4)], W=['park_s5%d' % slot])
            if 's5' in dbg:
                kb.dma('sp', DBG('s5', [128, 4 * NTOK], BF16)[:, :], park['s5'].rearrange("p m t -> p (m t)"),
                       R=['park_s5%d' % i for i in range(4)], W=['dbg_s5'])
            kb.barrier()


def _mla_phase(st, l):
    kb, OP, nc, inp = st['kb'], st['OP'], st['nc'], st['inp']
    ones_bf, ones_f, e8, qmask = st['ones_bf'], st['ones_f'], st['e8'], st['qmask']
    psX, park, dbg, DBG = st['psX'], st['park'], st['dbg'], st['DBG']
    cc1_dst = st['cc1_dst']
    w_in, b_in = inp['w_in'], inp['b_in']
    HTK = ['hT%d' % k for k in range(8)]
    SCALE = 96.0 ** -0.5
    with contextlib.ExitStack() as ph:
        cqnT = kb.sbuf("cqnT", [128, 3, NTOK], BF16, ph)
        wqA = kb.sbuf("wqA", [128, 3, 8, 96], BF16, ph)
        wqB = kb.sbuf("wqB", [128, 3, 8, 32], BF16, ph)
        wkn = kb.sbuf("wkn", [128, 2, 8, 96], BF16, ph)
        wkvv = kb.sbuf("wkvv", [128, 2, 8, 64], BF16, ph)
        qg = kb.sbuf("qg", [128, 5], F32, ph)
        bcq = kb.sbuf("bcq", [128, 3], F32, ph)
        kb.dma('sp', qg[:, 0:3], _fm(inp['mla_q_norm'][l]), W=['qg0'], allow_slow_non_contiguous=True)
        kb.dma('sp', qg[:, 3:5], _fm(inp['mla_kv_norm'][l]), W=['qg1'], allow_slow_non_contiguous=True)
        kb.dma('sp', bcq[:], _fm(b_in[l, 512:896]), W=['bcq'], allow_slow_non_contiguous=True)
        with contextlib.ExitStack() as wp:
            wq_raw = kb.sbuf("wq_raw", [128, 3, 768], F32, wp)
            wkv_raw = kb.sbuf("wkv_raw", [128, 2, 1024], F32, wp)
            kb.dma('sp', wq_raw[:], inp['mla_w_q_up'][l].rearrange("(kt p) c -> p kt c", p=128), W=['wq_raw'])
            kb.dma('sp', wkv_raw[:], inp['mla_w_kv_up'][l].rearrange("(kt p) c -> p kt c", p=128), W=['wkv_raw'])
            OP('pool', 'memset', W=['wkn'], ap=wkn[:], constant=0.0)
            for kt in range(3):
                src = wq_raw[:, kt, :].rearrange("p (h c) -> p h c", c=96)
                g_ = qg[:, kt:kt + 1]
                OP('dve', 'tensor_scalar_mul', R=['wq_raw', 'qg0'], W=['wqA'], out=wqA[:, kt, :, 0:32], in0=src[:, :, 64:96], scalar1=g_)
                OP('dve', 'tensor_scalar_mul', R=['wq_raw', 'qg0'], W=['wqA'], out=wqA[:, kt, :, 32:96], in0=src[:, :, 0:64], scalar1=g_)
                OP('dve', 'tensor_scalar', R=['wq_raw', 'qg0'], W=['wqB'], out=wqB[:, kt, :, 0:16], in0=src[:, :, 80:96], scalar1=g_,
                   scalar2=-1.0, op0=ALU.mult, op1=ALU.mult)
                OP('dve', 'tensor_scalar_mul', R=['wq_raw', 'qg0'], W=['wqB'], out=wqB[:, kt, :, 16:32], in0=src[:, :, 64:80], scalar1=g_)
            for kt in range(2):
                src = wkv_raw[:, kt, :].rearrange("p (h c) -> p h c", c=128)
                g_ = qg[:, 3 + kt:4 + kt]
                OP('dve', 'tensor_scalar_mul', R=['wkv_raw', 'qg1', 'wkn'], W=['wkn'], out=wkn[:, kt, :, 32:96], in0=src[:, :, 0:64], scalar1=g_)
                OP('dve', 'tensor_scalar_mul', R=['wkv_raw', 'qg1'], W=['wkvv'], out=wkvv[:, kt, :, :], in0=src[:, :, 64:128], scalar1=g_)
            kb.barrier()
        with contextlib.ExitStack() as cqp:
            hT = kb.sbuf("hTm", [128, 8, TT], BF16, cqp)
            wcq = kb.sbuf("wcq", [128, 8, 384], BF16, cqp)
            cq_f = kb.sbuf("cq_f", [128, 3, TT], F32, cqp)
            sq = kb.sbuf("sqm", [128, 3, TT], BF16, cqp)
            rstd = kb.sbuf("rstdm", [128, TT], F32, cqp)
            kb.dma('pool', wcq[:], w_in[l][:, 512:896].rearrange("(kt p) c -> p kt c", p=128), W=['wcq'])
            for slot in range(NSLOT):
                sl = slice(slot * TT, (slot + 1) * TT)
                _make_hT(st, l, slot, hT, 0, 1)
                for m in range(3):
                    ps, pk = kb.next_ps()
                    kb.mm(ps[:, :], [(wcq[:, k, m * 128:(m + 1) * 128], hT[:, k, :]) for k in range(8)], R=HTK + ['wcq'], W=[pk])
                    OP('act', 'activation', R=[pk, 'bcq'], W=['cq_f%d' % m], out=cq_f[:, m, :], in_=ps[:, :], func=AF.Identity,
                       bias=bcq[:, m:m + 1])
                    OP('act', 'activation', R=[pk, 'bcq'], W=['sq%d' % m], out=sq[:, m, :], in_=ps[:, :], func=AF.Square,
                       bias=bcq[:, m:m + 1])
                ps, pk = kb.next_ps()
                kb.mm(ps[:, :], [(ones_bf[:, :], sq[:, m, :]) for m in range(3)], R=['sq0', 'sq1', 'sq2', 'ones_bf'], W=[pk])
                OP('dve', 'tensor_scalar', R=[pk], W=['rstd'], out=rstd[:], in0=ps[:, :], scalar1=1.0 / 384, scalar2=RMS_EPS,
                   op0=ALU.mult, op1=ALU.add)
                OP('act', 'activation', R=['rstd'], W=['rstd'], out=rstd[:], in_=rstd[:], func=AF.Sqrt)
                OP('dve', 'reciprocal', R=['rstd'], W=['rstd'], out=rstd[:], in_=rstd[:])
                for m in range(3):
                    OP('dve' if m != 1 else 'pool', 'tensor_tensor', R=['cq_f%d' % m, 'rstd'], W=['cqn%d_%d' % (m, slot)],
                       out=cqnT[:, m, sl], in0=cq_f[:, m, :], in1=rstd[:], op=ALU.mult)
            kb.barrier()
        ckvT = kb.sbuf("ckvT", [128, 2, S], BF16, ph)
        KT = [kb.sbuf("KT%d" % i, [96, S], BF16, ph) for i in range(2)]
        QT = [kb.sbuf("QT%d" % i, [96, NTOK], BF16, ph) for i in range(2)]
        Vp = [kb.sbuf("Vp%d" % i, [128, 32, 128], BF16, ph) for i in range(2)]
        pT = [kb.sbuf("pT%d" % i, [128, TT], BF16, ph) for i in range(4)]
        csq = kb.sbuf("csq", [32, 2, NTOK], F32, ph)
        qt1 = kb.sbuf("qt1", [32, 2, TT], F32, ph)
        rden = kb.sbuf("rden", [128, TT], F32, ph)
        bc_sb = kb.sbuf("bc_sb", [128, TT], F32, ph)
        yt = [kb.sbuf("yt%d" % i, [128, TT], BF16, ph) for i in range(2)]
        kb.dma('sp', csq[:, 0, :], st['ropec'][:, :], W=['csq0'])
        kb.dma('sp', csq[:, 1, :], st['ropes'][:, :], W=['csq1'])
        for r in range(2):
            for slot in range(NSLOT):
                G = TILES[r][slot]
                gs = slice(G * TT, (G + 1) * TT)
                sl = slice(slot * TT, (slot + 1) * TT)
                for m in range(2):
                    kb.dma('sp', ckvT[:, m, gs], cc1_dst.ap()[r * 288 + m * 128:r * 288 + (m + 1) * 128, sl], R=['cc1_dst'],
                           W=['ckvT%d_%d' % (m, G)])
                for i in range(2):
                    kb.dma('sp', KT[i][0:32, gs], cc1_dst.ap()[r * 288 + 256:r * 288 + 288, sl], R=['cc1_dst'], W=['KTk%d_%d' % (i, G)])
        CKV = ['ckvT%d_%d' % (m, G) for m in range(2) for G in range(8)]
        for i in range(2):
            OP('pool', 'memset', W=['Vp%d' % i], ap=Vp[i][:], constant=0.0)
        OP('pool', 'memset', R=['Vp0'], W=['Vp0'], ap=Vp[0][:, :, 64:65], constant=1.0)
        OP('pool', 'memset', R=['Vp1'], W=['Vp1'], ap=Vp[1][:, :, 32:33], constant=1.0)
        CQN = ['cqn%d_%d' % (m, s_) for m in range(3) for s_ in range(NSLOT)]
        ecnt = 0
        pend_tail = [None]
        NJUNK = int(os.environ.get('NJUNK', '0'))
        psJ = kb.psb[5]
        if NJUNK:
            kb.psb = kb.psb[:5]
        if l == 0:
            st['cast_layer'](0)
        if l + 1 < st['n_layers']:
            st['cast_layer'](l + 1)
        for h in range(8):
            i = h % 2
            KT_, QT_, Vp_ = KT[i], QT[i], Vp[i]
            voff = 0 if i == 0 else 64
            for kc in range(8):
                cs_ = slice(kc * TT, (kc + 1) * TT)
                ps, pk = kb.next_ps()
                kb.mm(ps[0:96, :], [(wkn[:, kt, h, :], ckvT[:, kt, cs_]) for kt in range(2)], R=CKV + ['wkn'], W=[pk])
                OP('act', 'activation', R=[pk], W=['KTn%d_%d' % (i, kc)], out=KT_[32:64, cs_], in_=ps[32:64, :], func=AF.Copy)
                OP('dve', 'tensor_copy', R=[pk], W=['KTm%d_%d' % (i, kc)], out=KT_[64:96, cs_], in_=ps[64:96, :])
            for k8 in range(4):
                ps, pk = kb.next_ps()
                for j in range(8):
                    kt = k8 * 8 + j
                    kb.mm(ps[:, j * 64:(j + 1) * 64], [(ckvT[:, m, kt * 128:(kt + 1) * 128], wkvv[:, m, h, :]) for m in range(2)],
                          R=CKV + ['wkvv'], W=[pk])
                ecnt += 1
                kw = dict(out=Vp_[:, k8 * 8:(k8 + 1) * 8, voff:voff + 64], in_=ps[:, :].rearrange("p (j d) -> p j d", d=64))
                if ecnt % 2:
                    OP('act', 'activation', R=[pk], W=['Vp%d' % i], func=AF.Copy, **kw)
                else:
                    OP('dve', 'tensor_copy', R=[pk], W=['Vp%d' % i], **kw)
            for slot in range(NSLOT):
                sl = slice(slot * TT, (slot + 1) * TT)
                psa, pka = kb.next_ps()
                kb.mm(psa[0:96, :], [(wqA[:, kt, h, :], cqnT[:, kt, sl]) for kt in range(3)], R=CQN + ['wqA'], W=[pka])
                psb, pkb = kb.next_ps()
                kb.mm(psb[0:32, :], [(wqB[:, kt, h, :], cqnT[:, kt, sl]) for kt in range(3)], R=CQN + ['wqB'], W=[pkb])
                OP('act', 'activation', R=[pka], W=['QTn%d_%d' % (i, slot)], out=QT_[32:64, sl], in_=psa[32:64, :], func=AF.Copy)
                OP('act', 'activation', R=[pka], W=['QTm%d_%d' % (i, slot)], out=QT_[64:96, sl], in_=psa[64:96, :], func=AF.Copy)
                OP('dve', 'tensor_tensor', R=[pka, 'csq0'], W=['qt1a'], out=qt1[:, 0, :], in0=psa[0:32, :], in1=csq[:, 0, sl], op=ALU.mult)
                OP('dve', 'tensor_tensor', R=[pkb, 'csq1'], W=['qt1b'], out=qt1[:, 1, :], in0=psb[0:32, :], in1=csq[:, 1, sl], op=ALU.mult)
                OP('dve', 'tensor_tensor', R=['qt1a', 'qt1b'], W=['QTr%d_%d' % (i, slot)], out=QT_[0:32, sl], in0=qt1[:, 0, :],
                   in1=qt1[:, 1, :], op=ALU.add)
            KTK = ['KTn%d_%d' % (i, kc) for kc in range(8)] + ['KTm%d_%d' % (i, kc) for kc in range(8)] + ['KTk%d_%d' % (i, G) for G in range(8)]
            for slot in range(NSLOT):
                sl = slice(slot * TT, (slot + 1) * TT)
                o_ps, o_k = psX[slot % 2]
                nkt = 8 * (slot + 1)
                QK = ['QTn%d_%d' % (i, slot), 'QTm%d_%d' % (i, slot), 'QTr%d_%d' % (i, slot)]
                def S_emit(kt, slot=slot, sl=sl, i=i, KT_=KT_, QT_=QT_, KTK=KTK, QK=QK):
                    t512 = kt // 4
                    pairs = [(KT_[0:96, kt * 128:(kt + 1) * 128], QT_[0:96, sl])]
                    if t512 >= 2 * slot:
                        cand = t512 - 2 * slot
                        pairs.append((e8[0:8, (kt % 4) * 128:(kt % 4 + 1) * 128],
                                      qmask[0:8, (slot * 2 + cand) * TT:(slot * 2 + cand + 1) * TT]))
                    ps, pk = kb.next_ps()
                    kb.mm(ps[:, :], pairs, R=KTK + QK + ['e8', 'qmask'], W=[pk])
                    pt = pT[kt % 4]
                    OP('act', 'activation', R=[pk], W=['pT%d' % (kt % 4)], out=pt[:], in_=ps[:, :], func=AF.Exp, scale=SCALE)
                    for _j in range(NJUNK):
                        kb.mm(psJ[0][:, :], [(ones_bf[:, :], cqnT[:, 0, 0:512])], R=[], W=[])

                def PV_emit(kt, nkt=nkt, o_ps=o_ps, o_k=o_k, Vp_=Vp_, i=i):
                    kb.mmg([(o_ps[:, :], Vp_[:, kt, :], pT[kt % 4][:], kt == 0, kt == nkt - 1)],
                           R=['pT%d' % (kt % 4), 'Vp%d' % i], W=[o_k])

                def tail(slot=slot, sl=sl, i=i, h=h, o_ps=o_ps, o_k=o_k):
                    row = 64 if i == 0 else 32
                    rows = slice(0, 64) if i == 0 else slice(64, 128)
                    y_ = yt[slot % 2]
                    OP('dve', 'reciprocal', R=[o_k], W=['rden'], out=rden[row:row + 1, :], in_=o_ps[row:row + 1, :])
                    ps, pk = kb.next_ps()
                    kb.mm(ps[:, :], [(ones_f[row:row + 1, :], rden[row:row + 1, :])], R=['rden', 'ones_f'], W=[pk])
                    OP('dve', 'tensor_copy', R=[pk], W=['bc_sb'], out=bc_sb[rows, :], in_=ps[rows, :])
                    OP('dve', 'tensor_tensor', R=[o_k, 'bc_sb'], W=['yt%d' % (slot % 2)], out=y_[rows, :], in0=o_ps[rows, :],
                       in1=bc_sb[rows, :], op=ALU.mult)
                    kb.dma('sp', park['mla'][rows, h // 2, sl], y_[rows, :], R=['yt%d' % (slot % 2)], W=['park_mla%d_%d' % (h, slot)])
                LA = 2
                for step in range(nkt + LA):
                    if step < nkt:
                        S_emit(step)
                    if step == 3 and pend_tail[0] is not None:
                        pend_tail[0]()
                        pend_tail[0] = None
                    if step >= LA:
                        PV_emit(step - LA)
                pend_tail[0] = tail
        pend_tail[0]()
        if 'mla' in dbg:
            kb.dma('sp', DBG('mla', [128, 4 * NTOK], BF16)[:, :], park['mla'].rearrange("p m t -> p (m t)"),
                   R=['park_mla%d_%d' % (h, s_) for h in range(8) for s_ in range(NSLOT)], W=['dbg_mla'])
        kb.barrier()


def _resid_ln(st, l, slot, m, ps, pk, gj, zb, zq):
    OP, xT, ada = st['OP'], st['xT'], st['ada']
    sl = slice(slot * TT, (slot + 1) * TT)
    xk = 'xT%d' % m
    OP('dve', 'scalar_tensor_tensor', R=[pk, xk, 'ada'], W=[xk], out=xT[:, m, sl], in0=ps[:, :], scalar=ada[:, l, gj * 8 + m:gj * 8 + m + 1],
       in1=xT[:, m, sl], op0=ALU.mult, op1=ALU.add)
    OP('act', 'activation', R=[xk], W=['zb%d' % m], out=zb[:, m, :], in_=xT[:, m, sl], func=AF.Copy)
    OP('act', 'activation', R=[xk], W=['zq%d' % m], out=zq[:, m, :], in_=xT[:, m, sl], func=AF.Square)


def _a3_phase(st, l):
    kb, OP, nc, inp = st['kb'], st['OP'], st['nc'], st['inp']
    park = st['park']
    wsc_g, wsc_br, wsc_o = st['wsc_g'], st['wsc_br'], st['wsc_o']
    HTK = ['hT%d' % k for k in range(8)]
    PK = ('s5', 'mla', 'sgu')
    base_psb = kb.psb
    kb.psb = list(base_psb) + list(st['psX'])
    with contextlib.ExitStack() as ph:
        hTs = [kb.sbuf("hT3_%d" % i, [128, 8, TT], BF16, ph) for i in range(2)]
        yb = kb.sbuf("yb", [128, 3, 4, TT], BF16, ph)
        gw = [kb.sbuf("gw%d" % i, [128, 3, 8, 128], BF16, ph) for i in range(2)]
        bw = [kb.sbuf("bw%d" % i, [128, 3, 4, 128], BF16, ph) for i in range(2)]
        ow = [kb.sbuf("ow%d" % i, [128, 8, 128], BF16, ph) for i in range(2)]
        bg = kb.sbuf("bg", [128, 24], F32, ph)
        gt = kb.sbuf("gt", [128, 3, TT], F32, ph)
        acc = kb.sbuf("acc", [128, TT], F32, ph)
        tmp = kb.sbuf("tmp3", [128, TT], F32, ph)
        mg = kb.sbuf("mg", [128, 8, TT], BF16, ph)
        zb = kb.sbuf("zb", [128, 8, TT], BF16, ph)
        zq = kb.sbuf("zq", [128, 8, TT], BF16, ph)
        mean_sb = kb.sbuf("mean_sb", [128, TT], F32, ph)
        r_sb = kb.sbuf("r_sb", [128, TT], F32, ph)
        tmpf = kb.sbuf("tmpf", [128, 2, TT], F32, ph)
        kb.dma('sp', bg[:], _fm(inp['b_in'][l, 2208:5280]), W=['bg'], allow_slow_non_contiguous=True)
        _make_hT(st, l, 0, hTs[0], 0, 1, 'hTa')
        for slot in range(NSLOT):
            sl = slice(slot * TT, (slot + 1) * TT)
            hT = hTs[slot % 2]
            HTK = [('hTa' if slot % 2 == 0 else 'hTb') + '%d' % k for k in range(8)]
            for b, k_ in enumerate(PK):
                kb.dma('sp', yb[:, b], park[k_][:, :, sl], W=['yb%d' % b])
            for m in range(8):
                i = m % 2
                for b in range(3):
                    kb.dma('sp', gw[i][:, b], wsc_g[l][b][m], R=['wsc_g%d_%d' % (l, b)], W=['gw%d_%d' % (i, b)])
                    kb.dma('sp', bw[i][:, b], wsc_br[l][b][m], R=['wsc_br%d_%d' % (l, b)], W=['bw%d_%d' % (i, b)])
                for b in range(3):
                    psg, pkg = kb.next_ps()
                    kb.mm(psg[:, :], [(gw[i][:, b, k, :], hT[:, k, :]) for k in range(8)], R=HTK + ['gw%d_%d' % (i, b)], W=[pkg])
                    OP('act', 'activation', R=[pkg, 'bg'], W=['gt%d' % b], out=gt[:, b, :], in_=psg[:, :], func=AF.Sigmoid,
                       bias=bg[:, b * 8 + m:b * 8 + m + 1])
                    psp, pkp = kb.next_ps()
                    kb.mm(psp[:, :], [(bw[i][:, b, k, :], yb[:, b, k, :]) for k in range(4)], R=['yb%d' % b, 'bw%d_%d' % (i, b)], W=[pkp])
                    if b == 0:
                        OP('dve', 'tensor_tensor', R=[pkp, 'gt0'], W=['acc'], out=acc[:], in0=psp[:, :], in1=gt[:, 0, :], op=ALU.mult)
                    else:
                        OP('dve', 'tensor_tensor', R=[pkp, 'gt%d' % b], W=['tmp'], out=tmp[:], in0=psp[:, :], in1=gt[:, b, :], op=ALU.mult)
                        if b == 1:
                            OP('pool', 'tensor_tensor', R=['acc', 'tmp'], W=['acc'], out=acc[:], in0=acc[:], in1=tmp[:], op=ALU.add)
                        else:
                            OP('pool', 'tensor_tensor', R=['acc', 'tmp'], W=['mg%d' % m], out=mg[:, m, :], in0=acc[:], in1=tmp[:], op=ALU.add)
            MG = ['mg%d' % m for m in range(8)]
            if slot + 1 < NSLOT:
                _make_hT(st, l, slot + 1, hTs[(slot + 1) % 2], 0, 1, 'hTb' if slot % 2 == 0 else 'hTa')
            for m in range(8):
                i = m % 2
                kb.dma('sp', ow[i][:], wsc_o[l][m], R=['wsc_o%d' % l], W=['ow%d' % i])
                ps, pk = kb.next_ps()
                kb.mm(ps[:, :], [(ow[i][:, k, :], mg[:, k, :]) for k in range(8)], R=MG + ['ow%d' % i], W=[pk])
                _resid_ln(st, l, slot, m, ps, pk, 2, zb, zq)
            _ln_slot(st, l, slot, 2, 0, None, (zb, zq, mean_sb, r_sb, tmpf))
        kb.barrier()
    kb.psb = base_psb


def _ffn_phase(st, l):
    kb, OP, nc, inp = st['kb'], st['OP'], st['nc'], st['inp']
    wsc_fi, wsc_fo = st['wsc_fi'], st['wsc_fo']
    HTK = ['hT%d' % k for k in range(8)]
    base_psb = kb.psb
    kb.psb = list(base_psb) + list(st['psX'])
    with contextlib.ExitStack() as ph:
        hTs = [kb.sbuf("hT4_%d" % i, [128, 8, TT], BF16, ph) for i in range(2)]
        fw = [kb.sbuf("fw%d" % i, [128, 2, 8, 128], BF16, ph) for i in range(2)]
        fo = [kb.sbuf("fo%d" % i, [128, 22, 128], BF16, ph) for i in range(2)]
        sa = [kb.sbuf("sa%d" % i, [128, TT], F32, ph) for i in range(2)]
        gT = kb.sbuf("gT", [128, 22, TT], BF16, ph)
        zb = kb.sbuf("zb4", [128, 8, TT], BF16, ph)
        zq = kb.sbuf("zq4", [128, 8, TT], BF16, ph)
        mean_sb = kb.sbuf("mean_sb4", [128, TT], F32, ph)
        r_sb = kb.sbuf("r_sb4", [128, TT], F32, ph)
        tmpf = kb.sbuf("tmpf4", [128, 2, TT], F32, ph)
        _make_hT(st, l, 0, hTs[0], 3, 4, 'hTa')
        for slot in range(NSLOT):
            hT = hTs[slot % 2]
            HTK = [('hTa' if slot % 2 == 0 else 'hTb') + '%d' % k for k in range(8)]
            for j in range(22):
                i = j % 2
                for h in range(2):
                    kb.dma('sp', fw[i][:, h], wsc_fi[l][h][j], R=['wsc_fi%d_%d' % (l, h)], W=['fw%d_%d' % (i, h)])
                psa, pka = kb.next_ps()
                kb.mm(psa[:, :], [(fw[i][:, 0, k, :], hT[:, k, :]) for k in range(8)], R=HTK + ['fw%d_0' % i], W=[pka])
                psb, pkb = kb.next_ps()
                kb.mm(psb[:, :], [(fw[i][:, 1, k, :], hT[:, k, :]) for k in range(8)], R=HTK + ['fw%d_1' % i], W=[pkb])
                OP('act', 'activation', R=[pka], W=['sa%d' % i], out=sa[i][:], in_=psa[:, :], func=AF.Silu)
                OP('dve', 'tensor_tensor', R=[pkb, 'sa%d' % i], W=['gT%d' % j], out=gT[:, j, :], in0=psb[:, :], in1=sa[i][:], op=ALU.mult)
            GT = ['gT%d' % j for j in range(22)]
            if slot + 1 < NSLOT:
                _make_hT(st, l, slot + 1, hTs[(slot + 1) % 2], 3, 4, 'hTb' if slot % 2 == 0 else 'hTa')
            for m in range(8):
                i = m % 2
                kb.dma('sp', fo[i][:], wsc_fo[l][m], R=['wsc_fo%d' % l], W=['fo%d' % i])
                ps, pk = kb.next_ps()
                kb.mm(ps[:, :], [(fo[i][:, j, :], gT[:, j, :]) for j in range(22)], R=GT + ['fo%d' % i], W=[pk])
                _resid_ln(st, l, slot, m, ps, pk, 5, zb, zq)
            _ln_slot(st, l, slot, 5, 2, None, (zb, zq, mean_sb, r_sb, tmpf))
        kb.barrier()
    kb.psb = base_psb
```
